# Optimizing a Trainium2 kernel written in Bass

```python
import jax, jax.numpy as jnp
from jax import lax
import numpy as np

D_MODEL = 1024
BATCH = 4
SEQ = 4096
DEPTH = 2
DEC_BATCH = 128
DEC_SEQ = 1
PAST_LEN = 16384
PAGE_SIZE = 128

CONV_WIDTH = D_MODEL
CONV_KERNEL = 31
CONV_BUF = CONV_KERNEL - 1
N_HEADS = 16
HEAD_DIM = 64
N_KV_HEADS = 2
GROUP = N_HEADS // N_KV_HEADS
ATTN_WIDTH = N_HEADS * HEAD_DIM
KV_WIDTH = N_KV_HEADS * HEAD_DIM
WINDOW = 128
BLOCK = 128
ROPE_DIM = HEAD_DIM // 4
ROPE_THETA = 500000.0
EPS = 1e-6
NEG = -1e30
IN_SIZES = (2 * CONV_WIDTH, CONV_WIDTH, ATTN_WIDTH, KV_WIDTH, KV_WIDTH, ATTN_WIDTH, D_MODEL, D_MODEL)
IN_COLS = sum(IN_SIZES)

kernel_name = "hybrid_conformer_swa_sink_gated_step"


def rms_norm(x, g):
    xf = x.astype(jnp.float32)
    y = xf * lax.rsqrt(jnp.mean(xf * xf, axis=-1, keepdims=True) + EPS)
    return (y * g.astype(jnp.float32)).astype(x.dtype)


def layer_norm(x, g, b):
    xf = x.astype(jnp.float32)
    mu = jnp.mean(xf, axis=-1, keepdims=True)
    var = jnp.mean(jnp.square(xf - mu), axis=-1, keepdims=True)
    y = (xf - mu) * lax.rsqrt(var + EPS)
    return (y * g.astype(jnp.float32) + b.astype(jnp.float32)).astype(x.dtype)


def partial_rope(x, pos):
    half = ROPE_DIM // 2
    inv = ROPE_THETA ** (-jnp.arange(0, ROPE_DIM, 2, dtype=jnp.float32) / ROPE_DIM)
    ang = pos.astype(jnp.float32)[:, None] * inv[None, :]
    cos = jnp.cos(ang)[None, :, None, :]
    sin = jnp.sin(ang)[None, :, None, :]
    xf = x.astype(jnp.float32)
    x1, x2, rest = xf[..., :half], xf[..., half:ROPE_DIM], xf[..., ROPE_DIM:]
    out = jnp.concatenate([x1 * cos - x2 * sin, x2 * cos + x1 * sin, rest], axis=-1)
    return out.astype(x.dtype)


def split_in(z):
    idx = np.cumsum(np.array(IN_SIZES))[:-1].tolist()
    return jnp.split(z, idx, axis=-1)


def sink_attention(q, k, v, mask, sinks):
    s = jnp.einsum('nbqkgd,nbskd->nbkgqs', q.astype(jnp.float32), k.astype(jnp.float32))
    s = s * (HEAD_DIM ** -0.5)
    s = jnp.where(mask[None, :, None, None], s, NEG)
    sk = sinks.astype(jnp.float32).reshape(1, 1, N_KV_HEADS, GROUP, 1, 1)
    m = jnp.maximum(jnp.max(s, axis=-1, keepdims=True), sk)
    p = jnp.exp(s - m)
    denom = jnp.sum(p, axis=-1, keepdims=True) + jnp.exp(sk - m)
    o = jnp.einsum('nbkgqs,nbskd->nbqkgd', p / denom, v.astype(jnp.float32))
    return o.astype(q.dtype)


def attend_prompt(q, k, v, sinks):
    n, t = q.shape[0], q.shape[1]
    nb = t // BLOCK
    qb = q.reshape(n, nb, BLOCK, N_KV_HEADS, GROUP, HEAD_DIM)
    kb = k.reshape(n, nb, BLOCK, N_KV_HEADS, HEAD_DIM)
    vb = v.reshape(n, nb, BLOCK, N_KV_HEADS, HEAD_DIM)
    zero = jnp.zeros_like(kb[:, :1])
    kk = jnp.concatenate([jnp.concatenate([zero, kb[:, :-1]], axis=1), kb], axis=2)
    vv = jnp.concatenate([jnp.concatenate([zero, vb[:, :-1]], axis=1), vb], axis=2)
    i = jnp.arange(BLOCK)[None, :, None]
    j = jnp.arange(2 * BLOCK)[None, None, :]
    blk = jnp.arange(nb)[:, None, None]
    diff = BLOCK + i - j
    kpos = (blk - 1) * BLOCK + j
    mask = (diff >= 0) & (diff < WINDOW) & (kpos >= 0)
    o = sink_attention(qb, kk, vv, mask, sinks)
    return o.reshape(n, t, ATTN_WIDTH), k[:, -WINDOW:], v[:, -WINDOW:]


def make_attend_sample(k_buf, v_buf):
    def attend_sample(q, k, v, sinks):
        n, t = q.shape[0], q.shape[1]
        kk = jnp.concatenate([k_buf, k], axis=1)
        vv = jnp.concatenate([v_buf, v], axis=1)
        qpos = PAST_LEN + jnp.arange(t)
        kpos = jnp.concatenate([PAST_LEN - WINDOW + jnp.arange(WINDOW), qpos])
        diff = qpos[:, None] - kpos[None, :]
        mask = ((diff >= 0) & (diff < WINDOW))[None]
        qb = q.reshape(n, 1, t, N_KV_HEADS, GROUP, HEAD_DIM)
        o = sink_attention(qb, kk[:, None], vv[:, None], mask, sinks)
        return o.reshape(n, t, ATTN_WIDTH), kk[:, -WINDOW:], vv[:, -WINDOW:]
    return attend_sample


def hybrid_layer(x, pos, conv_buf, attend, norm_g, w_in, conv_w, conv_b, ln_g, ln_b,
                 w_conv_out, sinks, w_attn_out, w_out):
    n, t, _ = x.shape
    h = rms_norm(x, norm_g)
    z = jnp.einsum('ntd,dc->ntc', h, w_in)
    glu, gate_a, q, k, v, gate_b, mg_a, mg_b = split_in(z)
    u = glu[..., :CONV_WIDTH] * jax.nn.sigmoid(glu[..., CONV_WIDTH:])
    full = jnp.concatenate([conv_buf, u], axis=1)
    c = lax.conv_general_dilated(full, conv_w[:, None, :], window_strides=(1,), padding='VALID',
                                 dimension_numbers=('NWC', 'WIO', 'NWC'),
                                 feature_group_count=CONV_WIDTH) + conv_b
    c = jax.nn.silu(layer_norm(c, ln_g, ln_b)) * jax.nn.silu(gate_a)
    y_a = jnp.einsum('ntc,cd->ntd', c, w_conv_out)
    new_conv = full[:, -CONV_BUF:]
    q = partial_rope(q.reshape(n, t, N_HEADS, HEAD_DIM), pos)
    k = partial_rope(k.reshape(n, t, N_KV_HEADS, HEAD_DIM), pos)
    v = v.reshape(n, t, N_KV_HEADS, HEAD_DIM)
    o, new_k, new_v = attend(q, k, v, sinks)
    y_b = jnp.einsum('nte,ed->ntd', o * jax.nn.silu(gate_b), w_attn_out)
    y = jax.nn.sigmoid(mg_a) * y_a + jax.nn.sigmoid(mg_b) * y_b
    return x + jnp.einsum('ntd,de->nte', y, w_out), new_conv, new_k, new_v


def setup_inputs(seed: int = 0) -> dict:
    key = jax.random.key(seed)
    ks = jax.random.split(key, 17)
    f = jnp.float32
    nrm = lambda k, shape, s: jax.random.normal(k, shape, f) * s
    return {
        "x_prompt": nrm(ks[0], (BATCH, SEQ, D_MODEL), 1.0),
        "x_sample": nrm(ks[1], (DEC_BATCH, DEC_SEQ, D_MODEL), 1.0),
        "state_conv": nrm(ks[2], (DEPTH, DEC_BATCH, CONV_BUF, CONV_WIDTH), 0.5),
        "cache_k_win": nrm(ks[3], (DEPTH, DEC_BATCH, WINDOW, N_KV_HEADS, HEAD_DIM), 1.0),
        "cache_v_win": nrm(ks[4], (DEPTH, DEC_BATCH, WINDOW, N_KV_HEADS, HEAD_DIM), 1.0),
        "norm_g": 1.0 + nrm(ks[5], (DEPTH, D_MODEL), 0.1),
        "w_in": nrm(ks[6], (DEPTH, D_MODEL, IN_COLS), D_MODEL ** -0.5),
        "conv_w": nrm(ks[7], (DEPTH, CONV_KERNEL, CONV_WIDTH), CONV_KERNEL ** -0.5),
        "conv_b": nrm(ks[8], (DEPTH, CONV_WIDTH), 0.02),
        "conv_ln_g": 1.0 + nrm(ks[9], (DEPTH, CONV_WIDTH), 0.1),
        "conv_ln_b": nrm(ks[10], (DEPTH, CONV_WIDTH), 0.02),
        "w_conv_out": nrm(ks[11], (DEPTH, CONV_WIDTH, D_MODEL), CONV_WIDTH ** -0.5),
        "attn_sinks": nrm(ks[12], (DEPTH, N_HEADS), 0.5),
        "w_attn_out": nrm(ks[13], (DEPTH, ATTN_WIDTH, D_MODEL), ATTN_WIDTH ** -0.5),
        "w_out": nrm(ks[14], (DEPTH, D_MODEL, D_MODEL), D_MODEL ** -0.5),
        "final_norm_g": 1.0 + nrm(ks[15], (D_MODEL,), 0.1),
    }


def reference(x_prompt, x_sample, state_conv, cache_k_win, cache_v_win, norm_g, w_in, conv_w,
              conv_b, conv_ln_g, conv_ln_b, w_conv_out, attn_sinks, w_attn_out, w_out, final_norm_g):
    t_p = x_prompt.shape[1]
    t_s = x_sample.shape[1]
    pos_p = jnp.arange(t_p)
    pos_s = PAST_LEN + jnp.arange(t_s)
    hp, hs = x_prompt, x_sample
    conv_p, k_p, v_p, conv_s, k_s, v_s = [], [], [], [], [], []
    for l in range(DEPTH):
        params = (norm_g[l], w_in[l], conv_w[l], conv_b[l], conv_ln_g[l], conv_ln_b[l],
                  w_conv_out[l], attn_sinks[l], w_attn_out[l], w_out[l])
        zero_buf = jnp.zeros((hp.shape[0], CONV_BUF, CONV_WIDTH), hp.dtype)
        hp, c1, k1, v1 = hybrid_layer(hp, pos_p, zero_buf, attend_prompt, *params)
        hs, c2, k2, v2 = hybrid_layer(hs, pos_s, state_conv[l],
                                      make_attend_sample(cache_k_win[l], cache_v_win[l]), *params)
        conv_p.append(c1); k_p.append(k1); v_p.append(v1)
        conv_s.append(c2); k_s.append(k2); v_s.append(v2)
    y_prompt = rms_norm(hp, final_norm_g)
    y_sample = rms_norm(hs, final_norm_g)
    new_conv_prompt = jnp.stack(conv_p)
    new_k_prompt = jnp.stack(k_p)
    new_v_prompt = jnp.stack(v_p)
    new_conv_sample = jnp.stack(conv_s)
    new_k_sample = jnp.stack(k_s)
    new_v_sample = jnp.stack(v_s)
    return (y_prompt, y_sample, new_conv_prompt, new_k_prompt, new_v_prompt,
            new_conv_sample, new_k_sample, new_v_sample)
```

```python
import contextlib
import numpy as np
import concourse.bass as bass
import concourse.mybir as mybir
from concourse.bass_utils import run_bass_kernel_spmd

F32 = mybir.dt.float32
BF16 = mybir.dt.bfloat16
AF = mybir.ActivationFunctionType
ALU = mybir.AluOpType

NCORES = 8
D = 1024
NCH = 8
SEQ = 4096
OWN = 2048
HALO = 256
NS = 16
PAST = 16384
CK = 31
CB = 30
EPS = 1e-6
NCHUNK = 84
NSLAB = NCHUNK // 2
NWB = 4
TMAX = 512
NEG = -30000.0
PP_G, PP_CB, PP_LG, PP_LB, PP_GF, PP_CW = 0, 16, 32, 48, 64, 72
PP_N = 72 + 2 * 8 * CK

ENGS = ("pe", "act", "dve", "pool", "sp")


class Buf:
    __slots__ = ("name", "last_w", "readers", "sem", "dma_cnt", "excl")

    def __init__(self, name, sem=None):
        self.excl = False
        self.name = name
        self.last_w = None
        self.readers = []
        self.sem = sem
        self.dma_cnt = 0


class Op:
    __slots__ = ("eng", "fn", "deps", "is_dma", "sig", "idx")

    def __init__(self, eng, fn, is_dma):
        self.eng = eng
        self.fn = fn
        self.deps = []
        self.is_dma = is_dma
        self.sig = None
        self.idx = None


class Sched:
    def __init__(self, nc):
        self.nc = nc
        self.ops = []
        self._sem_ctx = []
        self.dma_bufs = []

    def new_sem(self, name):
        ctx = self.nc.semaphore(name)
        s = ctx.__enter__()
        self._sem_ctx.append(ctx)
        return s

    def close(self):
        for c in reversed(self._sem_ctx):
            c.__exit__(None, None, None)

    def buf(self, name, dma=False):
        b = Buf(name, self.new_sem("d_" + name) if dma else None)
        if dma:
            self.dma_bufs.append(b)
        return b

    def _add(self, op, reads, writes):
        deps = set()
        xr = [b for b in reads if b.excl]
        if xr:
            reads = [b for b in reads if not b.excl]
            writes = list(writes) + [b for b in xr if b not in writes]
        for b in reads:
            if b.last_w is not None:
                deps.add(b.last_w)
        for b in writes:
            if b.last_w is not None:
                deps.add(b.last_w)
            for r in b.readers:
                deps.add(r)
        deps.discard(op)
        op.deps = list(deps)
        for b in reads:
            b.readers.append(op)
        for b in writes:
            b.last_w = op
            b.readers = []
        op.idx = len(self.ops)
        self.ops.append(op)
        return op

    def op(self, eng, fn, reads=(), writes=()):
        return self._add(Op(eng, fn, False), reads, writes)

    def dma(self, eng, out_ap, in_ap, sembuf, reads=(), writes=()):
        def fn(e):
            return e.dma_start(out=out_ap, in_=in_ap)
        o = Op(eng, fn, True)
        sembuf.dma_cnt += 1
        o.sig = (sembuf.sem, 16 * sembuf.dma_cnt)
        return self._add(o, reads, list(writes) + [sembuf])

    def emit(self, final_wait_bufs=()):
        nc = self.nc
        need = set()
        for o in self.ops:
            for d in o.deps:
                if d.is_dma:
                    continue
                if d.eng == "pe" and o.eng == "pe" and not o.is_dma:
                    continue
                need.add(d)
        esem = {e: self.new_sem("e_" + e) for e in ENGS}
        cnt = {e: 0 for e in ENGS}
        for o in self.ops:
            if o in need:
                cnt[o.eng] += 1
                o.sig = (esem[o.eng], cnt[o.eng])
        per = {e: [o for o in self.ops if o.eng == e] for e in ENGS}

        def run(eng_name, eng):
            waited = {}
            for o in per[eng_name]:
                req = {}
                for d in o.deps:
                    if d.sig is None:
                        continue
                    if (not d.is_dma) and d.eng == "pe" and eng_name == "pe" and not o.is_dma:
                        continue
                    s, v = d.sig
                    k = id(s)
                    if k not in req or req[k][1] < v:
                        req[k] = (s, v)
                for k, (s, v) in req.items():
                    if waited.get(k, 0) >= v:
                        continue
                    eng.wait_ge(s, v)
                    waited[k] = v
                inst = o.fn(eng)
                if o.sig is not None:
                    inst.then_inc(o.sig[0], 16 if o.is_dma else 1)
            if eng_name == "sp":
                for b in self.dma_bufs:
                    if b.dma_cnt:
                        eng.wait_ge(b.sem, 16 * b.dma_cnt)

        with nc.Block() as block:
            @block.tensor
            def _(e):
                run("pe", e)

            @block.scalar
            def _(e):
                run("act", e)

            @block.vector
            def _(e):
                run("dve", e)

            @block.gpsimd
            def _(e):
                run("pool", e)

            @block.sync
            def _(e):
                run("sp", e)


def _stream_cols():
    C = []
    o_glu, o_ga, o_q, o_k, o_v, o_gb, o_mga, o_mgb = 0, 2048, 3072, 4096, 4224, 4352, 5376, 6400
    for j in range(8):
        C.append(("in", o_glu + j * 128))
        C.append(("in", o_glu + 1024 + j * 128))
    for j in range(8):
        C.append(("in", o_q + j * 128))
    C.append(("k0", o_k))
    C.append(("k1", o_k + 64))
    C.append(("in", o_v))
    for j in range(8):
        C.append(("in", o_gb + j * 128))
    for j in range(8):
        C.append(("in", o_ga + j * 128))
    for j in range(8):
        C.append(("in", o_mga + j * 128))
        C.append(("co", j * 128))
    for j in range(8):
        C.append(("in", o_mgb + j * 128))
        C.append(("ao", j * 128))
    for j in range(8):
        C.append(("out", j * 128))
    assert len(C) == 83
    C.append(("pad", 0))
    return C


def _build_wstream(w_in, wco, wao, wout):
    C = _stream_cols()
    out = np.zeros((2, NCHUNK, D, 128), np.float32)
    for l in range(2):
        for i, (src, c0) in enumerate(C):
            if src == "in":
                out[l, i] = w_in[l][:, c0:c0 + 128]
            elif src in ("k0", "k1"):
                out[l, i, :, 0:64] = w_in[l][:, c0:c0 + 64]
                out[l, i, :, 64:128] = w_in[l][:, c0:c0 + 64]
            elif src == "co":
                out[l, i] = wco[l][:, c0:c0 + 128]
            elif src == "ao":
                out[l, i] = wao[l][:, c0:c0 + 128]
            elif src == "out":
                out[l, i] = wout[l][:, c0:c0 + 128]
    out = out.reshape(2, NSLAB, 2, NCH, 128, 128)
    out = out.transpose(0, 1, 4, 3, 2, 5)
    return np.ascontiguousarray(out.reshape(2, NSLAB, 128, NCH, 256))


TILES = [
    dict(name="A", Tp=512, Ts=0, xrow0=0, rope0=0, first_qb=2, halo=256, yblk0=2, last=False, yrow0=0),
    dict(name="B", Tp=512, Ts=0, xrow0=512, rope0=512, first_qb=-1, halo=0, yblk0=0, last=False, yrow0=256),
    dict(name="C", Tp=512, Ts=0, xrow0=1024, rope0=1024, first_qb=-1, halo=0, yblk0=0, last=False, yrow0=768),
    dict(name="D", Tp=384, Ts=0, xrow0=1536, rope0=1536, first_qb=-1, halo=0, yblk0=0, last=False, yrow0=1280),
    dict(name="E", Tp=384, Ts=NS, xrow0=1920, rope0=1920, first_qb=-1, halo=0, yblk0=0, last=True, yrow0=1664),
]
NCOLS = 272 + 2048
DBG_ROPE = 9
DBG_STOP = None


def build_program():
    nc = bass.Bass("TRN2", target_bir_lowering=False)
    dt = lambda n, s, k, d=F32: nc.dram_tensor(n, s, d, kind=k).ap()
    xp = dt("xp", [HALO + OWN, D], "ExternalInput")
    xs = dt("xs", [NS, D], "ExternalInput")
    stc = dt("stc", [2, NS, CB, D], "ExternalInput")
    ckd = dt("ck", [2, NS, 128, 128], "ExternalInput")
    cvd = dt("cv", [2, NS, 128, 128], "ExternalInput")
    wst = dt("wst", [2, NSLAB, 128, NCH, 256], "ExternalInput")
    ppd = dt("pp", [128, PP_N], "ExternalInput")
    snk = dt("snk", [32], "ExternalInput")
    rcd = dt("ropec", [128, NCOLS], "ExternalInput")
    rsd = dt("ropes", [128, NCOLS], "ExternalInput")
    mskd = dt("msk", [128, 3, 512], "ExternalInput")
    permd = dt("perm", [128, 128], "ExternalInput")
    vald = dt("valid", [128, 1], "ExternalInput")
    y_o = dt("y", [OWN, D], "ExternalOutput")
    ys_o = dt("ys", [NS, D], "ExternalOutput")
    ncp_o = dt("ncp", [2, CB, D], "ExternalOutput")
    nkp_o = dt("nkp", [2, 128, 128], "ExternalOutput")
    nvp_o = dt("nvp", [2, 128, 128], "ExternalOutput")
    ncs_o = dt("ncs", [2, NS, CB, D], "ExternalOutput")
    nks_o = dt("nks", [2, NS, 128, 128], "ExternalOutput")
    nvs_o = dt("nvs", [2, NS, 128, 128], "ExternalOutput")

    S = Sched(nc)
    es_ = contextlib.ExitStack()
    with es_:
        def sb(name, shape, d=F32):
            return es_.enter_context(nc.sbuf_tensor(name, shape, d))

        xT = sb("xT", [128, NCH, TMAX]); b_xT = [S.buf("xT%d" % j) for j in range(NCH)]
        big = [sb("big%d" % i, [128, D]) for i in range(2)]
        b_big = [S.buf("big%d" % i, dma=True) for i in range(2)]
        hT = sb("hT", [128, NCH, TMAX], BF16); b_hT = S.buf("hT")
        rot16 = [sb("r16_%d" % i, [128, TMAX], BF16) for i in range(4)]
        b_rot16 = [S.buf("r16_%d" % i) for i in range(4)]
        uT = sb("uT", [128, NCH, CB + TMAX], BF16); b_uT = [S.buf("uT%d" % j) for j in range(NCH)]
        uh = sb("uh", [128, 2, NCH, CB], BF16); b_uh = [S.buf("uh%d" % l) for l in range(2)]
        qT = sb("qT", [128, NCH, TMAX], BF16)
        b_q = [[S.buf("q%d_%d" % (kv, qb)) for qb in range(5)] for kv in range(2)]
        qraw = [sb("qraw%d" % i, [128, TMAX], BF16) for i in range(2)]
        b_qraw = [S.buf("qraw%d" % i) for i in range(2)]
        kT = [sb("kT%d" % l, [128, 2, 2, 128 + TMAX], BF16) for l in range(2)]
        b_kT = [S.buf("kT%d" % l) for l in range(2)]
        V4 = [sb("V4_%d" % l, [128, 5, 2, 2, 128], BF16) for l in range(2)]
        b_V4 = [S.buf("V4_%d" % l) for l in range(2)]
        sgb = sb("sgb", [128, NCH, TMAX], BF16); b_sgb = S.buf("sgb")
        Dg = [sb("Dg%d" % i, [128, 16, 128], BF16) for i in range(2)]
        b_Dg = [S.buf("Dg%d" % i) for i in range(2)]
        cF = sb("cF", [128, NCH, TMAX]); b_cF = [S.buf("cF%d" % j) for j in range(NCH)]
        sga = [sb("sga%d" % i, [128, TMAX], BF16) for i in range(2)]
        b_sga = [S.buf("sga%d" % i) for i in range(2)]
        cc = sb("cc", [128, NCH, TMAX], BF16); b_cc = S.buf("cc")
        yT = cc; b_yT = b_cc
        pT = [sb("pT%d" % i, [128, 2, TMAX], BF16) for i in range(4)]
        b_pT = [S.buf("pT%d" % i) for i in range(4)]
        tmp = [sb("tmp%d" % i, [128, TMAX]) for i in range(6)]
        b_tmp = [S.buf("tmp%d" % i) for i in range(6)]
        st = [sb("st%d" % i, [128, TMAX]) for i in range(4)]
        b_st = [S.buf("st%d" % i) for i in range(4)]
        ropeC = sb("ropeC", [128, TMAX]); ropeS = sb("ropeS", [128, TMAX]); b_rope = S.buf("rope", dma=True)
        msk = sb("mskb", [128, 3, 512], BF16)
        identb = sb("identb", [128, 128], BF16)
        identf = sb("identf", [128, 128])
        permb = sb("permb", [128, 128], BF16)
        onesS = sb("onesS", [128, 128], BF16)
        wbuf = [sb("wb%d" % i, [128, NCH, 256], BF16) for i in range(NWB)]
        b_wbuf = [S.buf("wb%d" % i, dma=True) for i in range(NWB)]
        pp = sb("pp_sb", [128, PP_N])
        cwh = sb("cwh", [128, 2, NCH, CK])
        esk = sb("esk", [128, 32])
        valid = sb("valid_sb", [128, 1])
        cst = sb("cst", [128, 2])
        b_const = S.buf("const", dma=True)
        b_const2 = S.buf("const2", dma=True)
        b_init = S.buf("init")
        ufin = sb("ufin", [128, NCH, CB]); b_ufin = S.buf("ufin")
        usf = sb("usf", [128, NCH, NS]); b_usf = S.buf("usf")
        kfin = sb("kfin", [128, 2, 128]); b_kfin = S.buf("kfin")
        ksf = sb("ksf", [128, 2, NS]); b_ksf = S.buf("ksf")
        vfin = sb("vfin", [128, 128]); b_vfin = S.buf("vfin", dma=True)
        nkb = sb("nkb", [128, 2, 64]); b_nkb = S.buf("nkb", dma=True)
        knew = sb("knew", [NS, 2, 64]); b_knew = S.buf("knew", dma=True)
        vnew = sb("vnew", [NS, 128]); b_vnew = S.buf("vnew", dma=True)
        stT = sb("stT", [128, NCH, NS, CB], BF16); b_stT = S.buf("stT")
        Ks2 = [sb("Ks%d" % i, [128, 4, 128]) for i in range(2)]; b_Ks2 = [S.buf("Ks%d" % i, dma=True) for i in range(2)]
        Vs2 = [sb("Vs%d" % i, [128, 4, 128]) for i in range(2)]; b_Vs2 = [S.buf("Vs%d" % i, dma=True) for i in range(2)]
        b_Ks2b = [S.buf("Ksb%d" % i, dma=True) for i in range(2)]; b_Vs2b = [S.buf("Vsb%d" % i, dma=True) for i in range(2)]
        KTs = sb("KTs", [128, 4, 2, 2, 128], BF16); b_KTs = S.buf("KTs")
        V4s = sb("V4s", [128, 4, 2, 2, 128], BF16); b_V4s = S.buf("V4s")
        pTs = sb("pTs", [128, 256], BF16); b_pTs = S.buf("pTs")
        b_dd = S.buf("dd", dma=True)
        ps = [es_.enter_context(nc.psum_tensor("ps%d" % i, [128, 512], F32)) for i in range(8)]
        b_ps = [S.buf("ps%d" % i) for i in range(8)]
        for b_ in b_ps:
            b_.excl = True
        bank_ctr = [0]

        nbank = [6]

        def bank():
            b = bank_ctr[0] % nbank[0]
            bank_ctr[0] += 1
            return b

        rot_ctr = {}

        def rot(key, n):
            v = rot_ctr.get(key, 0)
            rot_ctr[key] = v + 1
            return v % n

        def tmpf():
            i = rot("tmp", 6)
            return tmp[i], b_tmp[i]

        def r16():
            i = rot("r16", 4)
            return rot16[i], b_rot16[i]

        S.dma("sp", pp[:], ppd, b_const)
        S.dma("sp", esk[:], snk.partition_broadcast(128), b_const)
        S.dma("sp", valid[:], vald, b_const)
        S.dma("pool", msk[:], mskd, b_const2)
        S.dma("pool", permb[:], permd, b_const2)

        ini = lambda fn, w=(), r=(): S.op("pool", fn, reads=[b_init] + list(r), writes=[b_init] + list(w))
        ini(lambda e: e.memset(identf[:], 1.0))
        ini(lambda e: e.affine_select(out=identf[:], in_=identf[:], pattern=[[-1, 128]], compare_op=ALU.is_equal,
                                      fill=0.0, base=0, channel_multiplier=1))
        ini(lambda e: e.tensor_copy(out=identb[:], in_=identf[:]))
        ini(lambda e: e.memset(onesS[:], 1.0 / 1024.0))
        ini(lambda e: e.memset(cst[:, 0:1], EPS))
        ini(lambda e: e.memset(cst[:, 1:2], -0.5))
        ini(lambda e: e.memset(uh[:], 0.0), w=b_uh)
        for l0 in range(2):
            ini((lambda l0: lambda e: e.memset(kT[l0][:], 0.0))(l0), w=[b_kT[l0]])
            ini((lambda l0: lambda e: e.memset(V4[l0][:], 0.0))(l0), w=[b_V4[l0]])
            ini((lambda l0: lambda e: e.memset(V4[l0][:, :, 0, :, 64:128], 1.0))(l0), w=[b_V4[l0]])
            ini((lambda l0: lambda e: e.memset(V4[l0][:, :, 1, :, 0:64], 1.0))(l0), w=[b_V4[l0]])
        ini(lambda e: e.memset(KTs[:], 0.0), w=[b_KTs])
        ini(lambda e: e.memset(V4s[:, :, 0, :, 64:128], 1.0), w=[b_V4s])
        ini(lambda e: e.memset(V4s[:, :, 1, :, 0:64], 1.0), w=[b_V4s])
        ini(lambda e: e.memset(uT[:], 0.0), w=b_uT)
        S.op("pool", lambda e: e.tensor_scalar(out=cwh[:].rearrange("p a b c -> p (a b c)"), in0=pp[:, PP_CW:PP_N],
                                                scalar1=0.5, scalar2=None, op0=ALU.mult),
             reads=[b_const], writes=[b_init])
        S.op("act", lambda e: e.activation(out=esk[:], in_=esk[:], func=AF.Exp), reads=[b_const], writes=[b_const])

        wstate = dict(issued=0, total=5 * 2 * NSLAB)
        order = [(ti, l) for ti in range(len(TILES)) for l in range(2)]

        def w_issue(upto):
            while wstate["issued"] <= upto and wstate["issued"] < wstate["total"]:
                g = wstate["issued"]
                tl, s = divmod(g, NSLAB)
                l = tl % 2
                S.dma("pool", wbuf[g % NWB][:], wst[l, (s % 30) if DBG_ROPE == 5 else s], b_wbuf[g % NWB])
                wstate["issued"] += 1

        wctr = [0]

        def wnext():
            ci = wctr[0]
            wctr[0] += 1
            g = ci // 2
            w_issue(g + NWB - 1)
            t = wbuf[g % NWB]
            return t[:, :, (ci % 2) * 128:(ci % 2) * 128 + 128], b_wbuf[g % NWB]

        def proj(act, act_bufs, N, b=None):
            wap, wb = wnext()
            if b is None:
                b = bank()

            def f(e):
                for kc in range(NCH):
                    i = e.matmul(ps[b][:, 0:N], lhsT=wap[:, kc, :], rhs=act[:, kc, 0:N], start=(kc == 0), stop=(kc == NCH - 1))
                return i
            S.op("pe", f, reads=list(act_bufs) + [wb], writes=[b_ps[b]])
            return b

        def rms_accum(T, j, src, src_buf):
            r, rb = r16()
            S.op("act", (lambda j, r: lambda e: e.activation(out=r[:, 0:T], in_=src[:, j, 0:T], func=AF.Square))(j, r),
                 reads=[src_buf], writes=[rb])
            def mm():
                S.op("pe", (lambda j, r: lambda e: e.matmul(ps[6][:, 0:T], lhsT=onesS[:], rhs=r[:, 0:T], start=(j == 0), stop=(j == NCH - 1)))(j, r),
                     reads=[rb, b_init], writes=[b_ps[6]])
            return mm

        def rms_finish(T):
            S.op("act", lambda e: e.activation(out=st[0][:, 0:T], in_=ps[6][:, 0:T], func=AF.Sqrt, bias=cst[:, 0:1]),
                 reads=[b_ps[6], b_init], writes=[b_st[0]])
            S.op("dve", lambda e: e.reciprocal(out=st[1][:, 0:T], in_=st[0][:, 0:T]), reads=[b_st[0]], writes=[b_st[1]])

        def rms_stats(T, src, src_bufs):
            prev = None
            for j in range(NCH):
                m = rms_accum(T, j, src, src_bufs[j])
                if prev is not None:
                    prev()
                prev = m
            prev()
            rms_finish(T)

        halt = [False]

        def ck(tl, l, ph):
            if DBG_STOP is not None and DBG_STOP == (tl["name"], l, ph):
                halt[0] = True
            return halt[0]

        def do_tile(ti, tl):
            if halt[0]:
                return
            Tp, Ts = tl["Tp"], tl["Ts"]
            T = Tp + Ts
            nb = Tp // 128
            isH = Ts > 0
            S.dma("sp", ropeC[:, 0:T], rcd[:, tl["rope0"]:tl["rope0"] + T], b_rope)
            S.dma("sp", ropeS[:, 0:T], rsd[:, tl["rope0"]:tl["rope0"] + T], b_rope)
            for blk in range(nb):
                bi = rot("big", 2)
                S.dma("sp", big[bi][:], xp[tl["xrow0"] + blk * 128: tl["xrow0"] + (blk + 1) * 128, :], b_big[bi])
                for g in range(2):
                    b = bank()

                    def ftr(e, bi=bi, g=g, b=b):
                        for jj in range(4):
                            i = e.transpose(ps[b][:, jj * 128:(jj + 1) * 128], big[bi][:, (4 * g + jj) * 128:(4 * g + jj + 1) * 128], identf[:])
                        return i
                    S.op("pe", ftr, reads=[b_big[bi], b_init], writes=[b_ps[b]])
                    S.op("act", (lambda b, g, blk: lambda e: e.activation(
                        out=xT[:, 4 * g:4 * g + 4, blk * 128:(blk + 1) * 128],
                        in_=ps[b][:].rearrange("p (a c) -> p a c", a=4), func=AF.Copy))(b, g, blk),
                        reads=[b_ps[b]], writes=b_xT[4 * g:4 * g + 4])
            if Ts:
                bi = rot("big", 2)
                S.dma("sp", big[bi][0:NS, :], xs, b_big[bi])
                b = bank()

                def ftrs(e, bi=bi, b=b):
                    for j in range(NCH):
                        i = e.transpose(ps[b][:, j * NS:(j + 1) * NS], big[bi][0:NS, j * 128:(j + 1) * 128], identf[0:NS, 0:NS])
                    return i
                S.op("pe", ftrs, reads=[b_big[bi], b_init], writes=[b_ps[b]])
                S.op("act", (lambda b: lambda e: e.activation(out=xT[:, :, Tp:T], in_=ps[b][:, 0:NCH * NS].rearrange("p (a c) -> p a c", a=NCH),
                                                             func=AF.Copy))(b), reads=[b_ps[b]], writes=b_xT)

            if ck(tl, -1, 'P0'):
                return

            def do_layer(l):
                if halt[0]:
                    return
                gcol = lambda base, j: pp[:, base + l * 8 + j: base + l * 8 + j + 1]
                if l == 0:
                    rms_stats(T, xT, b_xT)
                for j in range(NCH):
                    S.op("dve", (lambda j: lambda e: e.scalar_tensor_tensor(
                        out=hT[:, j, 0:T], in0=xT[:, j, 0:T], scalar=gcol(PP_G, j), in1=st[1][:, 0:T],
                        op0=ALU.mult, op1=ALU.mult))(j), reads=[b_xT[j], b_st[1], b_const], writes=[b_hT])
                if ck(tl, l, 'P1'):
                    return
                if isH:
                    for r4 in range(4):
                        bi = rot("big", 2)
                        S.dma("sp", big[bi][0:120, :], stc[l, 4 * r4:4 * r4 + 4].rearrange("b j d -> (b j) d"), b_big[bi])
                        for g in range(2):
                            b = bank()

                            def ftst(e, bi=bi, g=g, b=b):
                                for jj in range(4):
                                    i = e.transpose(ps[b][:, jj * 120:(jj + 1) * 120], big[bi][0:120, (4 * g + jj) * 128:(4 * g + jj + 1) * 128],
                                                    identf[0:120, 0:120])
                                return i
                            S.op("pe", ftst, reads=[b_big[bi], b_init], writes=[b_ps[b]])
                            S.op("act", (lambda b, g, r4: lambda e: e.activation(
                                out=stT[:, 4 * g:4 * g + 4, 4 * r4:4 * r4 + 4, :].rearrange("p a b j -> p a (b j)"),
                                in_=ps[b][:, 0:480].rearrange("p (a c) -> p a c", a=4), func=AF.Copy, scale=2.0))(b, g, r4),
                                reads=[b_ps[b]], writes=[b_stT])
                    S.dma("sp", ncs_o[l, :, 0:CB - 1, :], stc[l, :, 1:CB, :], b_dd)
                    S.dma("sp", nks_o[l, :, 0:127, :], ckd[l, :, 1:128, :], b_dd)
                    S.dma("sp", nvs_o[l, :, 0:127, :], cvd[l, :, 1:128, :], b_dd)
                if ck(tl, l, 'S0'):
                    return
                S.op("pool", lambda e: e.tensor_copy(out=uT[:, :, 0:CB], in_=uh[:, l, :, :]), reads=[b_uh[l]], writes=b_uT)
                for j in range(NCH):
                    ba = proj(hT, [b_hT], T)
                    bb = proj(hT, [b_hT], T)
                    tg, tgb = tmpf()
                    S.op("act", (lambda bb, tg: lambda e: e.activation(out=tg[:, 0:T], in_=ps[bb][:, 0:T], func=AF.Tanh, scale=0.5))(bb, tg),
                         reads=[b_ps[bb]], writes=[tgb])
                    S.op("dve", (lambda j, ba, tg: lambda e: e.scalar_tensor_tensor(
                        out=uT[:, j, CB:CB + T], in0=tg[:, 0:T], scalar=1.0, in1=ps[ba][:, 0:T], op0=ALU.add, op1=ALU.mult))(j, ba, tg),
                        reads=[tgb, b_ps[ba]], writes=[b_uT[j]])
                    if tl["last"]:
                        S.op("dve", (lambda j, ba, tg: lambda e: e.scalar_tensor_tensor(
                            out=ufin[:, j, :], in0=tg[:, Tp - CB:Tp], scalar=1.0, in1=ps[ba][:, Tp - CB:Tp], op0=ALU.add, op1=ALU.mult))(j, ba, tg),
                            reads=[tgb, b_ps[ba]], writes=[b_ufin])
                    if Ts:
                        S.op("dve", (lambda j, ba, tg: lambda e: e.scalar_tensor_tensor(
                            out=usf[:, j, :], in0=tg[:, Tp:T], scalar=1.0, in1=ps[ba][:, Tp:T], op0=ALU.add, op1=ALU.mult))(j, ba, tg),
                            reads=[tgb, b_ps[ba]], writes=[b_usf])
                if ck(tl, l, 'P2a'):
                    return
                def rope_chunk(bq, dst_ap_fn, dst_bufs, extra=None):
                    qi = rot("qraw", 2)
                    S.op("act", (lambda bq, qi: lambda e: e.activation(out=qraw[qi][:, 0:T], in_=ps[bq][:, 0:T], func=AF.Copy))(bq, qi),
                         reads=[b_ps[bq]], writes=[b_qraw[qi]])
                    def rest():
                        bs = bank()
                        S.op("pe", (lambda bs, qi: lambda e: e.matmul(ps[bs][:, 0:T], lhsT=permb[:], rhs=qraw[qi][:, 0:T], start=True, stop=True))(bs, qi),
                             reads=[b_qraw[qi], b_const2], writes=[b_ps[bs]])
                        t1, t1b = tmpf()
                        t2, t2b = tmpf()
                        S.op("dve", (lambda bq, t1: lambda e: e.tensor_tensor(out=t1[:, 0:T], in0=ps[bq][:, 0:T], in1=ropeC[:, 0:T], op=ALU.mult))(bq, t1),
                             reads=[b_ps[bq], b_rope], writes=[t1b])
                        S.op("dve", (lambda bs, t2: lambda e: e.tensor_tensor(out=t2[:, 0:T], in0=ps[bs][:, 0:T], in1=ropeS[:, 0:T], op=ALU.mult))(bs, t2),
                             reads=[b_ps[bs], b_rope], writes=[t2b])
                        dsts = dst_ap_fn(0, T)
                        if not isinstance(dsts, list):
                            dsts = [(dsts, slice(0, 128))]
                        for (dap, rws) in dsts:
                            S.op("dve", (lambda t1, t2, dap, rws: lambda e: e.tensor_tensor(out=dap, in0=t1[rws, 0:T], in1=t2[rws, 0:T], op=ALU.add))(t1, t2, dap, rws),
                                 reads=[t1b, t2b], writes=dst_bufs)
                        if extra is not None:
                            for (oap, c0, c1, obuf) in extra:
                                S.op("dve", (lambda t1, t2, oap, c0, c1: lambda e: e.tensor_tensor(out=oap, in0=t1[:, c0:c1], in1=t2[:, c0:c1], op=ALU.add))(t1, t2, oap, c0, c1),
                                     reads=[t1b, t2b], writes=[obuf])

                    return rest

                pend_rope = None
                for j in range(NCH):
                    bq = proj(hT, [b_hT], T)
                    if pend_rope is not None:
                        pend_rope()
                    pend_rope = rope_chunk(bq, (lambda j: lambda c0, c1: qT[:, j, c0:c1])(j), b_q[j // 4])
                for kv in range(2):
                    bk = proj(hT, [b_hT], T)
                    if pend_rope is not None:
                        pend_rope()
                    extra = []
                    if tl["last"]:
                        extra.append((kfin[:, kv, :], Tp - 128, Tp, b_kfin))
                    if Ts:
                        extra.append((ksf[:, kv, :], Tp, T, b_ksf))
                    pend_rope = rope_chunk(bk, (lambda kv: lambda c0, c1: [(kT[l][0:64, kv, 0, 128 + c0:128 + c1], slice(0, 64)),
                                                               (kT[l][64:128, kv, 1, 128 + c0:128 + c1], slice(64, 128))])(kv), [b_kT[l]], extra)
                pend_rope()
                wv, wvb = wnext()
                bv = bank()

                def fv(e, bv=bv, wv=wv):
                    for blk in range(nb):
                        for kc in range(NCH):
                            i = e.matmul(ps[bv][:, blk * 128:(blk + 1) * 128], lhsT=hT[:, kc, blk * 128:(blk + 1) * 128], rhs=wv[:, kc, :],
                                         start=(kc == 0), stop=(kc == NCH - 1))
                    if Ts:
                        for kc in range(NCH):
                            i = e.matmul(ps[bv][0:NS, nb * 128:(nb + 1) * 128], lhsT=hT[:, kc, Tp:T], rhs=wv[:, kc, :],
                                         start=(kc == 0), stop=(kc == NCH - 1))
                    return i
                S.op("pe", fv, reads=[b_hT, wvb], writes=[b_ps[bv]])
                psv = ps[bv][:, 0:nb * 128].rearrange("p (b k d) -> p b k d", b=nb, k=2)
                S.op("act", (lambda psv: lambda e: e.activation(out=V4[l][:, 1:1 + nb, 0, :, 0:64], in_=psv, func=AF.Copy))(psv),
                     reads=[b_ps[bv]], writes=[b_V4[l]])
                S.op("act", (lambda psv: lambda e: e.activation(out=V4[l][:, 1:1 + nb, 1, :, 64:128], in_=psv, func=AF.Copy))(psv),
                     reads=[b_ps[bv]], writes=[b_V4[l]])
                if tl["last"]:
                    S.op("dve", (lambda bv: lambda e: e.tensor_copy(out=vfin[:], in_=ps[bv][:, (nb - 1) * 128:nb * 128]))(bv),
                         reads=[b_ps[bv]], writes=[b_vfin])
                    S.dma("sp", nvp_o[l], vfin[:], b_vfin, reads=[b_vfin])
                if Ts:
                    S.op("dve", (lambda bv: lambda e: e.tensor_copy(out=vnew[:], in_=ps[bv][0:NS, nb * 128:(nb + 1) * 128]))(bv),
                         reads=[b_ps[bv]], writes=[b_vnew])
                    S.dma("sp", nvs_o[l, :, 127, :], vnew[:], b_vnew, reads=[b_vnew])
                if ck(tl, l, 'P2v'):
                    return
                for j in range(NCH):
                    bg = proj(hT, [b_hT], T)
                    S.op("act", (lambda j, bg: lambda e: e.activation(out=sgb[:, j, 0:T], in_=ps[bg][:, 0:T], func=AF.Silu))(j, bg),
                         reads=[b_ps[bg]], writes=[b_sgb])
                if ck(tl, l, 'P2b'):
                    return
                pend_stat = []
                for j in range(NCH):
                    bc = bank()
                    halves = []
                    for half in range(2):
                        t0, t1_ = (0, 16) if half == 0 else (16, CK)
                        di = rot("dg", 2)
                        nt = t1_ - t0
                        halves.append((di, t0, t1_))
                        S.op("dve", (lambda di, j, t0, nt: lambda e: e.tensor_tensor(
                            out=Dg[di][:, 0:nt, :], in0=identf[:].unsqueeze(1).broadcast_to([128, nt, 128]),
                            in1=cwh[:, l, j, t0:t0 + nt].unsqueeze(2).broadcast_to([128, nt, 128]), op=ALU.mult))(di, j, t0, nt),
                            reads=[b_init], writes=[b_Dg[di]])

                        def fconv(e, di=di, j=j, t0=t0, t1_=t1_, bc=bc):
                            for tap in range(t0, t1_):
                                i = e.matmul(ps[bc][:, 0:Tp], lhsT=Dg[di][:, tap - t0, :], rhs=uT[:, j, tap:tap + Tp],
                                             start=(tap == 0), stop=(tap == CK - 1))
                            return i
                        S.op("pe", fconv, reads=[b_Dg[di], b_uT[j]], writes=[b_ps[bc]])
                    if Ts:
                        for (di, t0, t1_) in halves:
                            def fconvs(e, di=di, j=j, t0=t0, t1_=t1_, bc=bc):
                                for tap in range(t0, t1_):
                                    rhs = stT[:, j, :, tap] if tap < CB else uT[:, j, CB + Tp:CB + T]
                                    i = e.matmul(ps[bc][:, Tp:T], lhsT=Dg[di][:, tap - t0, :], rhs=rhs,
                                                 start=(tap == 0), stop=(tap == CK - 1))
                                return i
                            S.op("pe", fconvs, reads=[b_Dg[di], b_uT[j], b_stT], writes=[b_ps[bc]])
                    S.op("act", (lambda j, bc: lambda e: e.activation(out=cF[:, j, 0:T], in_=ps[bc][:, 0:T], func=AF.Identity, bias=gcol(PP_CB, j)))(j, bc),
                         reads=[b_ps[bc], b_const], writes=[b_cF[j]])
                    r1, r1b = r16()
                    r2, r2b = r16()
                    S.op("act", (lambda j, r1: lambda e: e.activation(out=r1[:, 0:T], in_=cF[:, j, 0:T], func=AF.Copy))(j, r1),
                         reads=[b_cF[j]], writes=[r1b])
                    S.op("act", (lambda j, r2: lambda e: e.activation(out=r2[:, 0:T], in_=cF[:, j, 0:T], func=AF.Square))(j, r2),
                         reads=[b_cF[j]], writes=[r2b])
                    def stat_mm(j=j, r1=r1, r2=r2, r1b=r1b, r2b=r2b):
                        S.op("pe", lambda e: e.matmul(ps[6][:, 0:T], lhsT=onesS[:], rhs=r1[:, 0:T], start=(j == 0), stop=(j == NCH - 1)),
                             reads=[r1b, b_init], writes=[b_ps[6]])
                        S.op("pe", lambda e: e.matmul(ps[7][:, 0:T], lhsT=onesS[:], rhs=r2[:, 0:T], start=(j == 0), stop=(j == NCH - 1)),
                             reads=[r2b, b_init], writes=[b_ps[7]])
                    if pend_stat:
                        pend_stat.pop(0)()
                    pend_stat.append(stat_mm)
                while pend_stat:
                    pend_stat.pop(0)()
                S.op("pool", lambda e: e.tensor_copy(out=uh[:, l, :, :], in_=uT[:, :, Tp:Tp + CB]), reads=b_uT, writes=[b_uh[l]])
                S.op("act", lambda e: e.activation(out=st[0][:, 0:T], in_=ps[6][:, 0:T], func=AF.Copy), reads=[b_ps[6]], writes=[b_st[0]])
                S.op("dve", lambda e: e.tensor_tensor(out=st[1][:, 0:T], in0=st[0][:, 0:T], in1=st[0][:, 0:T], op=ALU.mult),
                     reads=[b_st[0]], writes=[b_st[1]])
                S.op("dve", lambda e: e.scalar_tensor_tensor(out=st[2][:, 0:T], in0=ps[7][:, 0:T], scalar=EPS, in1=st[1][:, 0:T],
                                                              op0=ALU.add, op1=ALU.subtract),
                     reads=[b_ps[7], b_st[1]], writes=[b_st[2]])
                S.op("act", lambda e: e.activation(out=st[1][:, 0:T], in_=st[2][:, 0:T], func=AF.Sqrt), reads=[b_st[2]], writes=[b_st[1]])
                S.op("dve", lambda e: e.reciprocal(out=st[2][:, 0:T], in_=st[1][:, 0:T]), reads=[b_st[1]], writes=[b_st[2]])
                S.op("dve", lambda e: e.scalar_tensor_tensor(out=st[3][:, 0:T], in0=st[0][:, 0:T], scalar=-1.0, in1=st[2][:, 0:T],
                                                               op0=ALU.mult, op1=ALU.mult),
                     reads=[b_st[0], b_st[2]], writes=[b_st[3]])
                for j in range(NCH):
                    bg = proj(hT, [b_hT], T)
                    si = rot("sga", 2)
                    S.op("act", (lambda bg, si: lambda e: e.activation(out=sga[si][:, 0:T], in_=ps[bg][:, 0:T], func=AF.Silu))(bg, si),
                         reads=[b_ps[bg]], writes=[b_sga[si]])
                    t1, t1b = tmpf()
                    t2, t2b = tmpf()
                    S.op("dve", (lambda j, t1: lambda e: e.tensor_tensor(out=t1[:, 0:T], in0=cF[:, j, 0:T], in1=st[2][:, 0:T], op=ALU.mult))(j, t1),
                         reads=[b_cF[j], b_st[2]], writes=[t1b])
                    S.op("dve", (lambda t1, t2: lambda e: e.tensor_tensor(out=t2[:, 0:T], in0=t1[:, 0:T], in1=st[3][:, 0:T], op=ALU.add))(t1, t2),
                         reads=[t1b, b_st[3]], writes=[t2b])
                    S.op("act", (lambda j, t2, t1: lambda e: e.activation(out=t1[:, 0:T], in_=t2[:, 0:T], func=AF.Silu,
                                                                        scale=gcol(PP_LG, j), bias=gcol(PP_LB, j)))(j, t2, t1),
                         reads=[t2b, b_const], writes=[t1b])
                    S.op("dve", (lambda j, t1, si: lambda e: e.tensor_tensor(out=cc[:, j, 0:T], in0=t1[:, 0:T], in1=sga[si][:, 0:T], op=ALU.mult))(j, t1, si),
                         reads=[t1b, b_sga[si]], writes=[b_cc])
                if ck(tl, l, 'P3'):
                    return
                if ck(tl, l, 'P4'):
                    return
                obT = qT
                groups = [(qb, kv, par) for qb in range(nb) for kv in range(2) for par in range(2)]
                pend = None

                def attn_pv(pair):
                    ta, tab = tmpf()
                    b3s = []
                    for (g, pi) in pair:
                        qb, kv, par = g
                        dh = slice((1 - par) * 64, (1 - par) * 64 + 64)
                        b3 = bank()
                        b3s.append(b3)

                        def fpv(e, b3=b3, qb=qb, kv=kv, par=par, pi=pi):
                            e.matmul(ps[b3][:], lhsT=V4[l][:, qb, par, kv, :], rhs=pT[pi][:, 0, :], start=True, stop=False)
                            return e.matmul(ps[b3][:], lhsT=V4[l][:, qb + 1, par, kv, :], rhs=pT[pi][:, 1, :], start=False, stop=True)
                        S.op("pe", fpv, reads=[b_V4[l], b_pT[pi]], writes=[b_ps[b3]])
                        hs = l * 16 + kv * 8 + par
                        S.op("dve", (lambda b3, dh, hs: lambda e: e.tensor_tensor(
                            out=ta[dh, :].rearrange("p (a c) -> p a c", a=4), in0=ps[b3][dh, :].rearrange("p (a c) -> p a c", a=4),
                            in1=esk[dh, hs:hs + 7:2].unsqueeze(2).broadcast_to([64, 4, 128]), op=ALU.add))(b3, dh, hs),
                            reads=[b_ps[b3], b_const], writes=[tab])
                    S.op("dve", lambda e: e.reciprocal(out=ta[:, :], in_=ta[:, :]), reads=[tab], writes=[tab])
                    for (g, pi), b3 in zip(pair, b3s):
                        qb, kv, par = g
                        oh = slice(par * 64, par * 64 + 64)
                        dh = slice((1 - par) * 64, (1 - par) * 64 + 64)
                        tb, tbb = tmpf()
                        S.op("dve", (lambda b3, oh, dh, tb: lambda e: e.tensor_tensor(out=tb[oh, :], in0=ps[b3][oh, :], in1=ta[dh, :], op=ALU.mult))(b3, oh, dh, tb),
                             reads=[b_ps[b3], tab], writes=[tbb])
                        S.op("pool", (lambda oh, tb, kv, qb: lambda e: e.tensor_tensor(
                            out=obT[oh, kv * 4:kv * 4 + 4, qb * 128:(qb + 1) * 128], in0=tb[oh, :].rearrange("p (a c) -> p a c", a=4),
                            in1=sgb[oh, kv * 4:kv * 4 + 4, qb * 128:(qb + 1) * 128], op=ALU.mult))(oh, tb, kv, qb),
                            reads=[tbb, b_sgb], writes=[b_q[kv][qb]])

                def attn_steps():
                    pairs = [[(qb, kv, 0), (qb, kv, 1)] for qb in range(nb) for kv in range(2)]
                    pend = None
                    for pr in pairs:
                        cur = []
                        for g in pr:
                            qb, kv, par = g
                            pi = rot("pT", 4)
                            mprev = msk[:, 2, :] if qb == tl["first_qb"] else msk[:, 0, :]
                            for kt in range(2):
                                bqk = bank()
                                mk = mprev if kt == 0 else msk[:, 1, :]

                                def fqk(e, bqk=bqk, kt=kt, mk=mk, qb=qb, kv=kv, par=par):
                                    e.matmul(ps[bqk][:].rearrange("p (a c) -> p a c", a=4), lhsT=kT[l][:, kv, par, (qb + kt) * 128:(qb + kt + 1) * 128],
                                             rhs=qT[:, kv * 4:kv * 4 + 4, qb * 128:(qb + 1) * 128], start=True, stop=False)
                                    return e.matmul(ps[bqk][:], lhsT=identb[:], rhs=mk, start=False, stop=True)
                                S.op("pe", fqk, reads=[b_kT[l], b_q[kv][qb], b_init, b_const2], writes=[b_ps[bqk]])
                                S.op("act", (lambda bqk, pi, kt: lambda e: e.activation(out=pT[pi][:, kt, :], in_=ps[bqk][:], func=AF.Exp, scale=0.125))(bqk, pi, kt),
                                     reads=[b_ps[bqk]], writes=[b_pT[pi]])
                            cur.append((g, pi))
                        if pend is not None:
                            attn_pv(pend)
                        pend = cur
                        yield
                    attn_pv(pend)
                    yield
                    S.op("pool", lambda e: e.tensor_copy(out=kT[l][:, :, :, 0:128], in_=kT[l][:, :, :, Tp:Tp + 128]), reads=[b_kT[l]], writes=[b_kT[l]])
                    S.op("pool", lambda e: e.tensor_copy(out=V4[l][:, 0], in_=V4[l][:, nb]), reads=[b_V4[l]], writes=[b_V4[l]])
                    if Ts:
                        bkn = bank()

                        def fkn(e, bkn=bkn):
                            e.transpose(ps[bkn][0:NS, 0:128], ksf[:, 0, :], identf[:])
                            return e.transpose(ps[bkn][0:NS, 128:256], ksf[:, 1, :], identf[:])
                        S.op("pe", fkn, reads=[b_ksf, b_init], writes=[b_ps[bkn]])
                        S.op("act", (lambda bkn: lambda e: e.activation(out=knew[:], in_=ps[bkn][0:NS, 0:256].rearrange("p (k d) -> p k d", k=2)[:, :, 0:64],
                                                                       func=AF.Copy))(bkn), reads=[b_ps[bkn]], writes=[b_knew])
                        S.dma("sp", nks_o[l, :, 127, :], knew[:].rearrange("p k d -> p (k d)"), b_knew, reads=[b_knew])
                        bss = 6
                        bos = 7
                        for g4 in range(4):
                            Ks, b_Ks, Vs, b_Vs = Ks2[g4 % 2], b_Ks2[g4 % 2], Vs2[g4 % 2], b_Vs2[g4 % 2]
                            b_Ksb, b_Vsb = b_Ks2b[g4 % 2], b_Vs2b[g4 % 2]
                            S.dma("sp", Ks[0:112, :, :], ckd[l, 4 * g4:4 * g4 + 4, 1:113, :].rearrange("b k d -> k b d"), b_Ks)
                            S.dma("sp", Ks[112:127, :, :], ckd[l, 4 * g4:4 * g4 + 4, 113:128, :].rearrange("b k d -> k b d"), b_Ksb, reads=[], writes=[])
                            S.dma("sp", Ks[127:128, :, :], knew[4 * g4:4 * g4 + 4, :, :].rearrange("p k d -> p (k d)"), b_Ksb, reads=[b_knew])
                            S.dma("sp", Vs[0:112, :, :], cvd[l, 4 * g4:4 * g4 + 4, 1:113, :].rearrange("b k d -> k b d"), b_Vs)
                            S.dma("sp", Vs[112:127, :, :], cvd[l, 4 * g4:4 * g4 + 4, 113:128, :].rearrange("b k d -> k b d"), b_Vsb)
                            S.dma("sp", Vs[127:128, :, :], vnew[4 * g4:4 * g4 + 4, :], b_Vsb, reads=[b_vnew])
                            bt = bank()

                            def ftk(e, bt=bt, Ks=Ks):
                                for i in range(4):
                                    r = e.transpose(ps[bt][:, i * 128:(i + 1) * 128], Ks[:, i, :], identf[:])
                                return r
                            S.op("pe", ftk, reads=[b_Ks, b_Ksb, b_init], writes=[b_ps[bt]])
                            pst = ps[bt][:].rearrange("p (a c) -> p a c", a=4)
                            S.op("act", (lambda pst: lambda e: e.activation(out=KTs[0:64, :, 0, 0, :], in_=pst[0:64], func=AF.Copy))(pst), reads=[b_ps[bt]], writes=[b_KTs])
                            S.op("dve", (lambda pst: lambda e: e.tensor_copy(out=KTs[64:128, :, 0, 1, :], in_=pst[0:64]))(pst), reads=[b_ps[bt]], writes=[b_KTs])
                            S.op("act", (lambda pst: lambda e: e.activation(out=KTs[0:64, :, 1, 0, :], in_=pst[64:128], func=AF.Copy))(pst), reads=[b_ps[bt]], writes=[b_KTs])
                            S.op("dve", (lambda pst: lambda e: e.tensor_copy(out=KTs[64:128, :, 1, 1, :], in_=pst[64:128]))(pst), reads=[b_ps[bt]], writes=[b_KTs])
                            vsv = Vs[:].rearrange("p b (k d) -> p b k d", k=2)
                            S.op("pool", (lambda vsv: lambda e: e.tensor_copy(out=V4s[:, :, 0, :, 0:64], in_=vsv))(vsv), reads=[b_Vs, b_Vsb], writes=[b_V4s])
                            S.op("pool", (lambda vsv: lambda e: e.tensor_copy(out=V4s[:, :, 1, :, 64:128], in_=vsv))(vsv), reads=[b_Vs, b_Vsb], writes=[b_V4s])

                            def fsqk(e, g4=g4):
                                for i in range(4):
                                    bsm = 4 * g4 + i
                                    for kv in range(2):
                                        for par in range(2):
                                            c0 = bsm * 16 + kv * 8 + par * 4
                                            r = e.matmul(ps[bss][:, c0:c0 + 4], lhsT=KTs[:, i, kv, par, :], rhs=qT[:, kv * 4:kv * 4 + 4, Tp + bsm],
                                                         start=True, stop=True)
                                return r
                            S.op("pe", fsqk, reads=[b_KTs, b_q[0][4], b_q[1][4]], writes=[b_ps[bss]])
                            S.op("act", (lambda g4: lambda e: e.activation(out=pTs[:, g4 * 64:(g4 + 1) * 64], in_=ps[bss][:, g4 * 64:(g4 + 1) * 64], func=AF.Exp, scale=0.125))(g4),
                                 reads=[b_ps[bss]], writes=[b_pTs])

                            def fspv(e, g4=g4):
                                for i in range(4):
                                    bsm = 4 * g4 + i
                                    for kv in range(2):
                                        for par in range(2):
                                            c0 = bsm * 16 + kv * 8 + par * 4
                                            r = e.matmul(ps[bos][:, c0:c0 + 4], lhsT=V4s[:, i, par, kv, :], rhs=pTs[:, c0:c0 + 4], start=True, stop=True)
                                return r
                            S.op("pe", fspv, reads=[b_V4s, b_pTs], writes=[b_ps[bos]])
                            yield
                        pov = ps[bos][:, 0:256].rearrange("p (b k r j) -> p b k r j", b=NS, k=2, r=2)
                        for par in range(2):
                            oh = slice(par * 64, par * 64 + 64)
                            dh = slice((1 - par) * 64, (1 - par) * 64 + 64)
                            ta, tab = tmpf()
                            tb, tbb = tmpf()
                            tav = ta[:, 0:128].rearrange("p (b k j) -> p b k j", b=NS, k=2)
                            tbv = tb[:, 0:128].rearrange("p (b k j) -> p b k j", b=NS, k=2)
                            hs = l * 16 + par
                            S.op("dve", (lambda par, dh, tav, hs: lambda e: e.tensor_tensor(
                                out=tav[dh], in0=pov[dh, :, :, par, :],
                                in1=esk[dh, hs:hs + 15:2].rearrange("p (k j) -> p k j", k=2).unsqueeze(1).broadcast_to([64, NS, 2, 4]), op=ALU.add))(par, dh, tav, hs),
                                reads=[b_ps[bos], b_const], writes=[tab])
                            S.op("dve", (lambda dh, ta: lambda e: e.reciprocal(out=ta[dh, 0:128], in_=ta[dh, 0:128]))(dh, ta), reads=[tab], writes=[tab])
                            S.op("dve", (lambda par, oh, dh, tav, tbv: lambda e: e.tensor_tensor(out=tbv[oh], in0=pov[oh, :, :, par, :], in1=tav[dh], op=ALU.mult))(par, oh, dh, tav, tbv),
                                 reads=[b_ps[bos], tab], writes=[tbb])
                            S.op("pool", (lambda oh, tb: lambda e: e.tensor_tensor(
                                out=obT[oh, :, Tp:T], in0=tb[oh, 0:128].rearrange("p (b c) -> p c b", b=NS),
                                in1=sgb[oh, :, Tp:T], op=ALU.mult))(oh, tb),
                                reads=[tbb, b_sgb], writes=[b_q[0][4], b_q[1][4]])
                def p4_steps():
                    for j in range(NCH):
                        b1 = proj(hT, [b_hT], T)
                        tm, tmb = tmpf()
                        S.op("act", (lambda b1, tm: lambda e: e.activation(out=tm[:, 0:T], in_=ps[b1][:, 0:T], func=AF.Tanh, scale=0.5))(b1, tm),
                             reads=[b_ps[b1]], writes=[tmb])
                        b2 = proj(cc, [b_cc], T)
                        S.op("dve", (lambda j, b2, tm: lambda e: e.scalar_tensor_tensor(
                            out=cF[:, j, 0:T], in0=tm[:, 0:T], scalar=1.0, in1=ps[b2][:, 0:T], op0=ALU.add, op1=ALU.mult))(j, b2, tm),
                            reads=[tmb, b_ps[b2]], writes=[b_cF[j]])
                        yield
                if not Ts:
                    nbank[0] = 8
                its = [attn_steps(), p4_steps()]
                for _ in range(2):
                    try:
                        next(its[0])
                    except StopIteration:
                        its.pop(0)
                        break
                while its:
                    for it in list(its):
                        try:
                            next(it)
                        except StopIteration:
                            its.remove(it)
                nbank[0] = 6
                if ck(tl, l, 'P5s'):
                    return
                for j in range(NCH):
                    b1 = proj(hT, [b_hT], T)
                    tm, tmb = tmpf()
                    S.op("act", (lambda b1, tm: lambda e: e.activation(out=tm[:, 0:T], in_=ps[b1][:, 0:T], func=AF.Tanh, scale=0.5))(b1, tm),
                         reads=[b_ps[b1]], writes=[tmb])
                    b2 = proj(obT, b_q[0] + b_q[1], T)
                    if DBG_ROPE == 5:
                        continue
                    t2, t2b = tmpf()
                    S.op("dve", (lambda b2, tm, t2: lambda e: e.scalar_tensor_tensor(
                        out=t2[:, 0:T], in0=tm[:, 0:T], scalar=1.0, in1=ps[b2][:, 0:T], op0=ALU.add, op1=ALU.mult))(b2, tm, t2),
                        reads=[tmb, b_ps[b2]], writes=[t2b])
                    if DBG_ROPE == 6:
                        continue
                    S.op("pool", (lambda j, t2: lambda e: e.tensor_tensor(out=yT[:, j, 0:T], in0=t2[:, 0:T], in1=cF[:, j, 0:T], op=ALU.add))(j, t2),
                         reads=[t2b, b_cF[j]], writes=[b_yT])
                if ck(tl, l, 'P6'):
                    return
                pend_rms = None
                for j in range(NCH):
                    bo = proj(yT, [b_yT], T)
                    if pend_rms is not None:
                        pend_rms()
                    S.op("dve", (lambda j, bo: lambda e: e.scalar_tensor_tensor(
                        out=xT[:, j, 0:T], in0=ps[bo][:, 0:T], scalar=0.5, in1=xT[:, j, 0:T], op0=ALU.mult, op1=ALU.add))(j, bo),
                        reads=[b_ps[bo], b_xT[j]], writes=[b_xT[j]])
                    if tl["halo"] and l == 0:
                        S.op("dve", (lambda j: lambda e: e.tensor_scalar(out=xT[:, j, 0:tl["halo"]], in0=xT[:, j, 0:tl["halo"]], scalar1=valid[:, 0:1], scalar2=None, op0=ALU.mult))(j),
                             reads=[b_xT[j], b_const], writes=[b_xT[j]])
                    pend_rms = rms_accum(T, j, xT, b_xT[j])
                pend_rms()
                rms_finish(T)
                wnext()
                if ck(tl, l, 'P7'):
                    return
                if tl["last"]:
                    bi = rot("big", 2)
                    for g in range(2):
                        b = bank()

                        def ftu(e, g=g, b=b):
                            for jj in range(4):
                                i = e.transpose(ps[b][0:CB, jj * 128:(jj + 1) * 128], ufin[:, 4 * g + jj, :], identf[:])
                            return i
                        S.op("pe", ftu, reads=[b_ufin, b_init], writes=[b_ps[b]])
                        S.op("act", (lambda b, g, bi: lambda e: e.activation(out=big[bi][0:CB, g * 512:(g + 1) * 512], in_=ps[b][0:CB, :], func=AF.Copy, scale=0.5))(b, g, bi),
                             reads=[b_ps[b]], writes=[b_big[bi]])
                    S.dma("sp", ncp_o[l], big[bi][0:CB, :], b_big[bi], reads=[b_big[bi]])
                    b = bank()

                    def ftkf(e, b=b):
                        e.transpose(ps[b][:, 0:128], kfin[:, 0, :], identf[:])
                        return e.transpose(ps[b][:, 128:256], kfin[:, 1, :], identf[:])
                    S.op("pe", ftkf, reads=[b_kfin, b_init], writes=[b_ps[b]])
                    S.op("act", (lambda b: lambda e: e.activation(out=nkb[:], in_=ps[b][:, 0:256].rearrange("p (k d) -> p k d", k=2)[:, :, 0:64], func=AF.Copy))(b),
                         reads=[b_ps[b]], writes=[b_nkb])
                    S.dma("sp", nkp_o[l], nkb[:].rearrange("p k d -> p (k d)"), b_nkb, reads=[b_nkb])
                if Ts:
                    bi = rot("big", 2)
                    for g in range(2):
                        b = bank()

                        def ftus(e, g=g, b=b):
                            for jj in range(4):
                                i = e.transpose(ps[b][0:NS, jj * 128:(jj + 1) * 128], usf[:, 4 * g + jj, :], identf[:])
                            return i
                        S.op("pe", ftus, reads=[b_usf, b_init], writes=[b_ps[b]])
                        S.op("act", (lambda b, g, bi: lambda e: e.activation(out=big[bi][0:NS, g * 512:(g + 1) * 512], in_=ps[b][0:NS, :], func=AF.Copy, scale=0.5))(b, g, bi),
                             reads=[b_ps[b]], writes=[b_big[bi]])
                    S.dma("sp", ncs_o[l, :, CB - 1, :], big[bi][0:NS, :], b_big[bi], reads=[b_big[bi]])

            for l_ in range(2):
                do_layer(l_)
            if ck(tl, 2, 'OUT'):
                return
            for j in range(NCH):
                S.op("dve", (lambda j: lambda e: e.scalar_tensor_tensor(
                    out=cF[:, j, 0:T], in0=xT[:, j, 0:T], scalar=pp[:, PP_GF + j:PP_GF + j + 1], in1=st[1][:, 0:T],
                    op0=ALU.mult, op1=ALU.mult))(j), reads=[b_xT[j], b_st[1], b_const], writes=[b_cF[j]])
            if ck(tl, 2, 'P8a'):
                return
            if True:
                for blk in range(tl["yblk0"], nb):
                    bi = rot("big", 2)
                    for g in range(2):
                        b = bank()

                        def fty(e, g=g, b=b, blk=blk):
                            for jj in range(4):
                                i = e.transpose(ps[b][:, jj * 128:(jj + 1) * 128], cF[:, 4 * g + jj, blk * 128:(blk + 1) * 128], identf[:])
                            return i
                        S.op("pe", fty, reads=b_cF + [b_init], writes=[b_ps[b]])
                        S.op("act", (lambda b, g, bi: lambda e: e.activation(out=big[bi][:, g * 512:(g + 1) * 512], in_=ps[b][:], func=AF.Copy))(b, g, bi),
                             reads=[b_ps[b]], writes=[b_big[bi]])
                    S.dma("sp", y_o[tl["yrow0"] + (blk - tl["yblk0"]) * 128: tl["yrow0"] + (blk - tl["yblk0"] + 1) * 128, :], big[bi][:], b_big[bi], reads=[b_big[bi]])
            if Ts:
                bi = rot("big", 2)
                for g in range(2):
                    b = bank()

                    def ftys(e, g=g, b=b):
                        for jj in range(4):
                            i = e.transpose(ps[b][0:NS, jj * 128:(jj + 1) * 128], cF[:, 4 * g + jj, Tp:T], identf[:])
                        return i
                    S.op("pe", ftys, reads=b_cF + [b_init], writes=[b_ps[b]])
                    S.op("act", (lambda b, g, bi: lambda e: e.activation(out=big[bi][0:NS, g * 512:(g + 1) * 512], in_=ps[b][0:NS, :], func=AF.Copy))(b, g, bi),
                         reads=[b_ps[b]], writes=[b_big[bi]])
                if DBG_ROPE != 8:
                    S.dma("sp", ys_o, big[bi][0:NS, :], b_big[bi], reads=[b_big[bi]])

        for ti_, tl_ in enumerate(TILES):
            do_tile(ti_, tl_)
        if DBG_ROPE == 7:
            for _ in range(16):
                wnext()
        assert DBG_STOP is not None or wctr[0] == len(TILES) * 2 * NCHUNK, wctr[0]
        S.emit(final_wait_bufs=b_big + [b_vfin, b_nkb, b_knew, b_vnew, b_dd])
        S.close()
    return nc


_CACHE = {}


def _rope_tables(half):
    inv = (np.float32(500000.0) ** (-(np.arange(0, 16, 2, dtype=np.float32)) / np.float32(16))).astype(np.float32)
    pos = np.zeros(NCOLS, np.float32)
    hp = np.arange(HALO, dtype=np.float32) + np.float32(half * OWN - HALO)
    pos[0:HALO] = np.maximum(hp, 0)
    pos[HALO:HALO + OWN] = np.arange(OWN, dtype=np.float32) + np.float32(half * OWN)
    pos[HALO + OWN:] = PAST
    ang = pos[None, :] * inv[:, None]
    cos = np.cos(ang).astype(np.float32)
    sin = np.sin(ang).astype(np.float32)
    C = np.ones((128, NCOLS), np.float32)
    Sg = np.zeros((128, NCOLS), np.float32)
    for base in (0, 64):
        C[base:base + 8] = cos
        C[base + 8:base + 16] = cos
        Sg[base:base + 8] = -sin
        Sg[base + 8:base + 16] = sin
    return C, Sg


def kernel(x_prompt, x_sample, state_conv, cache_k_win, cache_v_win, norm_g, w_in, conv_w, conv_b, conv_ln_g,
           conv_ln_b, w_conv_out, attn_sinks, w_attn_out, w_out, final_norm_g):
    f = lambda a: np.asarray(a, dtype=np.float32)
    x_prompt, x_sample, state_conv = f(x_prompt), f(x_sample), f(state_conv)
    ck = f(cache_k_win).reshape(2, 128, 128, 128)
    cv = f(cache_v_win).reshape(2, 128, 128, 128)
    wstream = _build_wstream(f(w_in), f(w_conv_out), f(w_attn_out), f(w_out))
    pp = np.zeros((128, PP_N), np.float32)
    fm = lambda v: f(v).reshape(2, 8, 128).transpose(2, 0, 1).reshape(128, 16)
    pp[:, PP_G:PP_G + 16] = fm(norm_g)
    pp[:, PP_CB:PP_CB + 16] = fm(conv_b)
    pp[:, PP_LG:PP_LG + 16] = fm(conv_ln_g)
    pp[:, PP_LB:PP_LB + 16] = fm(conv_ln_b)
    pp[:, PP_GF:PP_GF + 8] = f(final_norm_g).reshape(8, 128).T
    pp[:, PP_CW:] = f(conv_w).reshape(2, CK, 8, 128).transpose(3, 0, 2, 1).reshape(128, 2 * 8 * CK)
    snk = f(attn_sinks).reshape(32)
    jj = np.arange(128)[:, None]
    ii = np.arange(128)[None, :]
    mprev = np.where(jj > ii, 0.0, NEG).astype(np.float32)
    mcur = np.where(jj <= ii, 0.0, NEG).astype(np.float32)
    perm = np.zeros((128, 128), np.float32)
    for m in range(128):
        d = m % 64
        if d < 8:
            perm[m + 8, m] = 1.0
        elif d < 16:
            perm[m - 8, m] = 1.0
    if "nc" not in _CACHE:
        _CACHE["nc"] = build_program()
    nc = _CACHE["nc"]
    in_maps = []
    for c in range(NCORES):
        s, half = divmod(c, 2)
        xp = np.zeros((HALO + OWN, D), np.float32)
        if half == 1:
            xp[0:HALO] = x_prompt[s, OWN - HALO:OWN]
        xp[HALO:] = x_prompt[s, half * OWN:(half + 1) * OWN]
        C, Sg = _rope_tables(half)
        msk = np.zeros((128, 3, 512), np.float32)
        msk[:, 0] = np.tile(mprev, (1, 4))
        msk[:, 1] = np.tile(mcur, (1, 4))
        msk[:, 2] = np.tile(mprev, (1, 4)) if half == 1 else NEG
        in_maps.append({
            "xp": xp,
            "xs": np.ascontiguousarray(x_sample[NS * c:NS * (c + 1), 0, :]),
            "stc": np.ascontiguousarray(state_conv[:, NS * c:NS * (c + 1)]),
            "ck": np.ascontiguousarray(ck[:, NS * c:NS * (c + 1)]),
            "cv": np.ascontiguousarray(cv[:, NS * c:NS * (c + 1)]),
            "wst": wstream,
            "pp": pp,
            "snk": snk,
            "ropec": C,
            "ropes": Sg,
            "msk": msk,
            "perm": perm,
            "valid": np.full((128, 1), float(half), np.float32),
        })
    res = run_bass_kernel_spmd(nc, in_maps, core_ids=list(range(NCORES)))
    R = res.results
    y_prompt = np.zeros((4, SEQ, D), np.float32)
    y_sample = np.zeros((128, 1, D), np.float32)
    ncp = np.zeros((2, 4, CB, D), np.float32)
    nkp = np.zeros((2, 4, 128, 2, 64), np.float32)
    nvp = np.zeros((2, 4, 128, 2, 64), np.float32)
    ncs = np.zeros((2, 128, CB, D), np.float32)
    nks = np.zeros((2, 128, 128, 2, 64), np.float32)
    nvs = np.zeros((2, 128, 128, 2, 64), np.float32)
    for c in range(NCORES):
        s, half = divmod(c, 2)
        r = R[c]
        y_prompt[s, half * OWN:(half + 1) * OWN] = r["y"]
        y_sample[NS * c:NS * (c + 1), 0] = r["ys"]
        if half == 1:
            ncp[:, s] = r["ncp"]
            nkp[:, s] = r["nkp"].reshape(2, 128, 2, 64)
            nvp[:, s] = r["nvp"].reshape(2, 128, 2, 64)
        ncs[:, NS * c:NS * (c + 1)] = r["ncs"]
        nks[:, NS * c:NS * (c + 1)] = r["nks"].reshape(2, NS, 128, 2, 64)
        nvs[:, NS * c:NS * (c + 1)] = r["nvs"].reshape(2, NS, 128, 2, 64)
    return (y_prompt, y_sample, ncp, nkp, nvp, ncs, nks, nvs)
```

```python
import contextlib
import numpy as np
import concourse.bass as bass
import concourse.mybir as mybir
from concourse.bass_utils import run_bass_kernel_spmd

F32 = mybir.dt.float32
BF16 = mybir.dt.bfloat16
AF = mybir.ActivationFunctionType
ALU = mybir.AluOpType

NCORES = 8
D = 1024
NCH = 8
SEQ = 4096
OWN = 2048
HALO = 256
NS = 16
PAST = 16384
CK = 31
CB = 30
EPS = 1e-6
NCHUNK = 84
NSLAB = NCHUNK // 2
NWB = 5
TMAX = 512
NEG = -30000.0
PP_G, PP_CB, PP_LG, PP_LB, PP_GF, PP_CW = 0, 16, 32, 48, 64, 72
PP_N = 72 + 2 * 8 * CK

ENGS = ("pe", "act", "dve", "pool", "sp")


class Buf:
    __slots__ = ("name", "last_w", "readers", "sem", "dma_cnt", "excl")

    def __init__(self, name, sem=None):
        self.excl = False
        self.name = name
        self.last_w = None
        self.readers = []
        self.sem = sem
        self.dma_cnt = 0


class Op:
    __slots__ = ("eng", "fn", "deps", "is_dma", "sig", "idx")

    def __init__(self, eng, fn, is_dma):
        self.eng = eng
        self.fn = fn
        self.deps = []
        self.is_dma = is_dma
        self.sig = None
        self.idx = None


class Sched:
    def __init__(self, nc):
        self.nc = nc
        self.ops = []
        self._sem_ctx = []
        self.dma_bufs = []

    def new_sem(self, name):
        ctx = self.nc.semaphore(name)
        s = ctx.__enter__()
        self._sem_ctx.append(ctx)
        return s

    def close(self):
        for c in reversed(self._sem_ctx):
            c.__exit__(None, None, None)

    def buf(self, name, dma=False):
        b = Buf(name, self.new_sem("d_" + name) if dma else None)
        if dma:
            self.dma_bufs.append(b)
        return b

    def _add(self, op, reads, writes):
        deps = set()
        xr = [b for b in reads if b.excl]
        if xr:
            reads = [b for b in reads if not b.excl]
            writes = list(writes) + [b for b in xr if b not in writes]
        for b in reads:
            if b.last_w is not None:
                deps.add(b.last_w)
        for b in writes:
            if b.last_w is not None:
                deps.add(b.last_w)
            for r in b.readers:
                deps.add(r)
        deps.discard(op)
        op.deps = list(deps)
        for b in reads:
            b.readers.append(op)
        for b in writes:
            b.last_w = op
            b.readers = []
        op.idx = len(self.ops)
        self.ops.append(op)
        return op

    def op(self, eng, fn, reads=(), writes=()):
        return self._add(Op(eng, fn, False), reads, writes)

    def dma(self, eng, out_ap, in_ap, sembuf, reads=(), writes=()):
        def fn(e):
            return e.dma_start(out=out_ap, in_=in_ap)
        o = Op(eng, fn, True)
        sembuf.dma_cnt += 1
        o.sig = (sembuf.sem, 16 * sembuf.dma_cnt)
        return self._add(o, reads, list(writes) + [sembuf])

    def emit(self, final_wait_bufs=()):
        nc = self.nc
        need = set()
        for o in self.ops:
            for d in o.deps:
                if d.is_dma:
                    continue
                if d.eng == "pe" and o.eng == "pe" and not o.is_dma:
                    continue
                need.add(d)
        esem = {e: self.new_sem("e_" + e) for e in ENGS}
        cnt = {e: 0 for e in ENGS}
        for o in self.ops:
            if o in need:
                cnt[o.eng] += 1
                o.sig = (esem[o.eng], cnt[o.eng])
        per = {e: [o for o in self.ops if o.eng == e] for e in ENGS}

        def run(eng_name, eng):
            waited = {}
            for o in per[eng_name]:
                req = {}
                for d in o.deps:
                    if d.sig is None:
                        continue
                    if (not d.is_dma) and d.eng == "pe" and eng_name == "pe" and not o.is_dma:
                        continue
                    s, v = d.sig
                    k = id(s)
                    if k not in req or req[k][1] < v:
                        req[k] = (s, v)
                for k, (s, v) in req.items():
                    if waited.get(k, 0) >= v:
                        continue
                    eng.wait_ge(s, v)
                    waited[k] = v
                inst = o.fn(eng)
                if o.sig is not None:
                    inst.then_inc(o.sig[0], 16 if o.is_dma else 1)
            if eng_name == "sp":
                for b in self.dma_bufs:
                    if b.dma_cnt:
                        eng.wait_ge(b.sem, 16 * b.dma_cnt)

        with nc.Block() as block:
            @block.tensor
            def _(e):
                run("pe", e)

            @block.scalar
            def _(e):
                run("act", e)

            @block.vector
            def _(e):
                run("dve", e)

            @block.gpsimd
            def _(e):
                run("pool", e)

            @block.sync
            def _(e):
                run("sp", e)


def _stream_cols():
    C = []
    o_glu, o_ga, o_q, o_k, o_v, o_gb, o_mga, o_mgb = 0, 2048, 3072, 4096, 4224, 4352, 5376, 6400
    for j in range(8):
        C.append(("in", o_glu + j * 128))
        C.append(("in", o_glu + 1024 + j * 128))
    for j in range(8):
        C.append(("in", o_q + j * 128))
    C.append(("k0", o_k))
    C.append(("k1", o_k + 64))
    C.append(("in", o_v))
    for j in range(8):
        C.append(("in", o_gb + j * 128))
    for j in range(8):
        C.append(("in", o_ga + j * 128))
    for j in range(8):
        C.append(("in", o_mga + j * 128))
        C.append(("co", j * 128))
    for j in range(8):
        C.append(("in", o_mgb + j * 128))
        C.append(("ao", j * 128))
    for j in range(8):
        C.append(("out", j * 128))
    assert len(C) == 83
    C.append(("pad", 0))
    return C


def _build_wstream(w_in, wco, wao, wout):
    C = _stream_cols()
    out = np.zeros((2, NCHUNK, D, 128), np.float32)
    for l in range(2):
        for i, (src, c0) in enumerate(C):
            if src == "in":
                out[l, i] = w_in[l][:, c0:c0 + 128]
            elif src in ("k0", "k1"):
                out[l, i, :, 0:64] = w_in[l][:, c0:c0 + 64]
                out[l, i, :, 64:128] = w_in[l][:, c0:c0 + 64]
            elif src == "co":
                out[l, i] = wco[l][:, c0:c0 + 128]
            elif src == "ao":
                out[l, i] = wao[l][:, c0:c0 + 128]
            elif src == "out":
                out[l, i] = wout[l][:, c0:c0 + 128]
    out = out.reshape(2, NSLAB, 2, NCH, 128, 128)
    out = out.transpose(0, 1, 4, 3, 2, 5)
    return np.ascontiguousarray(out.reshape(2, NSLAB, 128, NCH, 256))


TILES = [
    dict(name="A", Tp=512, Ts=0, xrow0=0, rope0=0, first_qb=2, halo=256, yblk0=2, last=False, yrow0=0),
    dict(name="B", Tp=512, Ts=0, xrow0=512, rope0=512, first_qb=-1, halo=0, yblk0=0, last=False, yrow0=256),
    dict(name="C", Tp=512, Ts=0, xrow0=1024, rope0=1024, first_qb=-1, halo=0, yblk0=0, last=False, yrow0=768),
    dict(name="D", Tp=384, Ts=0, xrow0=1536, rope0=1536, first_qb=-1, halo=0, yblk0=0, last=False, yrow0=1280),
    dict(name="E", Tp=384, Ts=NS, xrow0=1920, rope0=1920, first_qb=-1, halo=0, yblk0=0, last=True, yrow0=1664),
]
NCOLS = 272 + 2048
DBG_ROPE = 9
DBG_STOP = None


def build_program():
    nc = bass.Bass("TRN2", target_bir_lowering=False)
    dt = lambda n, s, k, d=F32: nc.dram_tensor(n, s, d, kind=k).ap()
    xp = dt("xp", [HALO + OWN, D], "ExternalInput")
    xs = dt("xs", [NS, D], "ExternalInput")
    stc = dt("stc", [2, NS, CB, D], "ExternalInput")
    ckd = dt("ck", [2, NS, 128, 128], "ExternalInput")
    cvd = dt("cv", [2, NS, 128, 128], "ExternalInput")
    wst = dt("wst", [2, NSLAB, 128, NCH, 256], "ExternalInput")
    ppd = dt("pp", [128, PP_N], "ExternalInput")
    snk = dt("snk", [32], "ExternalInput")
    rcd = dt("ropec", [128, NCOLS], "ExternalInput")
    rsd = dt("ropes", [128, NCOLS], "ExternalInput")
    mskd = dt("msk", [128, 3, 512], "ExternalInput")
    permd = dt("perm", [128, 128], "ExternalInput")
    vald = dt("valid", [128, 1], "ExternalInput")
    y_o = dt("y", [OWN, D], "ExternalOutput")
    ys_o = dt("ys", [NS, D], "ExternalOutput")
    ncp_o = dt("ncp", [2, CB, D], "ExternalOutput")
    nkp_o = dt("nkp", [2, 128, 128], "ExternalOutput")
    nvp_o = dt("nvp", [2, 128, 128], "ExternalOutput")
    ncs_o = dt("ncs", [2, NS, CB, D], "ExternalOutput")
    nks_o = dt("nks", [2, NS, 128, 128], "ExternalOutput")
    nvs_o = dt("nvs", [2, NS, 128, 128], "ExternalOutput")

    S = Sched(nc)
    es_ = contextlib.ExitStack()
    with es_:
        def sb(name, shape, d=F32):
            return es_.enter_context(nc.sbuf_tensor(name, shape, d))

        xT = sb("xT", [128, NCH, TMAX]); b_xT = [S.buf("xT%d" % j) for j in range(NCH)]
        big = [sb("big%d" % i, [128, D]) for i in range(2)]
        b_big = [S.buf("big%d" % i, dma=True) for i in range(2)]
        hT = sb("hT", [128, NCH, TMAX], BF16); b_hT = S.buf("hT")
        rot16 = [sb("r16_%d" % i, [128, TMAX], BF16) for i in range(4)]
        b_rot16 = [S.buf("r16_%d" % i) for i in range(4)]
        uT = sb("uT", [128, NCH, CB + TMAX], BF16); b_uT = [S.buf("uT%d" % j) for j in range(NCH)]
        uh = sb("uh", [128, 2, NCH, CB], BF16); b_uh = [S.buf("uh%d" % l) for l in range(2)]
        qT = sb("qT", [128, NCH, TMAX], BF16)
        b_q = [[S.buf("q%d_%d" % (kv, qb)) for qb in range(5)] for kv in range(2)]
        qraw = [sb("qraw%d" % i, [128, TMAX], BF16) for i in range(2)]
        b_qraw = [S.buf("qraw%d" % i) for i in range(2)]
        kT = [sb("kT%d" % l, [128, 2, 2, 128 + TMAX], BF16) for l in range(2)]
        b_kT = [S.buf("kT%d" % l) for l in range(2)]
        V4 = [sb("V4_%d" % l, [128, 5, 2, 2, 128], BF16) for l in range(2)]
        b_V4 = [S.buf("V4_%d" % l) for l in range(2)]
        sgb = sb("sgb", [128, NCH, TMAX], BF16); b_sgb = S.buf("sgb")
        Dg = [sb("Dg%d" % i, [128, 16, 128], BF16) for i in range(2)]
        b_Dg = [S.buf("Dg%d" % i) for i in range(2)]
        cF = sb("cF", [128, NCH, TMAX]); b_cF = [S.buf("cF%d" % j) for j in range(NCH)]
        sga = [sb("sga%d" % i, [128, TMAX], BF16) for i in range(2)]
        b_sga = [S.buf("sga%d" % i) for i in range(2)]
        cc = sb("cc", [128, NCH, TMAX], BF16); b_cc = S.buf("cc")
        yT = cc; b_yT = b_cc
        pT = [sb("pT%d" % i, [128, 2, TMAX], BF16) for i in range(4)]
        b_pT = [S.buf("pT%d" % i) for i in range(4)]
        tmp = [sb("tmp%d" % i, [128, TMAX]) for i in range(6)]
        b_tmp = [S.buf("tmp%d" % i) for i in range(6)]
        st = [sb("st%d" % i, [128, TMAX]) for i in range(4)]
        b_st = [S.buf("st%d" % i) for i in range(4)]
        ropeC = sb("ropeC", [128, TMAX]); ropeS = sb("ropeS", [128, TMAX]); b_rope = S.buf("rope", dma=True)
        msk = sb("mskb", [128, 3, 512], BF16)
        identb = sb("identb", [128, 128], BF16)
        identf = sb("identf", [128, 128])
        permb = sb("permb", [128, 128], BF16)
        onesS = sb("onesS", [128, 128], BF16)
        wbuf = [sb("wb%d" % i, [128, NCH, 256], BF16) for i in range(NWB)]
        b_wbuf = [S.buf("wb%d" % i, dma=True) for i in range(NWB)]
        pp = sb("pp_sb", [128, PP_N])
        cwh = sb("cwh", [128, 2, NCH, CK])
        esk = sb("esk", [128, 32])
        valid = sb("valid_sb", [128, 1])
        cst = sb("cst", [128, 2])
        b_const = S.buf("const", dma=True)
        b_const2 = S.buf("const2", dma=True)
        b_init = S.buf("init")
        ufin = sb("ufin", [128, NCH, CB]); b_ufin = S.buf("ufin")
        usf = sb("usf", [128, NCH, NS]); b_usf = S.buf("usf")
        kfin = sb("kfin", [128, 2, 128]); b_kfin = S.buf("kfin")
        ksf = sb("ksf", [128, 2, NS]); b_ksf = S.buf("ksf")
        vfin = sb("vfin", [128, 128]); b_vfin = S.buf("vfin", dma=True)
        nkb = sb("nkb", [128, 2, 64]); b_nkb = S.buf("nkb", dma=True)
        knew = sb("knew", [NS, 2, 64]); b_knew = S.buf("knew", dma=True)
        vnew = sb("vnew", [NS, 128]); b_vnew = S.buf("vnew", dma=True)
        stT = sb("stT", [128, NCH, NS, CB], BF16); b_stT = S.buf("stT")
        Ks2 = [sb("Ks%d" % i, [128, 4, 128]) for i in range(2)]; b_Ks2 = [S.buf("Ks%d" % i, dma=True) for i in range(2)]
        Vs2 = [sb("Vs%d" % i, [128, 4, 128]) for i in range(2)]; b_Vs2 = [S.buf("Vs%d" % i, dma=True) for i in range(2)]
        b_Ks2b = [S.buf("Ksb%d" % i, dma=True) for i in range(2)]; b_Vs2b = [S.buf("Vsb%d" % i, dma=True) for i in range(2)]
        KTs = sb("KTs", [128, 4, 2, 2, 128], BF16); b_KTs = S.buf("KTs")
        V4s = sb("V4s", [128, 4, 2, 2, 128], BF16); b_V4s = S.buf("V4s")
        pTs = sb("pTs", [128, 256], BF16); b_pTs = S.buf("pTs")
        b_dd = S.buf("dd", dma=True)
        ps = [es_.enter_context(nc.psum_tensor("ps%d" % i, [128, 512], F32)) for i in range(8)]
        b_ps = [S.buf("ps%d" % i) for i in range(8)]
        for b_ in b_ps:
            b_.excl = True
        bank_ctr = [0]

        nbank = [6]

        def bank():
            b = bank_ctr[0] % nbank[0]
            bank_ctr[0] += 1
            return b

        rot_ctr = {}

        def rot(key, n):
            v = rot_ctr.get(key, 0)
            rot_ctr[key] = v + 1
            return v % n

        def tmpf():
            i = rot("tmp", 6)
            return tmp[i], b_tmp[i]

        def r16():
            i = rot("r16", 4)
            return rot16[i], b_rot16[i]

        S.dma("sp", pp[:], ppd, b_const)
        S.dma("sp", esk[:], snk.partition_broadcast(128), b_const)
        S.dma("sp", valid[:], vald, b_const)
        S.dma("pool", msk[:], mskd, b_const2)
        S.dma("pool", permb[:], permd, b_const2)

        ini = lambda fn, w=(), r=(): S.op("pool", fn, reads=[b_init] + list(r), writes=[b_init] + list(w))
        ini(lambda e: e.memset(identf[:], 1.0))
        ini(lambda e: e.affine_select(out=identf[:], in_=identf[:], pattern=[[-1, 128]], compare_op=ALU.is_equal,
                                      fill=0.0, base=0, channel_multiplier=1))
        ini(lambda e: e.tensor_copy(out=identb[:], in_=identf[:]))
        ini(lambda e: e.memset(onesS[:], 1.0 / 1024.0))
        ini(lambda e: e.memset(cst[:, 0:1], EPS))
        ini(lambda e: e.memset(cst[:, 1:2], -0.5))
        ini(lambda e: e.memset(uh[:], 0.0), w=b_uh)
        for l0 in range(2):
            ini((lambda l0: lambda e: e.memset(kT[l0][:], 0.0))(l0), w=[b_kT[l0]])
            ini((lambda l0: lambda e: e.memset(V4[l0][:], 0.0))(l0), w=[b_V4[l0]])
            ini((lambda l0: lambda e: e.memset(V4[l0][:, :, 0, :, 64:128], 1.0))(l0), w=[b_V4[l0]])
            ini((lambda l0: lambda e: e.memset(V4[l0][:, :, 1, :, 0:64], 1.0))(l0), w=[b_V4[l0]])
        ini(lambda e: e.memset(KTs[:], 0.0), w=[b_KTs])
        ini(lambda e: e.memset(V4s[:, :, 0, :, 64:128], 1.0), w=[b_V4s])
        ini(lambda e: e.memset(V4s[:, :, 1, :, 0:64], 1.0), w=[b_V4s])
        ini(lambda e: e.memset(uT[:], 0.0), w=b_uT)
        S.op("pool", lambda e: e.tensor_scalar(out=cwh[:].rearrange("p a b c -> p (a b c)"), in0=pp[:, PP_CW:PP_N],
                                                scalar1=0.5, scalar2=None, op0=ALU.mult),
             reads=[b_const], writes=[b_init])
        S.op("act", lambda e: e.activation(out=esk[:], in_=esk[:], func=AF.Exp), reads=[b_const], writes=[b_const])

        wstate = dict(issued=0, total=5 * 2 * NSLAB)
        order = [(ti, l) for ti in range(len(TILES)) for l in range(2)]

        def w_issue(upto):
            while wstate["issued"] <= upto and wstate["issued"] < wstate["total"]:
                g = wstate["issued"]
                tl, s = divmod(g, NSLAB)
                l = tl % 2
                S.dma("pool", wbuf[g % NWB][:], wst[l, (s % 30) if DBG_ROPE == 5 else s], b_wbuf[g % NWB])
                wstate["issued"] += 1

        wctr = [0]

        def wnext():
            ci = wctr[0]
            wctr[0] += 1
            g = ci // 2
            w_issue(g + NWB - 1)
            t = wbuf[g % NWB]
            return t[:, :, (ci % 2) * 128:(ci % 2) * 128 + 128], b_wbuf[g % NWB]

        def proj(act, act_bufs, N, b=None):
            wap, wb = wnext()
            if b is None:
                b = bank()

            def f(e):
                for kc in range(NCH):
                    i = e.matmul(ps[b][:, 0:N], lhsT=wap[:, kc, :], rhs=act[:, kc, 0:N], start=(kc == 0), stop=(kc == NCH - 1))
                return i
            S.op("pe", f, reads=list(act_bufs) + [wb], writes=[b_ps[b]])
            return b

        def rms_accum(T, j, src, src_buf):
            r, rb = r16()
            S.op("act", (lambda j, r: lambda e: e.activation(out=r[:, 0:T], in_=src[:, j, 0:T], func=AF.Square))(j, r),
                 reads=[src_buf], writes=[rb])
            def mm():
                S.op("pe", (lambda j, r: lambda e: e.matmul(ps[6][:, 0:T], lhsT=onesS[:], rhs=r[:, 0:T], start=(j == 0), stop=(j == NCH - 1)))(j, r),
                     reads=[rb, b_init], writes=[b_ps[6]])
            return mm

        def rms_finish(T):
            S.op("act", lambda e: e.activation(out=st[0][:, 0:T], in_=ps[6][:, 0:T], func=AF.Sqrt, bias=cst[:, 0:1]),
                 reads=[b_ps[6], b_init], writes=[b_st[0]])
            S.op("dve", lambda e: e.reciprocal(out=st[1][:, 0:T], in_=st[0][:, 0:T]), reads=[b_st[0]], writes=[b_st[1]])

        def rms_stats(T, src, src_bufs):
            prev = None
            for j in range(NCH):
                m = rms_accum(T, j, src, src_bufs[j])
                if prev is not None:
                    prev()
                prev = m
            prev()
            rms_finish(T)

        halt = [False]

        def ck(tl, l, ph):
            if DBG_STOP is not None and DBG_STOP == (tl["name"], l, ph):
                halt[0] = True
            return halt[0]

        def do_tile(ti, tl):
            if halt[0]:
                return
            Tp, Ts = tl["Tp"], tl["Ts"]
            T = Tp + Ts
            nb = Tp // 128
            isH = Ts > 0
            S.dma("sp", ropeC[:, 0:T], rcd[:, tl["rope0"]:tl["rope0"] + T], b_rope)
            S.dma("sp", ropeS[:, 0:T], rsd[:, tl["rope0"]:tl["rope0"] + T], b_rope)
            for blk in range(nb):
                bi = rot("big", 2)
                S.dma("sp", big[bi][:], xp[tl["xrow0"] + blk * 128: tl["xrow0"] + (blk + 1) * 128, :], b_big[bi])
                for g in range(2):
                    b = bank()

                    def ftr(e, bi=bi, g=g, b=b):
                        for jj in range(4):
                            i = e.transpose(ps[b][:, jj * 128:(jj + 1) * 128], big[bi][:, (4 * g + jj) * 128:(4 * g + jj + 1) * 128], identf[:])
                        return i
                    S.op("pe", ftr, reads=[b_big[bi], b_init], writes=[b_ps[b]])
                    S.op("act", (lambda b, g, blk: lambda e: e.activation(
                        out=xT[:, 4 * g:4 * g + 4, blk * 128:(blk + 1) * 128],
                        in_=ps[b][:].rearrange("p (a c) -> p a c", a=4), func=AF.Copy))(b, g, blk),
                        reads=[b_ps[b]], writes=b_xT[4 * g:4 * g + 4])
            if Ts:
                bi = rot("big", 2)
                S.dma("sp", big[bi][0:NS, :], xs, b_big[bi])
                b = bank()

                def ftrs(e, bi=bi, b=b):
                    for j in range(NCH):
                        i = e.transpose(ps[b][:, j * NS:(j + 1) * NS], big[bi][0:NS, j * 128:(j + 1) * 128], identf[0:NS, 0:NS])
                    return i
                S.op("pe", ftrs, reads=[b_big[bi], b_init], writes=[b_ps[b]])
                S.op("act", (lambda b: lambda e: e.activation(out=xT[:, :, Tp:T], in_=ps[b][:, 0:NCH * NS].rearrange("p (a c) -> p a c", a=NCH),
                                                             func=AF.Copy))(b), reads=[b_ps[b]], writes=b_xT)

            if ck(tl, -1, 'P0'):
                return

            def do_layer(l):
                if halt[0]:
                    return
                gcol = lambda base, j: pp[:, base + l * 8 + j: base + l * 8 + j + 1]
                if l == 0:
                    rms_stats(T, xT, b_xT)
                for j in range(NCH):
                    S.op("dve", (lambda j: lambda e: e.scalar_tensor_tensor(
                        out=hT[:, j, 0:T], in0=xT[:, j, 0:T], scalar=gcol(PP_G, j), in1=st[1][:, 0:T],
                        op0=ALU.mult, op1=ALU.mult))(j), reads=[b_xT[j], b_st[1], b_const], writes=[b_hT])
                if ck(tl, l, 'P1'):
                    return
                if isH:
                    for r4 in range(4):
                        bi = rot("big", 2)
                        S.dma("sp", big[bi][0:120, :], stc[l, 4 * r4:4 * r4 + 4].rearrange("b j d -> (b j) d"), b_big[bi])
                        for g in range(2):
                            b = bank()

                            def ftst(e, bi=bi, g=g, b=b):
                                for jj in range(4):
                                    i = e.transpose(ps[b][:, jj * 120:(jj + 1) * 120], big[bi][0:120, (4 * g + jj) * 128:(4 * g + jj + 1) * 128],
                                                    identf[0:120, 0:120])
                                return i
                            S.op("pe", ftst, reads=[b_big[bi], b_init], writes=[b_ps[b]])
                            S.op("act", (lambda b, g, r4: lambda e: e.activation(
                                out=stT[:, 4 * g:4 * g + 4, 4 * r4:4 * r4 + 4, :].rearrange("p a b j -> p a (b j)"),
                                in_=ps[b][:, 0:480].rearrange("p (a c) -> p a c", a=4), func=AF.Copy, scale=2.0))(b, g, r4),
                                reads=[b_ps[b]], writes=[b_stT])
                    S.dma("sp", ncs_o[l, :, 0:CB - 1, :], stc[l, :, 1:CB, :], b_dd)
                    S.dma("sp", nks_o[l, :, 0:127, :], ckd[l, :, 1:128, :], b_dd)
                    S.dma("sp", nvs_o[l, :, 0:127, :], cvd[l, :, 1:128, :], b_dd)
                if ck(tl, l, 'S0'):
                    return
                S.op("pool", lambda e: e.tensor_copy(out=uT[:, :, 0:CB], in_=uh[:, l, :, :]), reads=[b_uh[l]], writes=b_uT)
                for j in range(NCH):
                    ba = proj(hT, [b_hT], T)
                    bb = proj(hT, [b_hT], T)
                    tg, tgb = tmpf()
                    S.op("act", (lambda bb, tg: lambda e: e.activation(out=tg[:, 0:T], in_=ps[bb][:, 0:T], func=AF.Tanh, scale=0.5))(bb, tg),
                         reads=[b_ps[bb]], writes=[tgb])
                    S.op("dve", (lambda j, ba, tg: lambda e: e.scalar_tensor_tensor(
                        out=uT[:, j, CB:CB + T], in0=tg[:, 0:T], scalar=1.0, in1=ps[ba][:, 0:T], op0=ALU.add, op1=ALU.mult))(j, ba, tg),
                        reads=[tgb, b_ps[ba]], writes=[b_uT[j]])
                    if tl["last"]:
                        S.op("dve", (lambda j, ba, tg: lambda e: e.scalar_tensor_tensor(
                            out=ufin[:, j, :], in0=tg[:, Tp - CB:Tp], scalar=1.0, in1=ps[ba][:, Tp - CB:Tp], op0=ALU.add, op1=ALU.mult))(j, ba, tg),
                            reads=[tgb, b_ps[ba]], writes=[b_ufin])
                    if Ts:
                        S.op("dve", (lambda j, ba, tg: lambda e: e.scalar_tensor_tensor(
                            out=usf[:, j, :], in0=tg[:, Tp:T], scalar=1.0, in1=ps[ba][:, Tp:T], op0=ALU.add, op1=ALU.mult))(j, ba, tg),
                            reads=[tgb, b_ps[ba]], writes=[b_usf])
                if ck(tl, l, 'P2a'):
                    return
                def rope_chunk(bq, dst_ap_fn, dst_bufs, extra=None):
                    qi = rot("qraw", 2)
                    S.op("act", (lambda bq, qi: lambda e: e.activation(out=qraw[qi][:, 0:T], in_=ps[bq][:, 0:T], func=AF.Copy))(bq, qi),
                         reads=[b_ps[bq]], writes=[b_qraw[qi]])
                    def rest():
                        bs = bank()
                        S.op("pe", (lambda bs, qi: lambda e: e.matmul(ps[bs][:, 0:T], lhsT=permb[:], rhs=qraw[qi][:, 0:T], start=True, stop=True))(bs, qi),
                             reads=[b_qraw[qi], b_const2], writes=[b_ps[bs]])
                        t1, t1b = tmpf()
                        t2, t2b = tmpf()
                        S.op("dve", (lambda bq, t1: lambda e: e.tensor_tensor(out=t1[:, 0:T], in0=ps[bq][:, 0:T], in1=ropeC[:, 0:T], op=ALU.mult))(bq, t1),
                             reads=[b_ps[bq], b_rope], writes=[t1b])
                        S.op("dve", (lambda bs, t2: lambda e: e.tensor_tensor(out=t2[:, 0:T], in0=ps[bs][:, 0:T], in1=ropeS[:, 0:T], op=ALU.mult))(bs, t2),
                             reads=[b_ps[bs], b_rope], writes=[t2b])
                        dsts = dst_ap_fn(0, T)
                        if not isinstance(dsts, list):
                            dsts = [(dsts, slice(0, 128))]
                        for (dap, rws) in dsts:
                            S.op("dve", (lambda t1, t2, dap, rws: lambda e: e.tensor_tensor(out=dap, in0=t1[rws, 0:T], in1=t2[rws, 0:T], op=ALU.add))(t1, t2, dap, rws),
                                 reads=[t1b, t2b], writes=dst_bufs)
                        if extra is not None:
                            for (oap, c0, c1, obuf) in extra:
                                S.op("dve", (lambda t1, t2, oap, c0, c1: lambda e: e.tensor_tensor(out=oap, in0=t1[:, c0:c1], in1=t2[:, c0:c1], op=ALU.add))(t1, t2, oap, c0, c1),
                                     reads=[t1b, t2b], writes=[obuf])

                    return rest

                pend_rope = None
                for j in range(NCH):
                    bq = proj(hT, [b_hT], T)
                    if pend_rope is not None:
                        pend_rope()
                    pend_rope = rope_chunk(bq, (lambda j: lambda c0, c1: qT[:, j, c0:c1])(j), b_q[j // 4])
                for kv in range(2):
                    bk = proj(hT, [b_hT], T)
                    if pend_rope is not None:
                        pend_rope()
                    extra = []
                    if tl["last"]:
                        extra.append((kfin[:, kv, :], Tp - 128, Tp, b_kfin))
                    if Ts:
                        extra.append((ksf[:, kv, :], Tp, T, b_ksf))
                    pend_rope = rope_chunk(bk, (lambda kv: lambda c0, c1: [(kT[l][0:64, kv, 0, 128 + c0:128 + c1], slice(0, 64)),
                                                               (kT[l][64:128, kv, 1, 128 + c0:128 + c1], slice(64, 128))])(kv), [b_kT[l]], extra)
                pend_rope()
                wv, wvb = wnext()
                bv = bank()

                def fv(e, bv=bv, wv=wv):
                    for blk in range(nb):
                        for kc in range(NCH):
                            i = e.matmul(ps[bv][:, blk * 128:(blk + 1) * 128], lhsT=hT[:, kc, blk * 128:(blk + 1) * 128], rhs=wv[:, kc, :],
                                         start=(kc == 0), stop=(kc == NCH - 1))
                    if Ts:
                        for kc in range(NCH):
                            i = e.matmul(ps[bv][0:NS, nb * 128:(nb + 1) * 128], lhsT=hT[:, kc, Tp:T], rhs=wv[:, kc, :],
                                         start=(kc == 0), stop=(kc == NCH - 1))
                    return i
                S.op("pe", fv, reads=[b_hT, wvb], writes=[b_ps[bv]])
                psv = ps[bv][:, 0:nb * 128].rearrange("p (b k d) -> p b k d", b=nb, k=2)
                S.op("act", (lambda psv: lambda e: e.activation(out=V4[l][:, 1:1 + nb, 0, :, 0:64], in_=psv, func=AF.Copy))(psv),
                     reads=[b_ps[bv]], writes=[b_V4[l]])
                S.op("act", (lambda psv: lambda e: e.activation(out=V4[l][:, 1:1 + nb, 1, :, 64:128], in_=psv, func=AF.Copy))(psv),
                     reads=[b_ps[bv]], writes=[b_V4[l]])
                if tl["last"]:
                    S.op("dve", (lambda bv: lambda e: e.tensor_copy(out=vfin[:], in_=ps[bv][:, (nb - 1) * 128:nb * 128]))(bv),
                         reads=[b_ps[bv]], writes=[b_vfin])
                    S.dma("sp", nvp_o[l], vfin[:], b_vfin, reads=[b_vfin])
                if Ts:
                    S.op("dve", (lambda bv: lambda e: e.tensor_copy(out=vnew[:], in_=ps[bv][0:NS, nb * 128:(nb + 1) * 128]))(bv),
                         reads=[b_ps[bv]], writes=[b_vnew])
                    S.dma("sp", nvs_o[l, :, 127, :], vnew[:], b_vnew, reads=[b_vnew])
                if ck(tl, l, 'P2v'):
                    return
                for j in range(NCH):
                    bg = proj(hT, [b_hT], T)
                    S.op("act", (lambda j, bg: lambda e: e.activation(out=sgb[:, j, 0:T], in_=ps[bg][:, 0:T], func=AF.Silu))(j, bg),
                         reads=[b_ps[bg]], writes=[b_sgb])
                if ck(tl, l, 'P2b'):
                    return
                pend_stat = []
                for j in range(NCH):
                    bc = bank()
                    halves = []
                    for half in range(2):
                        t0, t1_ = (0, 16) if half == 0 else (16, CK)
                        di = rot("dg", 2)
                        nt = t1_ - t0
                        halves.append((di, t0, t1_))
                        S.op("dve", (lambda di, j, t0, nt: lambda e: e.tensor_tensor(
                            out=Dg[di][:, 0:nt, :], in0=identf[:].unsqueeze(1).broadcast_to([128, nt, 128]),
                            in1=cwh[:, l, j, t0:t0 + nt].unsqueeze(2).broadcast_to([128, nt, 128]), op=ALU.mult))(di, j, t0, nt),
                            reads=[b_init], writes=[b_Dg[di]])

                        def fconv(e, di=di, j=j, t0=t0, t1_=t1_, bc=bc):
                            for tap in range(t0, t1_):
                                i = e.matmul(ps[bc][:, 0:Tp], lhsT=Dg[di][:, tap - t0, :], rhs=uT[:, j, tap:tap + Tp],
                                             start=(tap == 0), stop=(tap == CK - 1))
                            return i
                        S.op("pe", fconv, reads=[b_Dg[di], b_uT[j]], writes=[b_ps[bc]])
                    if Ts:
                        for (di, t0, t1_) in halves:
                            def fconvs(e, di=di, j=j, t0=t0, t1_=t1_, bc=bc):
                                for tap in range(t0, t1_):
                                    rhs = stT[:, j, :, tap] if tap < CB else uT[:, j, CB + Tp:CB + T]
                                    i = e.matmul(ps[bc][:, Tp:T], lhsT=Dg[di][:, tap - t0, :], rhs=rhs,
                                                 start=(tap == 0), stop=(tap == CK - 1))
                                return i
                            S.op("pe", fconvs, reads=[b_Dg[di], b_uT[j], b_stT], writes=[b_ps[bc]])
                    S.op("act", (lambda j, bc: lambda e: e.activation(out=cF[:, j, 0:T], in_=ps[bc][:, 0:T], func=AF.Identity, bias=gcol(PP_CB, j)))(j, bc),
                         reads=[b_ps[bc], b_const], writes=[b_cF[j]])
                    r1, r1b = r16()
                    r2, r2b = r16()
                    S.op("act", (lambda j, r1: lambda e: e.activation(out=r1[:, 0:T], in_=cF[:, j, 0:T], func=AF.Copy))(j, r1),
                         reads=[b_cF[j]], writes=[r1b])
                    S.op("act", (lambda j, r2: lambda e: e.activation(out=r2[:, 0:T], in_=cF[:, j, 0:T], func=AF.Square))(j, r2),
                         reads=[b_cF[j]], writes=[r2b])
                    def stat_mm(j=j, r1=r1, r2=r2, r1b=r1b, r2b=r2b):
                        S.op("pe", lambda e: e.matmul(ps[6][:, 0:T], lhsT=onesS[:], rhs=r1[:, 0:T], start=(j == 0), stop=(j == NCH - 1)),
                             reads=[r1b, b_init], writes=[b_ps[6]])
                        S.op("pe", lambda e: e.matmul(ps[7][:, 0:T], lhsT=onesS[:], rhs=r2[:, 0:T], start=(j == 0), stop=(j == NCH - 1)),
                             reads=[r2b, b_init], writes=[b_ps[7]])
                    if pend_stat:
                        pend_stat.pop(0)()
                    pend_stat.append(stat_mm)
                while pend_stat:
                    pend_stat.pop(0)()
                S.op("pool", lambda e: e.tensor_copy(out=uh[:, l, :, :], in_=uT[:, :, Tp:Tp + CB]), reads=b_uT, writes=[b_uh[l]])
                S.op("act", lambda e: e.activation(out=st[0][:, 0:T], in_=ps[6][:, 0:T], func=AF.Copy), reads=[b_ps[6]], writes=[b_st[0]])
                S.op("dve", lambda e: e.tensor_tensor(out=st[1][:, 0:T], in0=st[0][:, 0:T], in1=st[0][:, 0:T], op=ALU.mult),
                     reads=[b_st[0]], writes=[b_st[1]])
                S.op("dve", lambda e: e.scalar_tensor_tensor(out=st[2][:, 0:T], in0=ps[7][:, 0:T], scalar=EPS, in1=st[1][:, 0:T],
                                                              op0=ALU.add, op1=ALU.subtract),
                     reads=[b_ps[7], b_st[1]], writes=[b_st[2]])
                S.op("act", lambda e: e.activation(out=st[1][:, 0:T], in_=st[2][:, 0:T], func=AF.Sqrt), reads=[b_st[2]], writes=[b_st[1]])
                S.op("dve", lambda e: e.reciprocal(out=st[2][:, 0:T], in_=st[1][:, 0:T]), reads=[b_st[1]], writes=[b_st[2]])
                S.op("dve", lambda e: e.scalar_tensor_tensor(out=st[3][:, 0:T], in0=st[0][:, 0:T], scalar=-1.0, in1=st[2][:, 0:T],
                                                               op0=ALU.mult, op1=ALU.mult),
                     reads=[b_st[0], b_st[2]], writes=[b_st[3]])
                for j in range(NCH):
                    bg = proj(hT, [b_hT], T)
                    si = rot("sga", 2)
                    S.op("act", (lambda bg, si: lambda e: e.activation(out=sga[si][:, 0:T], in_=ps[bg][:, 0:T], func=AF.Silu))(bg, si),
                         reads=[b_ps[bg]], writes=[b_sga[si]])
                    t1, t1b = tmpf()
                    t2, t2b = tmpf()
                    S.op("dve", (lambda j, t1: lambda e: e.tensor_tensor(out=t1[:, 0:T], in0=cF[:, j, 0:T], in1=st[2][:, 0:T], op=ALU.mult))(j, t1),
                         reads=[b_cF[j], b_st[2]], writes=[t1b])
                    S.op("dve", (lambda t1, t2: lambda e: e.tensor_tensor(out=t2[:, 0:T], in0=t1[:, 0:T], in1=st[3][:, 0:T], op=ALU.add))(t1, t2),
                         reads=[t1b, b_st[3]], writes=[t2b])
                    S.op("act", (lambda j, t2, t1: lambda e: e.activation(out=t1[:, 0:T], in_=t2[:, 0:T], func=AF.Silu,
                                                                        scale=gcol(PP_LG, j), bias=gcol(PP_LB, j)))(j, t2, t1),
                         reads=[t2b, b_const], writes=[t1b])
                    S.op("dve", (lambda j, t1, si: lambda e: e.tensor_tensor(out=cc[:, j, 0:T], in0=t1[:, 0:T], in1=sga[si][:, 0:T], op=ALU.mult))(j, t1, si),
                         reads=[t1b, b_sga[si]], writes=[b_cc])
                if ck(tl, l, 'P3'):
                    return
                if ck(tl, l, 'P4'):
                    return
                obT = qT
                groups = [(qb, kv, par) for qb in range(nb) for kv in range(2) for par in range(2)]
                pend = None

                def attn_pv(pair):
                    ta, tab = tmpf()
                    b3s = []
                    for (g, pi) in pair:
                        qb, kv, par = g
                        dh = slice((1 - par) * 64, (1 - par) * 64 + 64)
                        b3 = bank()
                        b3s.append(b3)

                        def fpv(e, b3=b3, qb=qb, kv=kv, par=par, pi=pi):
                            e.matmul(ps[b3][:], lhsT=V4[l][:, qb, par, kv, :], rhs=pT[pi][:, 0, :], start=True, stop=False)
                            return e.matmul(ps[b3][:], lhsT=V4[l][:, qb + 1, par, kv, :], rhs=pT[pi][:, 1, :], start=False, stop=True)
                        S.op("pe", fpv, reads=[b_V4[l], b_pT[pi]], writes=[b_ps[b3]])
                        hs = l * 16 + kv * 8 + par
                        S.op("dve", (lambda b3, dh, hs: lambda e: e.tensor_tensor(
                            out=ta[dh, :].rearrange("p (a c) -> p a c", a=4), in0=ps[b3][dh, :].rearrange("p (a c) -> p a c", a=4),
                            in1=esk[dh, hs:hs + 7:2].unsqueeze(2).broadcast_to([64, 4, 128]), op=ALU.add))(b3, dh, hs),
                            reads=[b_ps[b3], b_const], writes=[tab])
                    S.op("dve", lambda e: e.reciprocal(out=ta[:, :], in_=ta[:, :]), reads=[tab], writes=[tab])
                    for (g, pi), b3 in zip(pair, b3s):
                        qb, kv, par = g
                        oh = slice(par * 64, par * 64 + 64)
                        dh = slice((1 - par) * 64, (1 - par) * 64 + 64)
                        tb, tbb = tmpf()
                        S.op("dve", (lambda b3, oh, dh, tb: lambda e: e.tensor_tensor(out=tb[oh, :], in0=ps[b3][oh, :], in1=ta[dh, :], op=ALU.mult))(b3, oh, dh, tb),
                             reads=[b_ps[b3], tab], writes=[tbb])
                        S.op("pool", (lambda oh, tb, kv, qb: lambda e: e.tensor_tensor(
                            out=obT[oh, kv * 4:kv * 4 + 4, qb * 128:(qb + 1) * 128], in0=tb[oh, :].rearrange("p (a c) -> p a c", a=4),
                            in1=sgb[oh, kv * 4:kv * 4 + 4, qb * 128:(qb + 1) * 128], op=ALU.mult))(oh, tb, kv, qb),
                            reads=[tbb, b_sgb], writes=[b_q[kv][qb]])

                def attn_steps():
                    pairs = [[(qb, kv, 0), (qb, kv, 1)] for qb in range(nb) for kv in range(2)]
                    pend = None
                    for pr in pairs:
                        cur = []
                        for g in pr:
                            qb, kv, par = g
                            pi = rot("pT", 4)
                            mprev = msk[:, 2, :] if qb == tl["first_qb"] else msk[:, 0, :]
                            for kt in range(2):
                                bqk = bank()
                                mk = mprev if kt == 0 else msk[:, 1, :]

                                def fqk(e, bqk=bqk, kt=kt, mk=mk, qb=qb, kv=kv, par=par):
                                    e.matmul(ps[bqk][:].rearrange("p (a c) -> p a c", a=4), lhsT=kT[l][:, kv, par, (qb + kt) * 128:(qb + kt + 1) * 128],
                                             rhs=qT[:, kv * 4:kv * 4 + 4, qb * 128:(qb + 1) * 128], start=True, stop=False)
                                    return e.matmul(ps[bqk][:], lhsT=identb[:], rhs=mk, start=False, stop=True)
                                S.op("pe", fqk, reads=[b_kT[l], b_q[kv][qb], b_init, b_const2], writes=[b_ps[bqk]])
                                S.op("act", (lambda bqk, pi, kt: lambda e: e.activation(out=pT[pi][:, kt, :], in_=ps[bqk][:], func=AF.Exp, scale=0.125))(bqk, pi, kt),
                                     reads=[b_ps[bqk]], writes=[b_pT[pi]])
                            cur.append((g, pi))
                        if pend is not None:
                            attn_pv(pend)
                        pend = cur
                        yield
                    attn_pv(pend)
                    yield
                    S.op("pool", lambda e: e.tensor_copy(out=kT[l][:, :, :, 0:128], in_=kT[l][:, :, :, Tp:Tp + 128]), reads=[b_kT[l]], writes=[b_kT[l]])
                    S.op("pool", lambda e: e.tensor_copy(out=V4[l][:, 0], in_=V4[l][:, nb]), reads=[b_V4[l]], writes=[b_V4[l]])
                    if Ts:
                        bkn = bank()

                        def fkn(e, bkn=bkn):
                            e.transpose(ps[bkn][0:NS, 0:128], ksf[:, 0, :], identf[:])
                            return e.transpose(ps[bkn][0:NS, 128:256], ksf[:, 1, :], identf[:])
                        S.op("pe", fkn, reads=[b_ksf, b_init], writes=[b_ps[bkn]])
                        S.op("act", (lambda bkn: lambda e: e.activation(out=knew[:], in_=ps[bkn][0:NS, 0:256].rearrange("p (k d) -> p k d", k=2)[:, :, 0:64],
                                                                       func=AF.Copy))(bkn), reads=[b_ps[bkn]], writes=[b_knew])
                        S.dma("sp", nks_o[l, :, 127, :], knew[:].rearrange("p k d -> p (k d)"), b_knew, reads=[b_knew])
                        bss = 6
                        bos = 7
                        for g4 in range(4):
                            Ks, b_Ks, Vs, b_Vs = Ks2[g4 % 2], b_Ks2[g4 % 2], Vs2[g4 % 2], b_Vs2[g4 % 2]
                            b_Ksb, b_Vsb = b_Ks2b[g4 % 2], b_Vs2b[g4 % 2]
                            S.dma("sp", Ks[0:112, :, :], ckd[l, 4 * g4:4 * g4 + 4, 1:113, :].rearrange("b k d -> k b d"), b_Ks)
                            S.dma("sp", Ks[112:127, :, :], ckd[l, 4 * g4:4 * g4 + 4, 113:128, :].rearrange("b k d -> k b d"), b_Ksb, reads=[], writes=[])
                            S.dma("sp", Ks[127:128, :, :], knew[4 * g4:4 * g4 + 4, :, :].rearrange("p k d -> p (k d)"), b_Ksb, reads=[b_knew])
                            S.dma("sp", Vs[0:112, :, :], cvd[l, 4 * g4:4 * g4 + 4, 1:113, :].rearrange("b k d -> k b d"), b_Vs)
                            S.dma("sp", Vs[112:127, :, :], cvd[l, 4 * g4:4 * g4 + 4, 113:128, :].rearrange("b k d -> k b d"), b_Vsb)
                            S.dma("sp", Vs[127:128, :, :], vnew[4 * g4:4 * g4 + 4, :], b_Vsb, reads=[b_vnew])
                            bt = bank()

                            def ftk(e, bt=bt, Ks=Ks):
                                for i in range(4):
                                    r = e.transpose(ps[bt][:, i * 128:(i + 1) * 128], Ks[:, i, :], identf[:])
                                return r
                            S.op("pe", ftk, reads=[b_Ks, b_Ksb, b_init], writes=[b_ps[bt]])
                            pst = ps[bt][:].rearrange("p (a c) -> p a c", a=4)
                            S.op("act", (lambda pst: lambda e: e.activation(out=KTs[0:64, :, 0, 0, :], in_=pst[0:64], func=AF.Copy))(pst), reads=[b_ps[bt]], writes=[b_KTs])
                            S.op("dve", (lambda pst: lambda e: e.tensor_copy(out=KTs[64:128, :, 0, 1, :], in_=pst[0:64]))(pst), reads=[b_ps[bt]], writes=[b_KTs])
                            S.op("act", (lambda pst: lambda e: e.activation(out=KTs[0:64, :, 1, 0, :], in_=pst[64:128], func=AF.Copy))(pst), reads=[b_ps[bt]], writes=[b_KTs])
                            S.op("dve", (lambda pst: lambda e: e.tensor_copy(out=KTs[64:128, :, 1, 1, :], in_=pst[64:128]))(pst), reads=[b_ps[bt]], writes=[b_KTs])
                            vsv = Vs[:].rearrange("p b (k d) -> p b k d", k=2)
                            S.op("pool", (lambda vsv: lambda e: e.tensor_copy(out=V4s[:, :, 0, :, 0:64], in_=vsv))(vsv), reads=[b_Vs, b_Vsb], writes=[b_V4s])
                            S.op("pool", (lambda vsv: lambda e: e.tensor_copy(out=V4s[:, :, 1, :, 64:128], in_=vsv))(vsv), reads=[b_Vs, b_Vsb], writes=[b_V4s])

                            def fsqk(e, g4=g4):
                                for i in range(4):
                                    bsm = 4 * g4 + i
                                    for kv in range(2):
                                        for par in range(2):
                                            c0 = bsm * 16 + kv * 8 + par * 4
                                            r = e.matmul(ps[bss][:, c0:c0 + 4], lhsT=KTs[:, i, kv, par, :], rhs=qT[:, kv * 4:kv * 4 + 4, Tp + bsm],
                                                         start=True, stop=True)
                                return r
                            S.op("pe", fsqk, reads=[b_KTs, b_q[0][4], b_q[1][4]], writes=[b_ps[bss]])
                            S.op("act", (lambda g4: lambda e: e.activation(out=pTs[:, g4 * 64:(g4 + 1) * 64], in_=ps[bss][:, g4 * 64:(g4 + 1) * 64], func=AF.Exp, scale=0.125))(g4),
                                 reads=[b_ps[bss]], writes=[b_pTs])

                            def fspv(e, g4=g4):
                                for i in range(4):
                                    bsm = 4 * g4 + i
                                    for kv in range(2):
                                        for par in range(2):
                                            c0 = bsm * 16 + kv * 8 + par * 4
                                            r = e.matmul(ps[bos][:, c0:c0 + 4], lhsT=V4s[:, i, par, kv, :], rhs=pTs[:, c0:c0 + 4], start=True, stop=True)
                                return r
                            S.op("pe", fspv, reads=[b_V4s, b_pTs], writes=[b_ps[bos]])
                            yield
                        pov = ps[bos][:, 0:256].rearrange("p (b k r j) -> p b k r j", b=NS, k=2, r=2)
                        for par in range(2):
                            oh = slice(par * 64, par * 64 + 64)
                            dh = slice((1 - par) * 64, (1 - par) * 64 + 64)
                            ta, tab = tmpf()
                            tb, tbb = tmpf()
                            tav = ta[:, 0:128].rearrange("p (b k j) -> p b k j", b=NS, k=2)
                            tbv = tb[:, 0:128].rearrange("p (b k j) -> p b k j", b=NS, k=2)
                            hs = l * 16 + par
                            S.op("dve", (lambda par, dh, tav, hs: lambda e: e.tensor_tensor(
                                out=tav[dh], in0=pov[dh, :, :, par, :],
                                in1=esk[dh, hs:hs + 15:2].rearrange("p (k j) -> p k j", k=2).unsqueeze(1).broadcast_to([64, NS, 2, 4]), op=ALU.add))(par, dh, tav, hs),
                                reads=[b_ps[bos], b_const], writes=[tab])
                            S.op("dve", (lambda dh, ta: lambda e: e.reciprocal(out=ta[dh, 0:128], in_=ta[dh, 0:128]))(dh, ta), reads=[tab], writes=[tab])
                            S.op("dve", (lambda par, oh, dh, tav, tbv: lambda e: e.tensor_tensor(out=tbv[oh], in0=pov[oh, :, :, par, :], in1=tav[dh], op=ALU.mult))(par, oh, dh, tav, tbv),
                                 reads=[b_ps[bos], tab], writes=[tbb])
                            S.op("pool", (lambda oh, tb: lambda e: e.tensor_tensor(
                                out=obT[oh, :, Tp:T], in0=tb[oh, 0:128].rearrange("p (b c) -> p c b", b=NS),
                                in1=sgb[oh, :, Tp:T], op=ALU.mult))(oh, tb),
                                reads=[tbb, b_sgb], writes=[b_q[0][4], b_q[1][4]])
                def p4_steps():
                    for j in range(NCH):
                        b1 = proj(hT, [b_hT], T)
                        tm, tmb = tmpf()
                        S.op("act", (lambda b1, tm: lambda e: e.activation(out=tm[:, 0:T], in_=ps[b1][:, 0:T], func=AF.Tanh, scale=0.5))(b1, tm),
                             reads=[b_ps[b1]], writes=[tmb])
                        b2 = proj(cc, [b_cc], T)
                        S.op("dve", (lambda j, b2, tm: lambda e: e.scalar_tensor_tensor(
                            out=cF[:, j, 0:T], in0=tm[:, 0:T], scalar=1.0, in1=ps[b2][:, 0:T], op0=ALU.add, op1=ALU.mult))(j, b2, tm),
                            reads=[tmb, b_ps[b2]], writes=[b_cF[j]])
                        yield
                if not Ts:
                    nbank[0] = 8
                its = [attn_steps(), p4_steps()]
                for _ in range(2):
                    try:
                        next(its[0])
                    except StopIteration:
                        its.pop(0)
                        break
                while its:
                    for it in list(its):
                        try:
                            next(it)
                        except StopIteration:
                            its.remove(it)
                nbank[0] = 6
                if ck(tl, l, 'P5s'):
                    return
                for j in range(NCH):
                    b1 = proj(hT, [b_hT], T)
                    tm, tmb = tmpf()
                    S.op("act", (lambda b1, tm: lambda e: e.activation(out=tm[:, 0:T], in_=ps[b1][:, 0:T], func=AF.Tanh, scale=0.5))(b1, tm),
                         reads=[b_ps[b1]], writes=[tmb])
                    b2 = proj(obT, b_q[0] + b_q[1], T)
                    if DBG_ROPE == 5:
                        continue
                    t2, t2b = tmpf()
                    S.op("dve", (lambda b2, tm, t2: lambda e: e.scalar_tensor_tensor(
                        out=t2[:, 0:T], in0=tm[:, 0:T], scalar=1.0, in1=ps[b2][:, 0:T], op0=ALU.add, op1=ALU.mult))(b2, tm, t2),
                        reads=[tmb, b_ps[b2]], writes=[t2b])
                    if DBG_ROPE == 6:
                        continue
                    S.op("pool", (lambda j, t2: lambda e: e.tensor_tensor(out=yT[:, j, 0:T], in0=t2[:, 0:T], in1=cF[:, j, 0:T], op=ALU.add))(j, t2),
                         reads=[t2b, b_cF[j]], writes=[b_yT])
                if ck(tl, l, 'P6'):
                    return
                pend_rms = None
                for j in range(NCH):
                    bo = proj(yT, [b_yT], T)
                    if pend_rms is not None:
                        pend_rms()
                    S.op("dve", (lambda j, bo: lambda e: e.scalar_tensor_tensor(
                        out=xT[:, j, 0:T], in0=ps[bo][:, 0:T], scalar=0.5, in1=xT[:, j, 0:T], op0=ALU.mult, op1=ALU.add))(j, bo),
                        reads=[b_ps[bo], b_xT[j]], writes=[b_xT[j]])
                    if tl["halo"] and l == 0:
                        S.op("dve", (lambda j: lambda e: e.tensor_scalar(out=xT[:, j, 0:tl["halo"]], in0=xT[:, j, 0:tl["halo"]], scalar1=valid[:, 0:1], scalar2=None, op0=ALU.mult))(j),
                             reads=[b_xT[j], b_const], writes=[b_xT[j]])
                    pend_rms = rms_accum(T, j, xT, b_xT[j])
                pend_rms()
                rms_finish(T)
                wnext()
                if ck(tl, l, 'P7'):
                    return
                if tl["last"]:
                    bi = rot("big", 2)
                    for g in range(2):
                        b = bank()

                        def ftu(e, g=g, b=b):
                            for jj in range(4):
                                i = e.transpose(ps[b][0:CB, jj * 128:(jj + 1) * 128], ufin[:, 4 * g + jj, :], identf[:])
                            return i
                        S.op("pe", ftu, reads=[b_ufin, b_init], writes=[b_ps[b]])
                        S.op("act", (lambda b, g, bi: lambda e: e.activation(out=big[bi][0:CB, g * 512:(g + 1) * 512], in_=ps[b][0:CB, :], func=AF.Copy, scale=0.5))(b, g, bi),
                             reads=[b_ps[b]], writes=[b_big[bi]])
                    S.dma("sp", ncp_o[l], big[bi][0:CB, :], b_big[bi], reads=[b_big[bi]])
                    b = bank()

                    def ftkf(e, b=b):
                        e.transpose(ps[b][:, 0:128], kfin[:, 0, :], identf[:])
                        return e.transpose(ps[b][:, 128:256], kfin[:, 1, :], identf[:])
                    S.op("pe", ftkf, reads=[b_kfin, b_init], writes=[b_ps[b]])
                    S.op("act", (lambda b: lambda e: e.activation(out=nkb[:], in_=ps[b][:, 0:256].rearrange("p (k d) -> p k d", k=2)[:, :, 0:64], func=AF.Copy))(b),
                         reads=[b_ps[b]], writes=[b_nkb])
                    S.dma("sp", nkp_o[l], nkb[:].rearrange("p k d -> p (k d)"), b_nkb, reads=[b_nkb])
                if Ts:
                    bi = rot("big", 2)
                    for g in range(2):
                        b = bank()

                        def ftus(e, g=g, b=b):
                            for jj in range(4):
                                i = e.transpose(ps[b][0:NS, jj * 128:(jj + 1) * 128], usf[:, 4 * g + jj, :], identf[:])
                            return i
                        S.op("pe", ftus, reads=[b_usf, b_init], writes=[b_ps[b]])
                        S.op("act", (lambda b, g, bi: lambda e: e.activation(out=big[bi][0:NS, g * 512:(g + 1) * 512], in_=ps[b][0:NS, :], func=AF.Copy, scale=0.5))(b, g, bi),
                             reads=[b_ps[b]], writes=[b_big[bi]])
                    S.dma("sp", ncs_o[l, :, CB - 1, :], big[bi][0:NS, :], b_big[bi], reads=[b_big[bi]])

            for l_ in range(2):
                do_layer(l_)
            if ck(tl, 2, 'OUT'):
                return
            for j in range(NCH):
                S.op("dve", (lambda j: lambda e: e.scalar_tensor_tensor(
                    out=cF[:, j, 0:T], in0=xT[:, j, 0:T], scalar=pp[:, PP_GF + j:PP_GF + j + 1], in1=st[1][:, 0:T],
                    op0=ALU.mult, op1=ALU.mult))(j), reads=[b_xT[j], b_st[1], b_const], writes=[b_cF[j]])
            if ck(tl, 2, 'P8a'):
                return
            if True:
                for blk in range(tl["yblk0"], nb):
                    bi = rot("big", 2)
                    for g in range(2):
                        b = bank()

                        def fty(e, g=g, b=b, blk=blk):
                            for jj in range(4):
                                i = e.transpose(ps[b][:, jj * 128:(jj + 1) * 128], cF[:, 4 * g + jj, blk * 128:(blk + 1) * 128], identf[:])
                            return i
                        S.op("pe", fty, reads=b_cF + [b_init], writes=[b_ps[b]])
                        S.op("act", (lambda b, g, bi: lambda e: e.activation(out=big[bi][:, g * 512:(g + 1) * 512], in_=ps[b][:], func=AF.Copy))(b, g, bi),
                             reads=[b_ps[b]], writes=[b_big[bi]])
                    S.dma("sp", y_o[tl["yrow0"] + (blk - tl["yblk0"]) * 128: tl["yrow0"] + (blk - tl["yblk0"] + 1) * 128, :], big[bi][:], b_big[bi], reads=[b_big[bi]])
            if Ts:
                bi = rot("big", 2)
                for g in range(2):
                    b = bank()

                    def ftys(e, g=g, b=b):
                        for jj in range(4):
                            i = e.transpose(ps[b][0:NS, jj * 128:(jj + 1) * 128], cF[:, 4 * g + jj, Tp:T], identf[:])
                        return i
                    S.op("pe", ftys, reads=b_cF + [b_init], writes=[b_ps[b]])
                    S.op("act", (lambda b, g, bi: lambda e: e.activation(out=big[bi][0:NS, g * 512:(g + 1) * 512], in_=ps[b][0:NS, :], func=AF.Copy))(b, g, bi),
                         reads=[b_ps[b]], writes=[b_big[bi]])
                if DBG_ROPE != 8:
                    S.dma("sp", ys_o, big[bi][0:NS, :], b_big[bi], reads=[b_big[bi]])

        for ti_, tl_ in enumerate(TILES):
            do_tile(ti_, tl_)
        if DBG_ROPE == 7:
            for _ in range(16):
                wnext()
        assert DBG_STOP is not None or wctr[0] == len(TILES) * 2 * NCHUNK, wctr[0]
        S.emit(final_wait_bufs=b_big + [b_vfin, b_nkb, b_knew, b_vnew, b_dd])
        S.close()
    return nc


_CACHE = {}


def _rope_tables(half):
    inv = (np.float32(500000.0) ** (-(np.arange(0, 16, 2, dtype=np.float32)) / np.float32(16))).astype(np.float32)
    pos = np.zeros(NCOLS, np.float32)
    hp = np.arange(HALO, dtype=np.float32) + np.float32(half * OWN - HALO)
    pos[0:HALO] = np.maximum(hp, 0)
    pos[HALO:HALO + OWN] = np.arange(OWN, dtype=np.float32) + np.float32(half * OWN)
    pos[HALO + OWN:] = PAST
    ang = pos[None, :] * inv[:, None]
    cos = np.cos(ang).astype(np.float32)
    sin = np.sin(ang).astype(np.float32)
    C = np.ones((128, NCOLS), np.float32)
    Sg = np.zeros((128, NCOLS), np.float32)
    for base in (0, 64):
        C[base:base + 8] = cos
        C[base + 8:base + 16] = cos
        Sg[base:base + 8] = -sin
        Sg[base + 8:base + 16] = sin
    return C, Sg


def kernel(x_prompt, x_sample, state_conv, cache_k_win, cache_v_win, norm_g, w_in, conv_w, conv_b, conv_ln_g,
           conv_ln_b, w_conv_out, attn_sinks, w_attn_out, w_out, final_norm_g):
    f = lambda a: np.asarray(a, dtype=np.float32)
    x_prompt, x_sample, state_conv = f(x_prompt), f(x_sample), f(state_conv)
    ck = f(cache_k_win).reshape(2, 128, 128, 128)
    cv = f(cache_v_win).reshape(2, 128, 128, 128)
    wstream = _build_wstream(f(w_in), f(w_conv_out), f(w_attn_out), f(w_out))
    pp = np.zeros((128, PP_N), np.float32)
    fm = lambda v: f(v).reshape(2, 8, 128).transpose(2, 0, 1).reshape(128, 16)
    pp[:, PP_G:PP_G + 16] = fm(norm_g)
    pp[:, PP_CB:PP_CB + 16] = fm(conv_b)
    pp[:, PP_LG:PP_LG + 16] = fm(conv_ln_g)
    pp[:, PP_LB:PP_LB + 16] = fm(conv_ln_b)
    pp[:, PP_GF:PP_GF + 8] = f(final_norm_g).reshape(8, 128).T
    pp[:, PP_CW:] = f(conv_w).reshape(2, CK, 8, 128).transpose(3, 0, 2, 1).reshape(128, 2 * 8 * CK)
    snk = f(attn_sinks).reshape(32)
    jj = np.arange(128)[:, None]
    ii = np.arange(128)[None, :]
    mprev = np.where(jj > ii, 0.0, NEG).astype(np.float32)
    mcur = np.where(jj <= ii, 0.0, NEG).astype(np.float32)
    perm = np.zeros((128, 128), np.float32)
    for m in range(128):
        d = m % 64
        if d < 8:
            perm[m + 8, m] = 1.0
        elif d < 16:
            perm[m - 8, m] = 1.0
    if "nc" not in _CACHE:
        _CACHE["nc"] = build_program()
    nc = _CACHE["nc"]
    in_maps = []
    for c in range(NCORES):
        s, half = divmod(c, 2)
        xp = np.zeros((HALO + OWN, D), np.float32)
        if half == 1:
            xp[0:HALO] = x_prompt[s, OWN - HALO:OWN]
        xp[HALO:] = x_prompt[s, half * OWN:(half + 1) * OWN]
        C, Sg = _rope_tables(half)
        msk = np.zeros((128, 3, 512), np.float32)
        msk[:, 0] = np.tile(mprev, (1, 4))
        msk[:, 1] = np.tile(mcur, (1, 4))
        msk[:, 2] = np.tile(mprev, (1, 4)) if half == 1 else NEG
        in_maps.append({
            "xp": xp,
            "xs": np.ascontiguousarray(x_sample[NS * c:NS * (c + 1), 0, :]),
            "stc": np.ascontiguousarray(state_conv[:, NS * c:NS * (c + 1)]),
            "ck": np.ascontiguousarray(ck[:, NS * c:NS * (c + 1)]),
            "cv": np.ascontiguousarray(cv[:, NS * c:NS * (c + 1)]),
            "wst": wstream,
            "pp": pp,
            "snk": snk,
            "ropec": C,
            "ropes": Sg,
            "msk": msk,
            "perm": perm,
            "valid": np.full((128, 1), float(half), np.float32),
        })
    res = run_bass_kernel_spmd(nc, in_maps, core_ids=list(range(NCORES)))
    R = res.results
    y_prompt = np.zeros((4, SEQ, D), np.float32)
    y_sample = np.zeros((128, 1, D), np.float32)
    ncp = np.zeros((2, 4, CB, D), np.float32)
    nkp = np.zeros((2, 4, 128, 2, 64), np.float32)
    nvp = np.zeros((2, 4, 128, 2, 64), np.float32)
    ncs = np.zeros((2, 128, CB, D), np.float32)
    nks = np.zeros((2, 128, 128, 2, 64), np.float32)
    nvs = np.zeros((2, 128, 128, 2, 64), np.float32)
    for c in range(NCORES):
        s, half = divmod(c, 2)
        r = R[c]
        y_prompt[s, half * OWN:(half + 1) * OWN] = r["y"]
        y_sample[NS * c:NS * (c + 1), 0] = r["ys"]
        if half == 1:
            ncp[:, s] = r["ncp"]
            nkp[:, s] = r["nkp"].reshape(2, 128, 2, 64)
            nvp[:, s] = r["nvp"].reshape(2, 128, 2, 64)
        ncs[:, NS * c:NS * (c + 1)] = r["ncs"]
        nks[:, NS * c:NS * (c + 1)] = r["nks"].reshape(2, NS, 128, 2, 64)
        nvs[:, NS * c:NS * (c + 1)] = r["nvs"].reshape(2, NS, 128, 2, 64)
    return (y_prompt, y_sample, ncp, nkp, nvp, ncs, nks, nvs)
```

```python
import contextlib
import numpy as np
import concourse.bass as bass
import concourse.mybir as mybir
from concourse.bass_utils import run_bass_kernel_spmd

F32 = mybir.dt.float32
BF16 = mybir.dt.bfloat16
AF = mybir.ActivationFunctionType
ALU = mybir.AluOpType

NCORES = 8
D = 1024
NCH = 8
SEQ = 4096
OWN = 2048
HALO = 256
NS = 16
PAST = 16384
CK = 31
CB = 30
EPS = 1e-6
NCHUNK = 84
NSLAB = NCHUNK // 2
NWB = 5
TMAX = 512
NEG = -30000.0
PP_G, PP_CB, PP_LG, PP_LB, PP_GF, PP_CW = 0, 16, 32, 48, 64, 72
PP_N = 72 + 2 * 8 * CK

ENGS = ("pe", "act", "dve", "pool", "sp")


class Buf:
    __slots__ = ("name", "last_w", "readers", "sem", "dma_cnt", "excl")

    def __init__(self, name, sem=None):
        self.excl = False
        self.name = name
        self.last_w = None
        self.readers = []
        self.sem = sem
        self.dma_cnt = 0


class Op:
    __slots__ = ("eng", "fn", "deps", "is_dma", "sig", "idx")

    def __init__(self, eng, fn, is_dma):
        self.eng = eng
        self.fn = fn
        self.deps = []
        self.is_dma = is_dma
        self.sig = None
        self.idx = None


class Sched:
    def __init__(self, nc):
        self.nc = nc
        self.ops = []
        self._sem_ctx = []
        self.dma_bufs = []

    def new_sem(self, name):
        ctx = self.nc.semaphore(name)
        s = ctx.__enter__()
        self._sem_ctx.append(ctx)
        return s

    def close(self):
        for c in reversed(self._sem_ctx):
            c.__exit__(None, None, None)

    def buf(self, name, dma=False):
        b = Buf(name, self.new_sem("d_" + name) if dma else None)
        if dma:
            self.dma_bufs.append(b)
        return b

    def _add(self, op, reads, writes):
        deps = set()
        xr = [b for b in reads if b.excl]
        if xr:
            reads = [b for b in reads if not b.excl]
            writes = list(writes) + [b for b in xr if b not in writes]
        for b in reads:
            if b.last_w is not None:
                deps.add(b.last_w)
        for b in writes:
            if b.last_w is not None:
                deps.add(b.last_w)
            for r in b.readers:
                deps.add(r)
        deps.discard(op)
        op.deps = list(deps)
        for b in reads:
            b.readers.append(op)
        for b in writes:
            b.last_w = op
            b.readers = []
        op.idx = len(self.ops)
        self.ops.append(op)
        return op

    def op(self, eng, fn, reads=(), writes=()):
        return self._add(Op(eng, fn, False), reads, writes)

    def dma(self, eng, out_ap, in_ap, sembuf, reads=(), writes=()):
        def fn(e):
            return e.dma_start(out=out_ap, in_=in_ap)
        o = Op(eng, fn, True)
        sembuf.dma_cnt += 1
        o.sig = (sembuf.sem, 16 * sembuf.dma_cnt)
        return self._add(o, reads, list(writes) + [sembuf])

    def emit(self, final_wait_bufs=()):
        nc = self.nc
        need = set()
        for o in self.ops:
            for d in o.deps:
                if d.is_dma:
                    continue
                if d.eng == "pe" and o.eng == "pe" and not o.is_dma:
                    continue
                need.add(d)
        esem = {e: self.new_sem("e_" + e) for e in ENGS}
        cnt = {e: 0 for e in ENGS}
        for o in self.ops:
            if o in need:
                cnt[o.eng] += 1
                o.sig = (esem[o.eng], cnt[o.eng])
        per = {e: [o for o in self.ops if o.eng == e] for e in ENGS}

        def run(eng_name, eng):
            waited = {}
            for o in per[eng_name]:
                req = {}
                for d in o.deps:
                    if d.sig is None:
                        continue
                    if (not d.is_dma) and d.eng == "pe" and eng_name == "pe" and not o.is_dma:
                        continue
                    s, v = d.sig
                    k = id(s)
                    if k not in req or req[k][1] < v:
                        req[k] = (s, v)
                for k, (s, v) in req.items():
                    if waited.get(k, 0) >= v:
                        continue
                    eng.wait_ge(s, v)
                    waited[k] = v
                inst = o.fn(eng)
                if o.sig is not None:
                    inst.then_inc(o.sig[0], 16 if o.is_dma else 1)
            if eng_name == "sp":
                for b in self.dma_bufs:
                    if b.dma_cnt:
                        eng.wait_ge(b.sem, 16 * b.dma_cnt)

        with nc.Block() as block:
            @block.tensor
            def _(e):
                run("pe", e)

            @block.scalar
            def _(e):
                run("act", e)

            @block.vector
            def _(e):
                run("dve", e)

            @block.gpsimd
            def _(e):
                run("pool", e)

            @block.sync
            def _(e):
                run("sp", e)


def _stream_cols():
    C = []
    o_glu, o_ga, o_q, o_k, o_v, o_gb, o_mga, o_mgb = 0, 2048, 3072, 4096, 4224, 4352, 5376, 6400
    for j in range(8):
        C.append(("in", o_glu + j * 128))
        C.append(("in", o_glu + 1024 + j * 128))
    for j in range(8):
        C.append(("in", o_q + j * 128))
    C.append(("k0", o_k))
    C.append(("k1", o_k + 64))
    C.append(("in", o_v))
    for j in range(8):
        C.append(("in", o_gb + j * 128))
    for j in range(8):
        C.append(("in", o_ga + j * 128))
    for j in range(8):
        C.append(("in", o_mga + j * 128))
        C.append(("co", j * 128))
    for j in range(8):
        C.append(("in", o_mgb + j * 128))
        C.append(("ao", j * 128))
    for j in range(8):
        C.append(("out", j * 128))
    assert len(C) == 83
    C.append(("pad", 0))
    return C


def _build_wstream(w_in, wco, wao, wout):
    C = _stream_cols()
    out = np.zeros((2, NCHUNK, D, 128), np.float32)
    for l in range(2):
        for i, (src, c0) in enumerate(C):
            if src == "in":
                out[l, i] = w_in[l][:, c0:c0 + 128]
            elif src in ("k0", "k1"):
                out[l, i, :, 0:64] = w_in[l][:, c0:c0 + 64]
                out[l, i, :, 64:128] = w_in[l][:, c0:c0 + 64]
            elif src == "co":
                out[l, i] = wco[l][:, c0:c0 + 128]
            elif src == "ao":
                out[l, i] = wao[l][:, c0:c0 + 128]
            elif src == "out":
                out[l, i] = wout[l][:, c0:c0 + 128]
    out = out.reshape(2, NSLAB, 2, NCH, 128, 128)
    out = out.transpose(0, 1, 4, 3, 2, 5)
    return np.ascontiguousarray(out.reshape(2, NSLAB, 128, NCH, 256))


TILES = [
    dict(name="A", Tp=512, Ts=0, xrow0=0, rope0=0, first_qb=2, halo=256, yblk0=2, last=False, yrow0=0),
    dict(name="B", Tp=512, Ts=0, xrow0=512, rope0=512, first_qb=-1, halo=0, yblk0=0, last=False, yrow0=256),
    dict(name="C", Tp=512, Ts=0, xrow0=1024, rope0=1024, first_qb=-1, halo=0, yblk0=0, last=False, yrow0=768),
    dict(name="D", Tp=384, Ts=0, xrow0=1536, rope0=1536, first_qb=-1, halo=0, yblk0=0, last=False, yrow0=1280),
    dict(name="E", Tp=384, Ts=NS, xrow0=1920, rope0=1920, first_qb=-1, halo=0, yblk0=0, last=True, yrow0=1664),
]
NCOLS = 272 + 2048
DBG_ROPE = 9
DBG_STOP = None


def build_program():
    nc = bass.Bass("TRN2", target_bir_lowering=False)
    dt = lambda n, s, k, d=F32: nc.dram_tensor(n, s, d, kind=k).ap()
    xp = dt("xp", [HALO + OWN, D], "ExternalInput")
    xs = dt("xs", [NS, D], "ExternalInput")
    stc = dt("stc", [2, NS, CB, D], "ExternalInput")
    ckd = dt("ck", [2, NS, 128, 128], "ExternalInput")
    cvd = dt("cv", [2, NS, 128, 128], "ExternalInput")
    wst = dt("wst", [2, NSLAB, 128, NCH, 256], "ExternalInput")
    ppd = dt("pp", [128, PP_N], "ExternalInput")
    snk = dt("snk", [32], "ExternalInput")
    rcd = dt("ropec", [128, NCOLS], "ExternalInput")
    rsd = dt("ropes", [128, NCOLS], "ExternalInput")
    mskd = dt("msk", [128, 3, 512], "ExternalInput")
    permd = dt("perm", [128, 128], "ExternalInput")
    vald = dt("valid", [128, 1], "ExternalInput")
    y_o = dt("y", [OWN, D], "ExternalOutput")
    ys_o = dt("ys", [NS, D], "ExternalOutput")
    ncp_o = dt("ncp", [2, CB, D], "ExternalOutput")
    nkp_o = dt("nkp", [2, 128, 128], "ExternalOutput")
    nvp_o = dt("nvp", [2, 128, 128], "ExternalOutput")
    ncs_o = dt("ncs", [2, NS, CB, D], "ExternalOutput")
    nks_o = dt("nks", [2, NS, 128, 128], "ExternalOutput")
    nvs_o = dt("nvs", [2, NS, 128, 128], "ExternalOutput")

    S = Sched(nc)
    es_ = contextlib.ExitStack()
    with es_:
        def sb(name, shape, d=F32):
            return es_.enter_context(nc.sbuf_tensor(name, shape, d))

        xT = sb("xT", [128, NCH, TMAX]); b_xT = [S.buf("xT%d" % j) for j in range(NCH)]
        big = [sb("big%d" % i, [128, D]) for i in range(2)]
        b_big = [S.buf("big%d" % i, dma=True) for i in range(2)]
        hT = sb("hT", [128, NCH, TMAX], BF16); b_hT = S.buf("hT")
        rot16 = [sb("r16_%d" % i, [128, TMAX], BF16) for i in range(4)]
        b_rot16 = [S.buf("r16_%d" % i) for i in range(4)]
        uT = sb("uT", [128, NCH, CB + TMAX], BF16); b_uT = [S.buf("uT%d" % j) for j in range(NCH)]
        uh = sb("uh", [128, 2, NCH, CB], BF16); b_uh = [S.buf("uh%d" % l) for l in range(2)]
        qT = sb("qT", [128, NCH, TMAX], BF16)
        b_q = [[S.buf("q%d_%d" % (kv, qb)) for qb in range(5)] for kv in range(2)]
        qraw = [sb("qraw%d" % i, [128, TMAX], BF16) for i in range(2)]
        b_qraw = [S.buf("qraw%d" % i) for i in range(2)]
        kT = [sb("kT%d" % l, [128, 2, 2, 128 + TMAX], BF16) for l in range(2)]
        b_kT = [S.buf("kT%d" % l) for l in range(2)]
        V4 = [sb("V4_%d" % l, [128, 5, 2, 2, 128], BF16) for l in range(2)]
        b_V4 = [S.buf("V4_%d" % l) for l in range(2)]
        sgb = sb("sgb", [128, NCH, TMAX], BF16); b_sgb = S.buf("sgb")
        Dg = [sb("Dg%d" % i, [128, 16, 128], BF16) for i in range(2)]
        b_Dg = [S.buf("Dg%d" % i) for i in range(2)]
        cF = sb("cF", [128, NCH, TMAX]); b_cF = [S.buf("cF%d" % j) for j in range(NCH)]
        sga = [sb("sga%d" % i, [128, TMAX], BF16) for i in range(2)]
        b_sga = [S.buf("sga%d" % i) for i in range(2)]
        cc = sb("cc", [128, NCH, TMAX], BF16); b_cc = S.buf("cc")
        yT = cc; b_yT = b_cc
        pT = [sb("pT%d" % i, [128, 2, TMAX], BF16) for i in range(4)]
        b_pT = [S.buf("pT%d" % i) for i in range(4)]
        tmp = [sb("tmp%d" % i, [128, TMAX]) for i in range(6)]
        b_tmp = [S.buf("tmp%d" % i) for i in range(6)]
        st = [sb("st%d" % i, [128, TMAX]) for i in range(4)]
        b_st = [S.buf("st%d" % i) for i in range(4)]
        ropeC = sb("ropeC", [128, TMAX]); ropeS = sb("ropeS", [128, TMAX]); b_rope = S.buf("rope", dma=True)
        msk = sb("mskb", [128, 3, 512], BF16)
        identb = sb("identb", [128, 128], BF16)
        identf = sb("identf", [128, 128])
        permb = sb("permb", [128, 128], BF16)
        onesS = sb("onesS", [128, 128], BF16)
        wbuf = [sb("wb%d" % i, [128, NCH, 256], BF16) for i in range(NWB)]
        b_wbuf = [S.buf("wb%d" % i, dma=True) for i in range(NWB)]
        pp = sb("pp_sb", [128, PP_N])
        cwh = sb("cwh", [128, 2, NCH, CK])
        esk = sb("esk", [128, 32])
        valid = sb("valid_sb", [128, 1])
        cst = sb("cst", [128, 2])
        b_const = S.buf("const", dma=True)
        b_const2 = S.buf("const2", dma=True)
        b_init = S.buf("init")
        ufin = sb("ufin", [128, NCH, CB]); b_ufin = S.buf("ufin")
        usf = sb("usf", [128, NCH, NS]); b_usf = S.buf("usf")
        kfin = sb("kfin", [128, 2, 128]); b_kfin = S.buf("kfin")
        ksf = sb("ksf", [128, 2, NS]); b_ksf = S.buf("ksf")
        vfin = sb("vfin", [128, 128]); b_vfin = S.buf("vfin", dma=True)
        nkb = sb("nkb", [128, 2, 64]); b_nkb = S.buf("nkb", dma=True)
        knew = sb("knew", [NS, 2, 64]); b_knew = S.buf("knew", dma=True)
        vnew = sb("vnew", [NS, 128]); b_vnew = S.buf("vnew", dma=True)
        stT = sb("stT", [128, NCH, NS, CB], BF16); b_stT = S.buf("stT")
        Ks2 = [sb("Ks%d" % i, [128, 4, 128]) for i in range(2)]; b_Ks2 = [S.buf("Ks%d" % i, dma=True) for i in range(2)]
        Vs2 = [sb("Vs%d" % i, [128, 4, 128]) for i in range(2)]; b_Vs2 = [S.buf("Vs%d" % i, dma=True) for i in range(2)]
        b_Ks2b = [S.buf("Ksb%d" % i, dma=True) for i in range(2)]; b_Vs2b = [S.buf("Vsb%d" % i, dma=True) for i in range(2)]
        KTs = sb("KTs", [128, 4, 2, 2, 128], BF16); b_KTs = S.buf("KTs")
        V4s = sb("V4s", [128, 4, 2, 2, 128], BF16); b_V4s = S.buf("V4s")
        pTs = sb("pTs", [128, 256], BF16); b_pTs = S.buf("pTs")
        b_dd = S.buf("dd", dma=True)
        ps = [es_.enter_context(nc.psum_tensor("ps%d" % i, [128, 512], F32)) for i in range(8)]
        b_ps = [S.buf("ps%d" % i) for i in range(8)]
        for b_ in b_ps:
            b_.excl = True
        bank_ctr = [0]

        nbank = [6]

        def bank():
            b = bank_ctr[0] % nbank[0]
            bank_ctr[0] += 1
            return b

        rot_ctr = {}

        def rot(key, n):
            v = rot_ctr.get(key, 0)
            rot_ctr[key] = v + 1
            return v % n

        def tmpf():
            i = rot("tmp", 6)
            return tmp[i], b_tmp[i]

        def r16():
            i = rot("r16", 4)
            return rot16[i], b_rot16[i]

        S.dma("sp", pp[:], ppd, b_const)
        S.dma("sp", esk[:], snk.partition_broadcast(128), b_const)
        S.dma("sp", valid[:], vald, b_const)
        S.dma("pool", msk[:], mskd, b_const2)
        S.dma("pool", permb[:], permd, b_const2)

        ini = lambda fn, w=(), r=(): S.op("pool", fn, reads=[b_init] + list(r), writes=[b_init] + list(w))
        ini(lambda e: e.memset(identf[:], 1.0))
        ini(lambda e: e.affine_select(out=identf[:], in_=identf[:], pattern=[[-1, 128]], compare_op=ALU.is_equal,
                                      fill=0.0, base=0, channel_multiplier=1))
        ini(lambda e: e.tensor_copy(out=identb[:], in_=identf[:]))
        ini(lambda e: e.memset(onesS[:], 1.0 / 1024.0))
        ini(lambda e: e.memset(cst[:, 0:1], EPS))
        ini(lambda e: e.memset(cst[:, 1:2], -0.5))
        ini(lambda e: e.memset(uh[:], 0.0), w=b_uh)
        for l0 in range(2):
            ini((lambda l0: lambda e: e.memset(kT[l0][:], 0.0))(l0), w=[b_kT[l0]])
            ini((lambda l0: lambda e: e.memset(V4[l0][:], 0.0))(l0), w=[b_V4[l0]])
            ini((lambda l0: lambda e: e.memset(V4[l0][:, :, 0, :, 64:128], 1.0))(l0), w=[b_V4[l0]])
            ini((lambda l0: lambda e: e.memset(V4[l0][:, :, 1, :, 0:64], 1.0))(l0), w=[b_V4[l0]])
        ini(lambda e: e.memset(KTs[:], 0.0), w=[b_KTs])
        ini(lambda e: e.memset(V4s[:, :, 0, :, 64:128], 1.0), w=[b_V4s])
        ini(lambda e: e.memset(V4s[:, :, 1, :, 0:64], 1.0), w=[b_V4s])
        ini(lambda e: e.memset(uT[:], 0.0), w=b_uT)
        S.op("pool", lambda e: e.tensor_scalar(out=cwh[:].rearrange("p a b c -> p (a b c)"), in0=pp[:, PP_CW:PP_N],
                                                scalar1=0.5, scalar2=None, op0=ALU.mult),
             reads=[b_const], writes=[b_init])
        S.op("act", lambda e: e.activation(out=esk[:], in_=esk[:], func=AF.Exp), reads=[b_const], writes=[b_const])

        wstate = dict(issued=0, total=5 * 2 * NSLAB)
        order = [(ti, l) for ti in range(len(TILES)) for l in range(2)]

        def w_issue(upto):
            while wstate["issued"] <= upto and wstate["issued"] < wstate["total"]:
                g = wstate["issued"]
                tl, s = divmod(g, NSLAB)
                l = tl % 2
                S.dma("pool", wbuf[g % NWB][:], wst[l, (s % 30) if DBG_ROPE == 5 else s], b_wbuf[g % NWB])
                wstate["issued"] += 1

        wctr = [0]

        def wnext():
            ci = wctr[0]
            wctr[0] += 1
            g = ci // 2
            w_issue(g + NWB - 1)
            t = wbuf[g % NWB]
            return t[:, :, (ci % 2) * 128:(ci % 2) * 128 + 128], b_wbuf[g % NWB]

        def proj(act, act_bufs, N, b=None):
            wap, wb = wnext()
            if b is None:
                b = bank()

            def f(e):
                for kc in range(NCH):
                    i = e.matmul(ps[b][:, 0:N], lhsT=wap[:, kc, :], rhs=act[:, kc, 0:N], start=(kc == 0), stop=(kc == NCH - 1))
                return i
            S.op("pe", f, reads=list(act_bufs) + [wb], writes=[b_ps[b]])
            return b

        def rms_accum(T, j, src, src_buf):
            r, rb = r16()
            S.op("act", (lambda j, r: lambda e: e.activation(out=r[:, 0:T], in_=src[:, j, 0:T], func=AF.Square))(j, r),
                 reads=[src_buf], writes=[rb])
            def mm():
                S.op("pe", (lambda j, r: lambda e: e.matmul(ps[6][:, 0:T], lhsT=onesS[:], rhs=r[:, 0:T], start=(j == 0), stop=(j == NCH - 1)))(j, r),
                     reads=[rb, b_init], writes=[b_ps[6]])
            return mm

        def rms_finish(T):
            S.op("act", lambda e: e.activation(out=st[0][:, 0:T], in_=ps[6][:, 0:T], func=AF.Sqrt, bias=cst[:, 0:1]),
                 reads=[b_ps[6], b_init], writes=[b_st[0]])
            S.op("dve", lambda e: e.reciprocal(out=st[1][:, 0:T], in_=st[0][:, 0:T]), reads=[b_st[0]], writes=[b_st[1]])

        def rms_stats(T, src, src_bufs):
            prev = None
            for j in range(NCH):
                m = rms_accum(T, j, src, src_bufs[j])
                if prev is not None:
                    prev()
                prev = m
            prev()
            rms_finish(T)

        halt = [False]

        def ck(tl, l, ph):
            if DBG_STOP is not None and DBG_STOP == (tl["name"], l, ph):
                halt[0] = True
            return halt[0]

        def do_tile(ti, tl):
            if halt[0]:
                return
            Tp, Ts = tl["Tp"], tl["Ts"]
            T = Tp + Ts
            nb = Tp // 128
            isH = Ts > 0
            S.dma("sp", ropeC[:, 0:T], rcd[:, tl["rope0"]:tl["rope0"] + T], b_rope)
            S.dma("sp", ropeS[:, 0:T], rsd[:, tl["rope0"]:tl["rope0"] + T], b_rope)
            for blk in range(nb):
                bi = rot("big", 2)
                S.dma("sp", big[bi][:], xp[tl["xrow0"] + blk * 128: tl["xrow0"] + (blk + 1) * 128, :], b_big[bi])
                for g in range(2):
                    b = bank()

                    def ftr(e, bi=bi, g=g, b=b):
                        for jj in range(4):
                            i = e.transpose(ps[b][:, jj * 128:(jj + 1) * 128], big[bi][:, (4 * g + jj) * 128:(4 * g + jj + 1) * 128], identf[:])
                        return i
                    S.op("pe", ftr, reads=[b_big[bi], b_init], writes=[b_ps[b]])
                    S.op("act", (lambda b, g, blk: lambda e: e.activation(
                        out=xT[:, 4 * g:4 * g + 4, blk * 128:(blk + 1) * 128],
                        in_=ps[b][:].rearrange("p (a c) -> p a c", a=4), func=AF.Copy))(b, g, blk),
                        reads=[b_ps[b]], writes=b_xT[4 * g:4 * g + 4])
            if Ts:
                bi = rot("big", 2)
                S.dma("sp", big[bi][0:NS, :], xs, b_big[bi])
                b = bank()

                def ftrs(e, bi=bi, b=b):
                    for j in range(NCH):
                        i = e.transpose(ps[b][:, j * NS:(j + 1) * NS], big[bi][0:NS, j * 128:(j + 1) * 128], identf[0:NS, 0:NS])
                    return i
                S.op("pe", ftrs, reads=[b_big[bi], b_init], writes=[b_ps[b]])
                S.op("act", (lambda b: lambda e: e.activation(out=xT[:, :, Tp:T], in_=ps[b][:, 0:NCH * NS].rearrange("p (a c) -> p a c", a=NCH),
                                                             func=AF.Copy))(b), reads=[b_ps[b]], writes=b_xT)

            if ck(tl, -1, 'P0'):
                return

            def do_layer(l):
                if halt[0]:
                    return
                gcol = lambda base, j: pp[:, base + l * 8 + j: base + l * 8 + j + 1]
                if l == 0:
                    rms_stats(T, xT, b_xT)
                for j in range(NCH):
                    S.op("dve", (lambda j: lambda e: e.scalar_tensor_tensor(
                        out=hT[:, j, 0:T], in0=xT[:, j, 0:T], scalar=gcol(PP_G, j), in1=st[1][:, 0:T],
                        op0=ALU.mult, op1=ALU.mult))(j), reads=[b_xT[j], b_st[1], b_const], writes=[b_hT])
                if ck(tl, l, 'P1'):
                    return
                if isH:
                    for r4 in range(4):
                        bi = rot("big", 2)
                        S.dma("sp", big[bi][0:120, :], stc[l, 4 * r4:4 * r4 + 4].rearrange("b j d -> (b j) d"), b_big[bi])
                        for g in range(2):
                            b = bank()

                            def ftst(e, bi=bi, g=g, b=b):
                                for jj in range(4):
                                    i = e.transpose(ps[b][:, jj * 120:(jj + 1) * 120], big[bi][0:120, (4 * g + jj) * 128:(4 * g + jj + 1) * 128],
                                                    identf[0:120, 0:120])
                                return i
                            S.op("pe", ftst, reads=[b_big[bi], b_init], writes=[b_ps[b]])
                            S.op("act", (lambda b, g, r4: lambda e: e.activation(
                                out=stT[:, 4 * g:4 * g + 4, 4 * r4:4 * r4 + 4, :].rearrange("p a b j -> p a (b j)"),
                                in_=ps[b][:, 0:480].rearrange("p (a c) -> p a c", a=4), func=AF.Copy, scale=2.0))(b, g, r4),
                                reads=[b_ps[b]], writes=[b_stT])
                    S.dma("sp", ncs_o[l, :, 0:CB - 1, :], stc[l, :, 1:CB, :], b_dd)
                    S.dma("sp", nks_o[l, :, 0:127, :], ckd[l, :, 1:128, :], b_dd)
                    S.dma("sp", nvs_o[l, :, 0:127, :], cvd[l, :, 1:128, :], b_dd)
                if ck(tl, l, 'S0'):
                    return
                nbank[0] = 8
                S.op("pool", lambda e: e.tensor_copy(out=uT[:, :, 0:CB], in_=uh[:, l, :, :]), reads=[b_uh[l]], writes=b_uT)
                for j in range(NCH):
                    ba = proj(hT, [b_hT], T)
                    bb = proj(hT, [b_hT], T)
                    tg, tgb = tmpf()
                    S.op("act", (lambda bb, tg: lambda e: e.activation(out=tg[:, 0:T], in_=ps[bb][:, 0:T], func=AF.Tanh, scale=0.5))(bb, tg),
                         reads=[b_ps[bb]], writes=[tgb])
                    S.op("dve", (lambda j, ba, tg: lambda e: e.scalar_tensor_tensor(
                        out=uT[:, j, CB:CB + T], in0=tg[:, 0:T], scalar=1.0, in1=ps[ba][:, 0:T], op0=ALU.add, op1=ALU.mult))(j, ba, tg),
                        reads=[tgb, b_ps[ba]], writes=[b_uT[j]])
                    if tl["last"]:
                        S.op("dve", (lambda j, ba, tg: lambda e: e.scalar_tensor_tensor(
                            out=ufin[:, j, :], in0=tg[:, Tp - CB:Tp], scalar=1.0, in1=ps[ba][:, Tp - CB:Tp], op0=ALU.add, op1=ALU.mult))(j, ba, tg),
                            reads=[tgb, b_ps[ba]], writes=[b_ufin])
                    if Ts:
                        S.op("dve", (lambda j, ba, tg: lambda e: e.scalar_tensor_tensor(
                            out=usf[:, j, :], in0=tg[:, Tp:T], scalar=1.0, in1=ps[ba][:, Tp:T], op0=ALU.add, op1=ALU.mult))(j, ba, tg),
                            reads=[tgb, b_ps[ba]], writes=[b_usf])
                if ck(tl, l, 'P2a'):
                    return
                def rope_chunk(bq, dst_ap_fn, dst_bufs, extra=None):
                    qi = rot("qraw", 2)
                    S.op("act", (lambda bq, qi: lambda e: e.activation(out=qraw[qi][:, 0:T], in_=ps[bq][:, 0:T], func=AF.Copy))(bq, qi),
                         reads=[b_ps[bq]], writes=[b_qraw[qi]])
                    def rest():
                        bs = bank()
                        S.op("pe", (lambda bs, qi: lambda e: e.matmul(ps[bs][:, 0:T], lhsT=permb[:], rhs=qraw[qi][:, 0:T], start=True, stop=True))(bs, qi),
                             reads=[b_qraw[qi], b_const2], writes=[b_ps[bs]])
                        t1, t1b = tmpf()
                        t2, t2b = tmpf()
                        S.op("dve", (lambda bq, t1: lambda e: e.tensor_tensor(out=t1[:, 0:T], in0=ps[bq][:, 0:T], in1=ropeC[:, 0:T], op=ALU.mult))(bq, t1),
                             reads=[b_ps[bq], b_rope], writes=[t1b])
                        S.op("dve", (lambda bs, t2: lambda e: e.tensor_tensor(out=t2[:, 0:T], in0=ps[bs][:, 0:T], in1=ropeS[:, 0:T], op=ALU.mult))(bs, t2),
                             reads=[b_ps[bs], b_rope], writes=[t2b])
                        dsts = dst_ap_fn(0, T)
                        if not isinstance(dsts, list):
                            dsts = [(dsts, slice(0, 128))]
                        for (dap, rws) in dsts:
                            S.op("dve", (lambda t1, t2, dap, rws: lambda e: e.tensor_tensor(out=dap, in0=t1[rws, 0:T], in1=t2[rws, 0:T], op=ALU.add))(t1, t2, dap, rws),
                                 reads=[t1b, t2b], writes=dst_bufs)
                        if extra is not None:
                            for (oap, c0, c1, obuf) in extra:
                                S.op("dve", (lambda t1, t2, oap, c0, c1: lambda e: e.tensor_tensor(out=oap, in0=t1[:, c0:c1], in1=t2[:, c0:c1], op=ALU.add))(t1, t2, oap, c0, c1),
                                     reads=[t1b, t2b], writes=[obuf])

                    return rest

                pend_rope = None
                for j in range(NCH):
                    bq = proj(hT, [b_hT], T)
                    if pend_rope is not None:
                        pend_rope()
                    pend_rope = rope_chunk(bq, (lambda j: lambda c0, c1: qT[:, j, c0:c1])(j), b_q[j // 4])
                for kv in range(2):
                    bk = proj(hT, [b_hT], T)
                    if pend_rope is not None:
                        pend_rope()
                    extra = []
                    if tl["last"]:
                        extra.append((kfin[:, kv, :], Tp - 128, Tp, b_kfin))
                    if Ts:
                        extra.append((ksf[:, kv, :], Tp, T, b_ksf))
                    pend_rope = rope_chunk(bk, (lambda kv: lambda c0, c1: [(kT[l][0:64, kv, 0, 128 + c0:128 + c1], slice(0, 64)),
                                                               (kT[l][64:128, kv, 1, 128 + c0:128 + c1], slice(64, 128))])(kv), [b_kT[l]], extra)
                pend_rope()
                wv, wvb = wnext()
                bv = bank()

                def fv(e, bv=bv, wv=wv):
                    for blk in range(nb):
                        for kc in range(NCH):
                            i = e.matmul(ps[bv][:, blk * 128:(blk + 1) * 128], lhsT=hT[:, kc, blk * 128:(blk + 1) * 128], rhs=wv[:, kc, :],
                                         start=(kc == 0), stop=(kc == NCH - 1))
                    if Ts:
                        for kc in range(NCH):
                            i = e.matmul(ps[bv][0:NS, nb * 128:(nb + 1) * 128], lhsT=hT[:, kc, Tp:T], rhs=wv[:, kc, :],
                                         start=(kc == 0), stop=(kc == NCH - 1))
                    return i
                S.op("pe", fv, reads=[b_hT, wvb], writes=[b_ps[bv]])
                psv = ps[bv][:, 0:nb * 128].rearrange("p (b k d) -> p b k d", b=nb, k=2)
                S.op("act", (lambda psv: lambda e: e.activation(out=V4[l][:, 1:1 + nb, 0, :, 0:64], in_=psv, func=AF.Copy))(psv),
                     reads=[b_ps[bv]], writes=[b_V4[l]])
                S.op("act", (lambda psv: lambda e: e.activation(out=V4[l][:, 1:1 + nb, 1, :, 64:128], in_=psv, func=AF.Copy))(psv),
                     reads=[b_ps[bv]], writes=[b_V4[l]])
                if tl["last"]:
                    S.op("dve", (lambda bv: lambda e: e.tensor_copy(out=vfin[:], in_=ps[bv][:, (nb - 1) * 128:nb * 128]))(bv),
                         reads=[b_ps[bv]], writes=[b_vfin])
                    S.dma("sp", nvp_o[l], vfin[:], b_vfin, reads=[b_vfin])
                if Ts:
                    S.op("dve", (lambda bv: lambda e: e.tensor_copy(out=vnew[:], in_=ps[bv][0:NS, nb * 128:(nb + 1) * 128]))(bv),
                         reads=[b_ps[bv]], writes=[b_vnew])
                    S.dma("sp", nvs_o[l, :, 127, :], vnew[:], b_vnew, reads=[b_vnew])
                if ck(tl, l, 'P2v'):
                    return
                for j in range(NCH):
                    bg = proj(hT, [b_hT], T)
                    S.op("act", (lambda j, bg: lambda e: e.activation(out=sgb[:, j, 0:T], in_=ps[bg][:, 0:T], func=AF.Silu))(j, bg),
                         reads=[b_ps[bg]], writes=[b_sgb])
                if ck(tl, l, 'P2b'):
                    return
                nbank[0] = 6
                pend_stat = []
                for j in range(NCH):
                    bc = bank()
                    halves = []
                    for half in range(2):
                        t0, t1_ = (0, 16) if half == 0 else (16, CK)
                        di = rot("dg", 2)
                        nt = t1_ - t0
                        halves.append((di, t0, t1_))
                        S.op("dve", (lambda di, j, t0, nt: lambda e: e.tensor_tensor(
                            out=Dg[di][:, 0:nt, :], in0=identf[:].unsqueeze(1).broadcast_to([128, nt, 128]),
                            in1=cwh[:, l, j, t0:t0 + nt].unsqueeze(2).broadcast_to([128, nt, 128]), op=ALU.mult))(di, j, t0, nt),
                            reads=[b_init], writes=[b_Dg[di]])

                        def fconv(e, di=di, j=j, t0=t0, t1_=t1_, bc=bc):
                            for tap in range(t0, t1_):
                                i = e.matmul(ps[bc][:, 0:Tp], lhsT=Dg[di][:, tap - t0, :], rhs=uT[:, j, tap:tap + Tp],
                                             start=(tap == 0), stop=(tap == CK - 1))
                            return i
                        S.op("pe", fconv, reads=[b_Dg[di], b_uT[j]], writes=[b_ps[bc]])
                    if Ts:
                        for (di, t0, t1_) in halves:
                            def fconvs(e, di=di, j=j, t0=t0, t1_=t1_, bc=bc):
                                for tap in range(t0, t1_):
                                    rhs = stT[:, j, :, tap] if tap < CB else uT[:, j, CB + Tp:CB + T]
                                    i = e.matmul(ps[bc][:, Tp:T], lhsT=Dg[di][:, tap - t0, :], rhs=rhs,
                                                 start=(tap == 0), stop=(tap == CK - 1))
                                return i
                            S.op("pe", fconvs, reads=[b_Dg[di], b_uT[j], b_stT], writes=[b_ps[bc]])
                    S.op("act", (lambda j, bc: lambda e: e.activation(out=cF[:, j, 0:T], in_=ps[bc][:, 0:T], func=AF.Identity, bias=gcol(PP_CB, j)))(j, bc),
                         reads=[b_ps[bc], b_const], writes=[b_cF[j]])
                    r1, r1b = r16()
                    r2, r2b = r16()
                    S.op("act", (lambda j, r1: lambda e: e.activation(out=r1[:, 0:T], in_=cF[:, j, 0:T], func=AF.Copy))(j, r1),
                         reads=[b_cF[j]], writes=[r1b])
                    S.op("act", (lambda j, r2: lambda e: e.activation(out=r2[:, 0:T], in_=cF[:, j, 0:T], func=AF.Square))(j, r2),
                         reads=[b_cF[j]], writes=[r2b])
                    def stat_mm(j=j, r1=r1, r2=r2, r1b=r1b, r2b=r2b):
                        S.op("pe", lambda e: e.matmul(ps[6][:, 0:T], lhsT=onesS[:], rhs=r1[:, 0:T], start=(j == 0), stop=(j == NCH - 1)),
                             reads=[r1b, b_init], writes=[b_ps[6]])
                        S.op("pe", lambda e: e.matmul(ps[7][:, 0:T], lhsT=onesS[:], rhs=r2[:, 0:T], start=(j == 0), stop=(j == NCH - 1)),
                             reads=[r2b, b_init], writes=[b_ps[7]])
                    if pend_stat:
                        pend_stat.pop(0)()
                    pend_stat.append(stat_mm)
                while pend_stat:
                    pend_stat.pop(0)()
                S.op("pool", lambda e: e.tensor_copy(out=uh[:, l, :, :], in_=uT[:, :, Tp:Tp + CB]), reads=b_uT, writes=[b_uh[l]])
                S.op("act", lambda e: e.activation(out=st[0][:, 0:T], in_=ps[6][:, 0:T], func=AF.Copy), reads=[b_ps[6]], writes=[b_st[0]])
                S.op("dve", lambda e: e.tensor_tensor(out=st[1][:, 0:T], in0=st[0][:, 0:T], in1=st[0][:, 0:T], op=ALU.mult),
                     reads=[b_st[0]], writes=[b_st[1]])
                S.op("dve", lambda e: e.scalar_tensor_tensor(out=st[2][:, 0:T], in0=ps[7][:, 0:T], scalar=EPS, in1=st[1][:, 0:T],
                                                              op0=ALU.add, op1=ALU.subtract),
                     reads=[b_ps[7], b_st[1]], writes=[b_st[2]])
                S.op("act", lambda e: e.activation(out=st[1][:, 0:T], in_=st[2][:, 0:T], func=AF.Sqrt), reads=[b_st[2]], writes=[b_st[1]])
                S.op("dve", lambda e: e.reciprocal(out=st[2][:, 0:T], in_=st[1][:, 0:T]), reads=[b_st[1]], writes=[b_st[2]])
                S.op("dve", lambda e: e.scalar_tensor_tensor(out=st[3][:, 0:T], in0=st[0][:, 0:T], scalar=-1.0, in1=st[2][:, 0:T],
                                                               op0=ALU.mult, op1=ALU.mult),
                     reads=[b_st[0], b_st[2]], writes=[b_st[3]])
                for j in range(NCH):
                    bg = proj(hT, [b_hT], T)
                    si = rot("sga", 2)
                    S.op("act", (lambda bg, si: lambda e: e.activation(out=sga[si][:, 0:T], in_=ps[bg][:, 0:T], func=AF.Silu))(bg, si),
                         reads=[b_ps[bg]], writes=[b_sga[si]])
                    t1, t1b = tmpf()
                    t2, t2b = tmpf()
                    S.op("dve", (lambda j, t1: lambda e: e.tensor_tensor(out=t1[:, 0:T], in0=cF[:, j, 0:T], in1=st[2][:, 0:T], op=ALU.mult))(j, t1),
                         reads=[b_cF[j], b_st[2]], writes=[t1b])
                    S.op("dve", (lambda t1, t2: lambda e: e.tensor_tensor(out=t2[:, 0:T], in0=t1[:, 0:T], in1=st[3][:, 0:T], op=ALU.add))(t1, t2),
                         reads=[t1b, b_st[3]], writes=[t2b])
                    S.op("act", (lambda j, t2, t1: lambda e: e.activation(out=t1[:, 0:T], in_=t2[:, 0:T], func=AF.Silu,
                                                                        scale=gcol(PP_LG, j), bias=gcol(PP_LB, j)))(j, t2, t1),
                         reads=[t2b, b_const], writes=[t1b])
                    S.op("dve", (lambda j, t1, si: lambda e: e.tensor_tensor(out=cc[:, j, 0:T], in0=t1[:, 0:T], in1=sga[si][:, 0:T], op=ALU.mult))(j, t1, si),
                         reads=[t1b, b_sga[si]], writes=[b_cc])
                if ck(tl, l, 'P3'):
                    return
                if ck(tl, l, 'P4'):
                    return
                obT = qT
                groups = [(qb, kv, par) for qb in range(nb) for kv in range(2) for par in range(2)]
                pend = None

                def attn_pv(pair):
                    ta, tab = tmpf()
                    b3s = []
                    for (g, pi) in pair:
                        qb, kv, par = g
                        dh = slice((1 - par) * 64, (1 - par) * 64 + 64)
                        b3 = bank()
                        b3s.append(b3)

                        def fpv(e, b3=b3, qb=qb, kv=kv, par=par, pi=pi):
                            e.matmul(ps[b3][:], lhsT=V4[l][:, qb, par, kv, :], rhs=pT[pi][:, 0, :], start=True, stop=False)
                            return e.matmul(ps[b3][:], lhsT=V4[l][:, qb + 1, par, kv, :], rhs=pT[pi][:, 1, :], start=False, stop=True)
                        S.op("pe", fpv, reads=[b_V4[l], b_pT[pi]], writes=[b_ps[b3]])
                        hs = l * 16 + kv * 8 + par
                        S.op("dve", (lambda b3, dh, hs: lambda e: e.tensor_tensor(
                            out=ta[dh, :].rearrange("p (a c) -> p a c", a=4), in0=ps[b3][dh, :].rearrange("p (a c) -> p a c", a=4),
                            in1=esk[dh, hs:hs + 7:2].unsqueeze(2).broadcast_to([64, 4, 128]), op=ALU.add))(b3, dh, hs),
                            reads=[b_ps[b3], b_const], writes=[tab])
                    S.op("dve", lambda e: e.reciprocal(out=ta[:, :], in_=ta[:, :]), reads=[tab], writes=[tab])
                    for (g, pi), b3 in zip(pair, b3s):
                        qb, kv, par = g
                        oh = slice(par * 64, par * 64 + 64)
                        dh = slice((1 - par) * 64, (1 - par) * 64 + 64)
                        tb, tbb = tmpf()
                        S.op("dve", (lambda b3, oh, dh, tb: lambda e: e.tensor_tensor(out=tb[oh, :], in0=ps[b3][oh, :], in1=ta[dh, :], op=ALU.mult))(b3, oh, dh, tb),
                             reads=[b_ps[b3], tab], writes=[tbb])
                        S.op("pool", (lambda oh, tb, kv, qb: lambda e: e.tensor_tensor(
                            out=obT[oh, kv * 4:kv * 4 + 4, qb * 128:(qb + 1) * 128], in0=tb[oh, :].rearrange("p (a c) -> p a c", a=4),
                            in1=sgb[oh, kv * 4:kv * 4 + 4, qb * 128:(qb + 1) * 128], op=ALU.mult))(oh, tb, kv, qb),
                            reads=[tbb, b_sgb], writes=[b_q[kv][qb]])

                def attn_steps():
                    pairs = [[(qb, kv, 0), (qb, kv, 1)] for qb in range(nb) for kv in range(2)]
                    pend = None
                    for pr in pairs:
                        cur = []
                        for g in pr:
                            qb, kv, par = g
                            pi = rot("pT", 4)
                            mprev = msk[:, 2, :] if qb == tl["first_qb"] else msk[:, 0, :]
                            for kt in range(2):
                                bqk = bank()
                                mk = mprev if kt == 0 else msk[:, 1, :]

                                def fqk(e, bqk=bqk, kt=kt, mk=mk, qb=qb, kv=kv, par=par):
                                    e.matmul(ps[bqk][:].rearrange("p (a c) -> p a c", a=4), lhsT=kT[l][:, kv, par, (qb + kt) * 128:(qb + kt + 1) * 128],
                                             rhs=qT[:, kv * 4:kv * 4 + 4, qb * 128:(qb + 1) * 128], start=True, stop=False)
                                    return e.matmul(ps[bqk][:], lhsT=identb[:], rhs=mk, start=False, stop=True)
                                S.op("pe", fqk, reads=[b_kT[l], b_q[kv][qb], b_init, b_const2], writes=[b_ps[bqk]])
                                S.op("act", (lambda bqk, pi, kt: lambda e: e.activation(out=pT[pi][:, kt, :], in_=ps[bqk][:], func=AF.Exp, scale=0.125))(bqk, pi, kt),
                                     reads=[b_ps[bqk]], writes=[b_pT[pi]])
                            cur.append((g, pi))
                        if pend is not None:
                            attn_pv(pend)
                        pend = cur
                        yield
                    attn_pv(pend)
                    yield
                    S.op("pool", lambda e: e.tensor_copy(out=kT[l][:, :, :, 0:128], in_=kT[l][:, :, :, Tp:Tp + 128]), reads=[b_kT[l]], writes=[b_kT[l]])
                    S.op("pool", lambda e: e.tensor_copy(out=V4[l][:, 0], in_=V4[l][:, nb]), reads=[b_V4[l]], writes=[b_V4[l]])
                    if Ts:
                        bkn = bank()

                        def fkn(e, bkn=bkn):
                            e.transpose(ps[bkn][0:NS, 0:128], ksf[:, 0, :], identf[:])
                            return e.transpose(ps[bkn][0:NS, 128:256], ksf[:, 1, :], identf[:])
                        S.op("pe", fkn, reads=[b_ksf, b_init], writes=[b_ps[bkn]])
                        S.op("act", (lambda bkn: lambda e: e.activation(out=knew[:], in_=ps[bkn][0:NS, 0:256].rearrange("p (k d) -> p k d", k=2)[:, :, 0:64],
                                                                       func=AF.Copy))(bkn), reads=[b_ps[bkn]], writes=[b_knew])
                        S.dma("sp", nks_o[l, :, 127, :], knew[:].rearrange("p k d -> p (k d)"), b_knew, reads=[b_knew])
                        bss = 6
                        bos = 7
                        for g4 in range(4):
                            Ks, b_Ks, Vs, b_Vs = Ks2[g4 % 2], b_Ks2[g4 % 2], Vs2[g4 % 2], b_Vs2[g4 % 2]
                            b_Ksb, b_Vsb = b_Ks2b[g4 % 2], b_Vs2b[g4 % 2]
                            S.dma("sp", Ks[0:112, :, :], ckd[l, 4 * g4:4 * g4 + 4, 1:113, :].rearrange("b k d -> k b d"), b_Ks)
                            S.dma("sp", Ks[112:127, :, :], ckd[l, 4 * g4:4 * g4 + 4, 113:128, :].rearrange("b k d -> k b d"), b_Ksb, reads=[], writes=[])
                            S.dma("sp", Ks[127:128, :, :], knew[4 * g4:4 * g4 + 4, :, :].rearrange("p k d -> p (k d)"), b_Ksb, reads=[b_knew])
                            S.dma("sp", Vs[0:112, :, :], cvd[l, 4 * g4:4 * g4 + 4, 1:113, :].rearrange("b k d -> k b d"), b_Vs)
                            S.dma("sp", Vs[112:127, :, :], cvd[l, 4 * g4:4 * g4 + 4, 113:128, :].rearrange("b k d -> k b d"), b_Vsb)
                            S.dma("sp", Vs[127:128, :, :], vnew[4 * g4:4 * g4 + 4, :], b_Vsb, reads=[b_vnew])
                            bt = bank()

                            def ftk(e, bt=bt, Ks=Ks):
                                for i in range(4):
                                    r = e.transpose(ps[bt][:, i * 128:(i + 1) * 128], Ks[:, i, :], identf[:])
                                return r
                            S.op("pe", ftk, reads=[b_Ks, b_Ksb, b_init], writes=[b_ps[bt]])
                            pst = ps[bt][:].rearrange("p (a c) -> p a c", a=4)
                            S.op("act", (lambda pst: lambda e: e.activation(out=KTs[0:64, :, 0, 0, :], in_=pst[0:64], func=AF.Copy))(pst), reads=[b_ps[bt]], writes=[b_KTs])
                            S.op("dve", (lambda pst: lambda e: e.tensor_copy(out=KTs[64:128, :, 0, 1, :], in_=pst[0:64]))(pst), reads=[b_ps[bt]], writes=[b_KTs])
                            S.op("act", (lambda pst: lambda e: e.activation(out=KTs[0:64, :, 1, 0, :], in_=pst[64:128], func=AF.Copy))(pst), reads=[b_ps[bt]], writes=[b_KTs])
                            S.op("dve", (lambda pst: lambda e: e.tensor_copy(out=KTs[64:128, :, 1, 1, :], in_=pst[64:128]))(pst), reads=[b_ps[bt]], writes=[b_KTs])
                            vsv = Vs[:].rearrange("p b (k d) -> p b k d", k=2)
                            S.op("pool", (lambda vsv: lambda e: e.tensor_copy(out=V4s[:, :, 0, :, 0:64], in_=vsv))(vsv), reads=[b_Vs, b_Vsb], writes=[b_V4s])
                            S.op("pool", (lambda vsv: lambda e: e.tensor_copy(out=V4s[:, :, 1, :, 64:128], in_=vsv))(vsv), reads=[b_Vs, b_Vsb], writes=[b_V4s])

                            def fsqk(e, g4=g4):
                                for i in range(4):
                                    bsm = 4 * g4 + i
                                    for kv in range(2):
                                        for par in range(2):
                                            c0 = bsm * 16 + kv * 8 + par * 4
                                            r = e.matmul(ps[bss][:, c0:c0 + 4], lhsT=KTs[:, i, kv, par, :], rhs=qT[:, kv * 4:kv * 4 + 4, Tp + bsm],
                                                         start=True, stop=True)
                                return r
                            S.op("pe", fsqk, reads=[b_KTs, b_q[0][4], b_q[1][4]], writes=[b_ps[bss]])
                            S.op("act", (lambda g4: lambda e: e.activation(out=pTs[:, g4 * 64:(g4 + 1) * 64], in_=ps[bss][:, g4 * 64:(g4 + 1) * 64], func=AF.Exp, scale=0.125))(g4),
                                 reads=[b_ps[bss]], writes=[b_pTs])

                            def fspv(e, g4=g4):
                                for i in range(4):
                                    bsm = 4 * g4 + i
                                    for kv in range(2):
                                        for par in range(2):
                                            c0 = bsm * 16 + kv * 8 + par * 4
                                            r = e.matmul(ps[bos][:, c0:c0 + 4], lhsT=V4s[:, i, par, kv, :], rhs=pTs[:, c0:c0 + 4], start=True, stop=True)
                                return r
                            S.op("pe", fspv, reads=[b_V4s, b_pTs], writes=[b_ps[bos]])
                            yield
                        pov = ps[bos][:, 0:256].rearrange("p (b k r j) -> p b k r j", b=NS, k=2, r=2)
                        for par in range(2):
                            oh = slice(par * 64, par * 64 + 64)
                            dh = slice((1 - par) * 64, (1 - par) * 64 + 64)
                            ta, tab = tmpf()
                            tb, tbb = tmpf()
                            tav = ta[:, 0:128].rearrange("p (b k j) -> p b k j", b=NS, k=2)
                            tbv = tb[:, 0:128].rearrange("p (b k j) -> p b k j", b=NS, k=2)
                            hs = l * 16 + par
                            S.op("dve", (lambda par, dh, tav, hs: lambda e: e.tensor_tensor(
                                out=tav[dh], in0=pov[dh, :, :, par, :],
                                in1=esk[dh, hs:hs + 15:2].rearrange("p (k j) -> p k j", k=2).unsqueeze(1).broadcast_to([64, NS, 2, 4]), op=ALU.add))(par, dh, tav, hs),
                                reads=[b_ps[bos], b_const], writes=[tab])
                            S.op("dve", (lambda dh, ta: lambda e: e.reciprocal(out=ta[dh, 0:128], in_=ta[dh, 0:128]))(dh, ta), reads=[tab], writes=[tab])
                            S.op("dve", (lambda par, oh, dh, tav, tbv: lambda e: e.tensor_tensor(out=tbv[oh], in0=pov[oh, :, :, par, :], in1=tav[dh], op=ALU.mult))(par, oh, dh, tav, tbv),
                                 reads=[b_ps[bos], tab], writes=[tbb])
                            S.op("pool", (lambda oh, tb: lambda e: e.tensor_tensor(
                                out=obT[oh, :, Tp:T], in0=tb[oh, 0:128].rearrange("p (b c) -> p c b", b=NS),
                                in1=sgb[oh, :, Tp:T], op=ALU.mult))(oh, tb),
                                reads=[tbb, b_sgb], writes=[b_q[0][4], b_q[1][4]])
                def p4_steps():
                    for j in range(NCH):
                        b1 = proj(hT, [b_hT], T)
                        tm, tmb = tmpf()
                        S.op("act", (lambda b1, tm: lambda e: e.activation(out=tm[:, 0:T], in_=ps[b1][:, 0:T], func=AF.Tanh, scale=0.5))(b1, tm),
                             reads=[b_ps[b1]], writes=[tmb])
                        b2 = proj(cc, [b_cc], T)
                        S.op("dve", (lambda j, b2, tm: lambda e: e.scalar_tensor_tensor(
                            out=cF[:, j, 0:T], in0=tm[:, 0:T], scalar=1.0, in1=ps[b2][:, 0:T], op0=ALU.add, op1=ALU.mult))(j, b2, tm),
                            reads=[tmb, b_ps[b2]], writes=[b_cF[j]])
                        yield
                if not Ts:
                    nbank[0] = 8
                its = [attn_steps(), p4_steps()]
                for _ in range(2):
                    try:
                        next(its[0])
                    except StopIteration:
                        its.pop(0)
                        break
                while its:
                    for it in list(its):
                        try:
                            next(it)
                        except StopIteration:
                            its.remove(it)
                nbank[0] = 6
                if ck(tl, l, 'P5s'):
                    return
                nbank[0] = 8
                for j in range(NCH):
                    b1 = proj(hT, [b_hT], T)
                    tm, tmb = tmpf()
                    S.op("act", (lambda b1, tm: lambda e: e.activation(out=tm[:, 0:T], in_=ps[b1][:, 0:T], func=AF.Tanh, scale=0.5))(b1, tm),
                         reads=[b_ps[b1]], writes=[tmb])
                    b2 = proj(obT, b_q[0] + b_q[1], T)
                    if DBG_ROPE == 5:
                        continue
                    t2, t2b = tmpf()
                    S.op("dve", (lambda b2, tm, t2: lambda e: e.scalar_tensor_tensor(
                        out=t2[:, 0:T], in0=tm[:, 0:T], scalar=1.0, in1=ps[b2][:, 0:T], op0=ALU.add, op1=ALU.mult))(b2, tm, t2),
                        reads=[tmb, b_ps[b2]], writes=[t2b])
                    if DBG_ROPE == 6:
                        continue
                    S.op("pool", (lambda j, t2: lambda e: e.tensor_tensor(out=yT[:, j, 0:T], in0=t2[:, 0:T], in1=cF[:, j, 0:T], op=ALU.add))(j, t2),
                         reads=[t2b, b_cF[j]], writes=[b_yT])
                if ck(tl, l, 'P6'):
                    return
                nbank[0] = 6
                pend_rms = None
                for j in range(NCH):
                    bo = proj(yT, [b_yT], T)
                    if pend_rms is not None:
                        pend_rms()
                    S.op("dve", (lambda j, bo: lambda e: e.scalar_tensor_tensor(
                        out=xT[:, j, 0:T], in0=ps[bo][:, 0:T], scalar=0.5, in1=xT[:, j, 0:T], op0=ALU.mult, op1=ALU.add))(j, bo),
                        reads=[b_ps[bo], b_xT[j]], writes=[b_xT[j]])
                    if tl["halo"] and l == 0:
                        S.op("dve", (lambda j: lambda e: e.tensor_scalar(out=xT[:, j, 0:tl["halo"]], in0=xT[:, j, 0:tl["halo"]], scalar1=valid[:, 0:1], scalar2=None, op0=ALU.mult))(j),
                             reads=[b_xT[j], b_const], writes=[b_xT[j]])
                    pend_rms = rms_accum(T, j, xT, b_xT[j])
                pend_rms()
                rms_finish(T)
                wnext()
                if ck(tl, l, 'P7'):
                    return
                if tl["last"]:
                    bi = rot("big", 2)
                    for g in range(2):
                        b = bank()

                        def ftu(e, g=g, b=b):
                            for jj in range(4):
                                i = e.transpose(ps[b][0:CB, jj * 128:(jj + 1) * 128], ufin[:, 4 * g + jj, :], identf[:])
                            return i
                        S.op("pe", ftu, reads=[b_ufin, b_init], writes=[b_ps[b]])
                        S.op("act", (lambda b, g, bi: lambda e: e.activation(out=big[bi][0:CB, g * 512:(g + 1) * 512], in_=ps[b][0:CB, :], func=AF.Copy, scale=0.5))(b, g, bi),
                             reads=[b_ps[b]], writes=[b_big[bi]])
                    S.dma("sp", ncp_o[l], big[bi][0:CB, :], b_big[bi], reads=[b_big[bi]])
                    b = bank()

                    def ftkf(e, b=b):
                        e.transpose(ps[b][:, 0:128], kfin[:, 0, :], identf[:])
                        return e.transpose(ps[b][:, 128:256], kfin[:, 1, :], identf[:])
                    S.op("pe", ftkf, reads=[b_kfin, b_init], writes=[b_ps[b]])
                    S.op("act", (lambda b: lambda e: e.activation(out=nkb[:], in_=ps[b][:, 0:256].rearrange("p (k d) -> p k d", k=2)[:, :, 0:64], func=AF.Copy))(b),
                         reads=[b_ps[b]], writes=[b_nkb])
                    S.dma("sp", nkp_o[l], nkb[:].rearrange("p k d -> p (k d)"), b_nkb, reads=[b_nkb])
                if Ts:
                    bi = rot("big", 2)
                    for g in range(2):
                        b = bank()

                        def ftus(e, g=g, b=b):
                            for jj in range(4):
                                i = e.transpose(ps[b][0:NS, jj * 128:(jj + 1) * 128], usf[:, 4 * g + jj, :], identf[:])
                            return i
                        S.op("pe", ftus, reads=[b_usf, b_init], writes=[b_ps[b]])
                        S.op("act", (lambda b, g, bi: lambda e: e.activation(out=big[bi][0:NS, g * 512:(g + 1) * 512], in_=ps[b][0:NS, :], func=AF.Copy, scale=0.5))(b, g, bi),
                             reads=[b_ps[b]], writes=[b_big[bi]])
                    S.dma("sp", ncs_o[l, :, CB - 1, :], big[bi][0:NS, :], b_big[bi], reads=[b_big[bi]])

            for l_ in range(2):
                do_layer(l_)
            if ck(tl, 2, 'OUT'):
                return
            for j in range(NCH):
                S.op("dve", (lambda j: lambda e: e.scalar_tensor_tensor(
                    out=cF[:, j, 0:T], in0=xT[:, j, 0:T], scalar=pp[:, PP_GF + j:PP_GF + j + 1], in1=st[1][:, 0:T],
                    op0=ALU.mult, op1=ALU.mult))(j), reads=[b_xT[j], b_st[1], b_const], writes=[b_cF[j]])
            if ck(tl, 2, 'P8a'):
                return
            if True:
                for blk in range(tl["yblk0"], nb):
                    bi = rot("big", 2)
                    for g in range(2):
                        b = bank()

                        def fty(e, g=g, b=b, blk=blk):
                            for jj in range(4):
                                i = e.transpose(ps[b][:, jj * 128:(jj + 1) * 128], cF[:, 4 * g + jj, blk * 128:(blk + 1) * 128], identf[:])
                            return i
                        S.op("pe", fty, reads=b_cF + [b_init], writes=[b_ps[b]])
                        S.op("act", (lambda b, g, bi: lambda e: e.activation(out=big[bi][:, g * 512:(g + 1) * 512], in_=ps[b][:], func=AF.Copy))(b, g, bi),
                             reads=[b_ps[b]], writes=[b_big[bi]])
                    S.dma("sp", y_o[tl["yrow0"] + (blk - tl["yblk0"]) * 128: tl["yrow0"] + (blk - tl["yblk0"] + 1) * 128, :], big[bi][:], b_big[bi], reads=[b_big[bi]])
            if Ts:
                bi = rot("big", 2)
                for g in range(2):
                    b = bank()

                    def ftys(e, g=g, b=b):
                        for jj in range(4):
                            i = e.transpose(ps[b][0:NS, jj * 128:(jj + 1) * 128], cF[:, 4 * g + jj, Tp:T], identf[:])
                        return i
                    S.op("pe", ftys, reads=b_cF + [b_init], writes=[b_ps[b]])
                    S.op("act", (lambda b, g, bi: lambda e: e.activation(out=big[bi][0:NS, g * 512:(g + 1) * 512], in_=ps[b][0:NS, :], func=AF.Copy))(b, g, bi),
                         reads=[b_ps[b]], writes=[b_big[bi]])
                if DBG_ROPE != 8:
                    S.dma("sp", ys_o, big[bi][0:NS, :], b_big[bi], reads=[b_big[bi]])

        for ti_, tl_ in enumerate(TILES):
            do_tile(ti_, tl_)
        if DBG_ROPE == 7:
            for _ in range(16):
                wnext()
        assert DBG_STOP is not None or wctr[0] == len(TILES) * 2 * NCHUNK, wctr[0]
        S.emit(final_wait_bufs=b_big + [b_vfin, b_nkb, b_knew, b_vnew, b_dd])
        S.close()
    return nc


_CACHE = {}


def _rope_tables(half):
    inv = (np.float32(500000.0) ** (-(np.arange(0, 16, 2, dtype=np.float32)) / np.float32(16))).astype(np.float32)
    pos = np.zeros(NCOLS, np.float32)
    hp = np.arange(HALO, dtype=np.float32) + np.float32(half * OWN - HALO)
    pos[0:HALO] = np.maximum(hp, 0)
    pos[HALO:HALO + OWN] = np.arange(OWN, dtype=np.float32) + np.float32(half * OWN)
    pos[HALO + OWN:] = PAST
    ang = pos[None, :] * inv[:, None]
    cos = np.cos(ang).astype(np.float32)
    sin = np.sin(ang).astype(np.float32)
    C = np.ones((128, NCOLS), np.float32)
    Sg = np.zeros((128, NCOLS), np.float32)
    for base in (0, 64):
        C[base:base + 8] = cos
        C[base + 8:base + 16] = cos
        Sg[base:base + 8] = -sin
        Sg[base + 8:base + 16] = sin
    return C, Sg


def kernel(x_prompt, x_sample, state_conv, cache_k_win, cache_v_win, norm_g, w_in, conv_w, conv_b, conv_ln_g,
           conv_ln_b, w_conv_out, attn_sinks, w_attn_out, w_out, final_norm_g):
    f = lambda a: np.asarray(a, dtype=np.float32)
    x_prompt, x_sample, state_conv = f(x_prompt), f(x_sample), f(state_conv)
    ck = f(cache_k_win).reshape(2, 128, 128, 128)
    cv = f(cache_v_win).reshape(2, 128, 128, 128)
    wstream = _build_wstream(f(w_in), f(w_conv_out), f(w_attn_out), f(w_out))
    pp = np.zeros((128, PP_N), np.float32)
    fm = lambda v: f(v).reshape(2, 8, 128).transpose(2, 0, 1).reshape(128, 16)
    pp[:, PP_G:PP_G + 16] = fm(norm_g)
    pp[:, PP_CB:PP_CB + 16] = fm(conv_b)
    pp[:, PP_LG:PP_LG + 16] = fm(conv_ln_g)
    pp[:, PP_LB:PP_LB + 16] = fm(conv_ln_b)
    pp[:, PP_GF:PP_GF + 8] = f(final_norm_g).reshape(8, 128).T
    pp[:, PP_CW:] = f(conv_w).reshape(2, CK, 8, 128).transpose(3, 0, 2, 1).reshape(128, 2 * 8 * CK)
    snk = f(attn_sinks).reshape(32)
    jj = np.arange(128)[:, None]
    ii = np.arange(128)[None, :]
    mprev = np.where(jj > ii, 0.0, NEG).astype(np.float32)
    mcur = np.where(jj <= ii, 0.0, NEG).astype(np.float32)
    perm = np.zeros((128, 128), np.float32)
    for m in range(128):
        d = m % 64
        if d < 8:
            perm[m + 8, m] = 1.0
        elif d < 16:
            perm[m - 8, m] = 1.0
    if "nc" not in _CACHE:
        _CACHE["nc"] = build_program()
    nc = _CACHE["nc"]
    in_maps = []
    for c in range(NCORES):
        s, half = divmod(c, 2)
        xp = np.zeros((HALO + OWN, D), np.float32)
        if half == 1:
            xp[0:HALO] = x_prompt[s, OWN - HALO:OWN]
        xp[HALO:] = x_prompt[s, half * OWN:(half + 1) * OWN]
        C, Sg = _rope_tables(half)
        msk = np.zeros((128, 3, 512), np.float32)
        msk[:, 0] = np.tile(mprev, (1, 4))
        msk[:, 1] = np.tile(mcur, (1, 4))
        msk[:, 2] = np.tile(mprev, (1, 4)) if half == 1 else NEG
        in_maps.append({
            "xp": xp,
            "xs": np.ascontiguousarray(x_sample[NS * c:NS * (c + 1), 0, :]),
            "stc": np.ascontiguousarray(state_conv[:, NS * c:NS * (c + 1)]),
            "ck": np.ascontiguousarray(ck[:, NS * c:NS * (c + 1)]),
            "cv": np.ascontiguousarray(cv[:, NS * c:NS * (c + 1)]),
            "wst": wstream,
            "pp": pp,
            "snk": snk,
            "ropec": C,
            "ropes": Sg,
            "msk": msk,
            "perm": perm,
            "valid": np.full((128, 1), float(half), np.float32),
        })
    res = run_bass_kernel_spmd(nc, in_maps, core_ids=list(range(NCORES)))
    R = res.results
    y_prompt = np.zeros((4, SEQ, D), np.float32)
    y_sample = np.zeros((128, 1, D), np.float32)
    ncp = np.zeros((2, 4, CB, D), np.float32)
    nkp = np.zeros((2, 4, 128, 2, 64), np.float32)
    nvp = np.zeros((2, 4, 128, 2, 64), np.float32)
    ncs = np.zeros((2, 128, CB, D), np.float32)
    nks = np.zeros((2, 128, 128, 2, 64), np.float32)
    nvs = np.zeros((2, 128, 128, 2, 64), np.float32)
    for c in range(NCORES):
        s, half = divmod(c, 2)
        r = R[c]
        y_prompt[s, half * OWN:(half + 1) * OWN] = r["y"]
        y_sample[NS * c:NS * (c + 1), 0] = r["ys"]
        if half == 1:
            ncp[:, s] = r["ncp"]
            nkp[:, s] = r["nkp"].reshape(2, 128, 2, 64)
            nvp[:, s] = r["nvp"].reshape(2, 128, 2, 64)
        ncs[:, NS * c:NS * (c + 1)] = r["ncs"]
        nks[:, NS * c:NS * (c + 1)] = r["nks"].reshape(2, NS, 128, 2, 64)
        nvs[:, NS * c:NS * (c + 1)] = r["nvs"].reshape(2, NS, 128, 2, 64)
    return (y_prompt, y_sample, ncp, nkp, nvp, ncs, nks, nvs)
```
